# Optimizing a Trainium2 kernel written in Bass

```python
import jax, jax.numpy as jnp
from jax import lax
import numpy as np

D_MODEL = 1024
BATCH = 2
SEQ = 8192
DEPTH = 2

N_MIXERS = 2
N_A_LAYERS = (DEPTH + 1) // 2
N_B_LAYERS = DEPTH // 2
EPS = 1e-6

D_FF = ((8 * D_MODEL // 3 + 127) // 128) * 128

A_DK = 128
A_DV = 128
A_HEADS = D_MODEL // A_DK
A_CONV = 4
A_CHUNK = 64

B_HD = 64
B_HEADS = D_MODEL // B_HD
B_KV_HEADS = 4
B_WINDOW = 128
B_BLOCK = 128

kernel_name = "hybrid_gdn_swa_sink_macaron"


def rmsnorm(x, w):
    xf = x.astype(jnp.float32)
    y = xf * lax.rsqrt(jnp.mean(xf * xf, axis=-1, keepdims=True) + EPS) * w.astype(jnp.float32)
    return y.astype(x.dtype)


def l2norm(x):
    xf = x.astype(jnp.float32)
    return xf * lax.rsqrt(jnp.sum(xf * xf, axis=-1, keepdims=True) + EPS)


def swiglu(h, w_gu, w_down):
    gate, up = jnp.split(h @ w_gu, 2, axis=-1)
    return (jax.nn.silu(gate) * up) @ w_down


def causal_depthwise_conv(x, w):
    C = x.shape[-1]
    return lax.conv_general_dilated(
        x, w[:, None, :].astype(x.dtype), window_strides=(1,), padding=[(A_CONV - 1, 0)],
        dimension_numbers=("NWC", "WIO", "NWC"), feature_group_count=C)


def gated_delta_rule_chunked(q, k, v, g, beta):
    Bsz, T, H, DK = q.shape
    DV = v.shape[-1]
    C = A_CHUNK
    N = T // C
    f32 = jnp.float32

    def chunk(t):
        return t.astype(f32).reshape(Bsz, N, C, H, -1).transpose(0, 3, 1, 2, 4)

    q, k, v = chunk(q), chunk(k), chunk(v)
    g = g.astype(f32).reshape(Bsz, N, C, H).transpose(0, 3, 1, 2)
    beta = beta.astype(f32).reshape(Bsz, N, C, H).transpose(0, 3, 1, 2)
    g = jnp.cumsum(g, axis=-1)

    idx = jnp.arange(C)
    lower_incl = idx[:, None] >= idx[None, :]
    strict = idx[:, None] > idx[None, :]
    decay = jnp.exp(jnp.where(lower_incl, g[..., :, None] - g[..., None, :], -jnp.inf))

    kb = k * beta[..., None]
    L = jnp.where(strict, jnp.einsum("bhnid,bhnjd->bhnij", kb, k) * decay, 0.0)
    rhs = jnp.concatenate([v * beta[..., None], kb * jnp.exp(g)[..., None]], axis=-1)
    sol = lax.linalg.triangular_solve(L, rhs, left_side=True, lower=True,
                                      transpose_a=False, conjugate_a=False, unit_diagonal=True)
    u, w = sol[..., :DV], sol[..., DV:]

    a_qk = jnp.einsum("bhnid,bhnjd->bhnij", q, k) * decay
    q_dec = q * jnp.exp(g)[..., None]
    k_dec = k * jnp.exp(g[..., -1:] - g)[..., None]
    g_last = jnp.exp(g[..., -1])

    xs = (jnp.moveaxis(q_dec, 2, 0), jnp.moveaxis(k_dec, 2, 0), jnp.moveaxis(u, 2, 0),
          jnp.moveaxis(w, 2, 0), jnp.moveaxis(a_qk, 2, 0), jnp.moveaxis(g_last, 2, 0))

    def step(S, inp):
        qd, kd, u_c, w_c, a_c, gl = inp
        v_new = u_c - jnp.einsum("bhck,bhkv->bhcv", w_c, S)
        o = jnp.einsum("bhck,bhkv->bhcv", qd, S) + jnp.einsum("bhij,bhjv->bhiv", a_c, v_new)
        S = S * gl[..., None, None] + jnp.einsum("bhck,bhcv->bhkv", kd, v_new)
        return S, o

    S0 = jnp.zeros((Bsz, H, DK, DV), f32)
    _, o = lax.scan(step, S0, xs)
    return o.transpose(1, 0, 3, 2, 4).reshape(Bsz, T, H, DV)


def mixer_gated_deltanet(h, w_in, w_conv, A_log, dt_bias, out_norm, w_out):
    Bsz, T, _ = h.shape
    HK = A_HEADS * A_DK
    HV = A_HEADS * A_DV
    proj = h @ w_in
    qkv = proj[..., :2 * HK + HV]
    z = proj[..., 2 * HK + HV:2 * HK + 2 * HV]
    b = proj[..., 2 * HK + 2 * HV:2 * HK + 2 * HV + A_HEADS]
    a = proj[..., 2 * HK + 2 * HV + A_HEADS:]
    qkv = jax.nn.silu(causal_depthwise_conv(qkv, w_conv))
    q = l2norm(qkv[..., :HK].reshape(Bsz, T, A_HEADS, A_DK)) * (A_DK ** -0.5)
    k = l2norm(qkv[..., HK:2 * HK].reshape(Bsz, T, A_HEADS, A_DK))
    v = qkv[..., 2 * HK:].reshape(Bsz, T, A_HEADS, A_DV)
    beta = jax.nn.sigmoid(b.astype(jnp.float32))
    g = -jnp.exp(A_log.astype(jnp.float32)) * jax.nn.softplus(a.astype(jnp.float32) + dt_bias.astype(jnp.float32))
    o = gated_delta_rule_chunked(q, k, v, g, beta)
    zf = z.reshape(Bsz, T, A_HEADS, A_DV).astype(jnp.float32)
    o = rmsnorm(o, out_norm) * jax.nn.silu(zf)
    return o.reshape(Bsz, T, HV).astype(h.dtype) @ w_out


def mixer_sliding_window_sinks(h, w_in, b_in, sinks, w_out, b_out):
    Bsz, T, _ = h.shape
    G = B_HEADS // B_KV_HEADS
    NB = T // B_BLOCK
    HQ = B_HEADS * B_HD
    HKV = B_KV_HEADS * B_HD
    proj = h @ w_in + b_in
    q = proj[..., :HQ].reshape(Bsz, NB, B_BLOCK, B_KV_HEADS, G, B_HD)
    k = proj[..., HQ:HQ + HKV].reshape(Bsz, NB, B_BLOCK, B_KV_HEADS, B_HD)
    v = proj[..., HQ + HKV:].reshape(Bsz, NB, B_BLOCK, B_KV_HEADS, B_HD)

    def with_prev(t):
        prev = jnp.concatenate([jnp.zeros_like(t[:, :1]), t[:, :-1]], axis=1)
        return jnp.concatenate([prev, t], axis=2)

    kk, vv = with_prev(k), with_prev(v)
    s = jnp.einsum("bnqhgd,bnkhd->bnhgqk", q, kk).astype(jnp.float32) * (B_HD ** -0.5)
    qi = jnp.arange(B_BLOCK)[:, None]
    kj = jnp.arange(2 * B_BLOCK)[None, :]
    rel = qi + B_BLOCK - kj
    band = (rel >= 0) & (rel < B_WINDOW)
    blk = jnp.arange(NB)[:, None, None]
    valid = band[None] & ((blk > 0) | (kj >= B_BLOCK)[None])
    s = jnp.where(valid[None, :, None, None], s, -jnp.inf)
    sink = jnp.broadcast_to(sinks.astype(jnp.float32).reshape(B_KV_HEADS, G)[None, None, :, :, None, None],
                            s.shape[:-1] + (1,))
    p = jax.nn.softmax(jnp.concatenate([s, sink], axis=-1), axis=-1)[..., :-1]
    o = jnp.einsum("bnhgqk,bnkhd->bnqhgd", p.astype(vv.dtype), vv)
    return o.reshape(Bsz, T, HQ) @ w_out + b_out


def setup_inputs(seed: int = 0) -> dict:
    key = jax.random.key(seed)
    ks = iter(jax.random.split(key, 32))
    f32 = jnp.float32
    nrm = lambda shape, scale: jax.random.normal(next(ks), shape, f32) * scale
    gain = lambda shape: 1.0 + 0.02 * jax.random.normal(next(ks), shape, f32)
    D = D_MODEL
    a_in_cols = 2 * A_HEADS * A_DK + 2 * A_HEADS * A_DV + 2 * A_HEADS
    b_in_cols = (B_HEADS + 2 * B_KV_HEADS) * B_HD
    dt = jnp.exp(jax.random.uniform(next(ks), (N_A_LAYERS, A_HEADS), f32, np.log(1e-3), np.log(1e-1)))
    return {
        "x": jax.random.normal(next(ks), (BATCH, SEQ, D), f32),
        "ffn1_norm": gain((DEPTH, D)),
        "ffn1_w_gu": nrm((DEPTH, D, 2 * D_FF), D ** -0.5),
        "ffn1_w_down": nrm((DEPTH, D_FF, D), D_FF ** -0.5),
        "mix_norm": gain((DEPTH, D)),
        "ffn2_norm": gain((DEPTH, D)),
        "ffn2_w_gu": nrm((DEPTH, D, 2 * D_FF), D ** -0.5),
        "ffn2_w_down": nrm((DEPTH, D_FF, D), D_FF ** -0.5),
        "a_w_in": nrm((N_A_LAYERS, D, a_in_cols), D ** -0.5),
        "a_w_conv": nrm((N_A_LAYERS, A_CONV, 2 * A_HEADS * A_DK + A_HEADS * A_DV), A_CONV ** -0.5),
        "a_A_log": jnp.log(jax.random.uniform(next(ks), (N_A_LAYERS, A_HEADS), f32, 1.0, 16.0)),
        "a_dt_bias": dt + jnp.log(-jnp.expm1(-dt)),
        "a_out_norm": gain((N_A_LAYERS, A_DV)),
        "a_w_out": nrm((N_A_LAYERS, A_HEADS * A_DV, D), (A_HEADS * A_DV) ** -0.5),
        "b_w_in": nrm((N_B_LAYERS, D, b_in_cols), D ** -0.5),
        "b_b_in": nrm((N_B_LAYERS, b_in_cols), 0.02),
        "b_sinks": nrm((N_B_LAYERS, B_HEADS), 1.0),
        "b_w_out": nrm((N_B_LAYERS, B_HEADS * B_HD, D), (B_HEADS * B_HD) ** -0.5),
        "b_b_out": nrm((N_B_LAYERS, D), 0.02),
        "final_norm": gain((D,)),
    }


def reference(x, ffn1_norm, ffn1_w_gu, ffn1_w_down, mix_norm, ffn2_norm, ffn2_w_gu, ffn2_w_down,
              a_w_in, a_w_conv, a_A_log, a_dt_bias, a_out_norm, a_w_out,
              b_w_in, b_b_in, b_sinks, b_w_out, b_b_out, final_norm):
    for layer in range(DEPTH):
        x = x + 0.5 * swiglu(rmsnorm(x, ffn1_norm[layer]), ffn1_w_gu[layer], ffn1_w_down[layer])
        h = rmsnorm(x, mix_norm[layer])
        j = layer // N_MIXERS
        if layer % N_MIXERS == 0:
            y = mixer_gated_deltanet(h, a_w_in[j], a_w_conv[j], a_A_log[j], a_dt_bias[j],
                                     a_out_norm[j], a_w_out[j])
        else:
            y = mixer_sliding_window_sinks(h, b_w_in[j], b_b_in[j], b_sinks[j], b_w_out[j], b_b_out[j])
        x = x + y
        x = x + 0.5 * swiglu(rmsnorm(x, ffn2_norm[layer]), ffn2_w_gu[layer], ffn2_w_down[layer])
    return rmsnorm(x, final_norm)
```

```python
import numpy as np
import ml_dtypes
import concourse.bass as bass
import concourse.mybir as mybir
from concourse.bass_utils import run_bass_kernel_spmd

F32 = mybir.dt.float32
BF16 = mybir.dt.bfloat16
AF = mybir.ActivationFunctionType
ALU = mybir.AluOpType

D = 1024
DFF = 2816
NFC = 22
EPS = 1e-6
NCORES = 8
ENGS = ["pe", "act", "dve", "pool", "sp"]
_UID = [0]


def sbt(nc, name, shape, dt):
    _UID[0] += 1
    return nc.sbuf_tensor("%s_u%d" % (name, _UID[0]), shape, dt)


def psum_bank_of(k):
    if isinstance(k, tuple):
        if k[0] == "ps":
            return k[1]
        if k[0] == "ps4":
            return 4
        if k[0] == "ps5":
            return 5
        if k[0] == "psA":
            return 6 + k[1] // 2
        if k[0] == "pso":
            return 2
        if k[0] == "psd":
            return 3
        return None
    if isinstance(k, str) and k.startswith("ps3_"):
        return 3
    return None


class Sched:
    def __init__(self):
        self.ops = {e: [] for e in ENGS}
        self.lastw = {}
        self.readers = {}
        self.lastx = {}

    def op(self, eng, fn, r=(), w=(), dma=False):
        idx = len(self.ops[eng])
        me = (eng, idx)
        deps = set()
        for k in r:
            lw = self.lastw.get(k)
            if lw is not None:
                deps.add(lw)
        for k in w:
            lw = self.lastw.get(k)
            if lw is not None:
                deps.add(lw)
            for rd in self.readers.get(k, ()):
                deps.add(rd)
        banks = set()
        for k in tuple(r) + tuple(w):
            b = psum_bank_of(k)
            if b is not None:
                banks.add(b)
        for b in banks:
            lx = self.lastx.setdefault(b, {})
            for e2, o2 in lx.items():
                if e2 != eng:
                    deps.add(o2)
            lx[eng] = me
        deps.discard(me)
        if eng == "pe":
            deps = {d for d in deps if d[0] != "pe"}
        self.ops[eng].append(dict(fn=fn, deps=deps, dma=dma, sig=False))
        for k in w:
            self.lastw[k] = me
            self.readers[k] = []
        for k in r:
            self.readers.setdefault(k, []).append(me)
        return me

    def pe(self, fn, r=(), w=()):
        return self.op("pe", fn, r, w)

    def act(self, fn, r=(), w=()):
        return self.op("act", fn, r, w)

    def dve(self, fn, r=(), w=()):
        return self.op("dve", fn, r, w)

    def pool(self, fn, r=(), w=()):
        return self.op("pool", fn, r, w)

    def dma(self, fn, r=(), w=(), q="sp"):
        return self.op(q, fn, r, w, dma=True)

    def cc(self, fn, r=(), w=()):
        me = self.op("pool", fn, r, w, dma=True)
        self.ops["pool"][me[1]]["cc"] = True
        return me


class SemState:
    def __init__(self, nc, stack, ndma=12):
        self.nc = nc
        self.stack = stack
        self.phase = 0
        self.eng_sem = {}
        self.eng_cnt = {}
        self.new_phase()
        self.dma_sem = {}
        self.dma_val = {}
        self.dma_rr = {}
        for q in ("sp", "pool", "act"):
            self.dma_sem[q] = [stack.enter_context(nc.semaphore("d_%s%d" % (q, i))) for i in range(ndma)]
            self.dma_val[q] = [0] * ndma
            self.dma_rr[q] = 0

    def new_phase(self):
        self.phase += 1
        for e in ENGS:
            self.eng_sem[e] = self.stack.enter_context(self.nc.semaphore("s%d_%s" % (self.phase, e)))
            self.eng_cnt[e] = 0

    def new_cc_sem(self):
        self.ncc = getattr(self, "ncc", 0) + 1
        return self.stack.enter_context(self.nc.semaphore("cc%d" % self.ncc))


def emit_phase(nc, sched, ss, final_wait=()):
    ops = sched.ops
    for e in ENGS:
        for op in ops[e]:
            for d in op["deps"]:
                ops[d[0]][d[1]]["sig"] = True
    for e in ENGS:
        for op in ops[e]:
            if op.get("cc"):
                op["prev"] = 0
                op["sem"] = ss.new_cc_sem()
                op["val"] = 1
            elif op["dma"]:
                j = ss.dma_rr[e]
                ss.dma_rr[e] = (j + 1) % len(ss.dma_sem[e])
                op["prev"] = ss.dma_val[e][j]
                ss.dma_val[e][j] += 16
                op["sem"] = ss.dma_sem[e][j]
                op["val"] = ss.dma_val[e][j]
            elif op["sig"]:
                ss.eng_cnt[e] += 1
                op["sem"] = ss.eng_sem[e]
                op["val"] = ss.eng_cnt[e]
    fw = [op for e in ENGS for op in ops[e] if op["dma"]]

    def run(e, h):
        waited = {}
        for op in ops[e]:
            need = {}
            for d in op["deps"]:
                dop = ops[d[0]][d[1]]
                k = id(dop["sem"])
                if need.get(k, (None, 0))[1] < dop["val"]:
                    need[k] = (dop["sem"], dop["val"])
            if op["dma"] and op["prev"] > 0:
                k = id(op["sem"])
                if need.get(k, (None, 0))[1] < op["prev"]:
                    need[k] = (op["sem"], op["prev"])
            for k, (s, v) in need.items():
                if waited.get(k, 0) < v:
                    h.wait_ge(s, v)
                    waited[k] = v
            inst = op["fn"](h)
            if op.get("cc"):
                inst.then_inc(op["sem"], 1)
            elif op["dma"]:
                inst.then_inc(op["sem"], 16)
            elif op["sig"]:
                inst.then_inc(op["sem"], 1)
        if e == "sp":
            for dop in fw:
                h.wait_ge(dop["sem"], dop["val"])

    with nc.Block() as block:
        @block.tensor
        def _(h):
            run("pe", h)

        @block.scalar
        def _(h):
            run("act", h)

        @block.vector
        def _(h):
            run("dve", h)

        @block.gpsimd
        def _(h):
            run("pool", h)

        @block.sync
        def _(h):
            run("sp", h)
    ss.new_phase()


def rmsnorm_fm(S, C, xT, hT, nwcol, T, tag):
    ps, ones_bf, sqb, rstd = C["ps"], C["ones_bf"], C["sqb"], C["rstd"]
    for tt in range(T // 512):
        sl = slice(tt * 512, (tt + 1) * 512)
        for c in range(8):
            S.act(lambda h, c=c, sl=sl: h.activation(out=sqb[:, c, :], in_=xT[:, c, sl], func=AF.Square),
                  r=[("xT", c, tt)], w=[("sqb", c)])
        bank = tt % 2
        for c in range(8):
            S.pe(lambda h, c=c, bank=bank: h.matmul(ps[:, bank, :], lhsT=ones_bf[:, :], rhs=sqb[:, c, :],
                                                     start=(c == 0), stop=(c == 7)),
                 r=[("sqb", c), "ones_bf"], w=[("ps", bank)])
        S.act(lambda h, bank=bank: h.activation(out=rstd[:, bank, :], in_=ps[:, bank, :], func=AF.Ln,
                                                 scale=1.0 / D, bias=C["eps_col"][:, 0:1]),
              r=[("ps", bank), "eps_col"], w=[("rstd", bank)])
        S.act(lambda h, bank=bank: h.activation(out=rstd[:, bank, :], in_=rstd[:, bank, :], func=AF.Exp, scale=-0.5),
              r=[("rstd", bank)], w=[("rstd", bank)])
        for c in range(8):
            S.dve(lambda h, c=c, sl=sl, bank=bank: h.scalar_tensor_tensor(
                out=hT[:, c, sl], in0=xT[:, c, sl], scalar=nwcol[:, c:c + 1], in1=rstd[:, bank, :],
                op0=ALU.mult, op1=ALU.mult),
                r=[("xT", c, tt), ("rstd", bank), "smalls"], w=[("hT", c, tt)])


def ffn_fm(S, C, xT, hT, w_gu, w_down, T, tag):
    ps, aT, sg, wg, wd = C["ps"], C["aT"], C["sg"], C["wg"], C["wd"]
    NT = T // 512
    assert NT == 4
    wgu_v = w_gu.rearrange("(c p) n -> p c n", p=128)
    wd_v = w_down.rearrange("(fc p) d -> p fc d", p=128)
    gi = 0
    di = 0
    for half in range(2):
        for fp in range(6):
            nf = 2 if fp < 5 else 1
            f0 = half * 11 + fp * 2
            b = gi % 2
            gi += 1
            S.dma(lambda h, b=b, f0=f0, nf=nf: h.dma_start(out=wg[:, b, :, 0:128 * nf],
                                                            in_=wgu_v[:, :, f0 * 128:(f0 + nf) * 128]),
                  w=[("wg", b, 0)], q="pool")
            S.dma(lambda h, b=b, f0=f0, nf=nf: h.dma_start(out=wg[:, b, :, 256:256 + 128 * nf],
                                                            in_=wgu_v[:, :, DFF + f0 * 128:DFF + (f0 + nf) * 128]),
                  w=[("wg", b, 1)], q="pool")
            for k in range(nf):
                fi = fp * 2 + k
                sb = fi % 2
                for c in range(8):
                    for tt in range(4):
                        S.pe(lambda h, b=b, k=k, c=c, tt=tt: h.matmul(
                            ps[:, tt, :], lhsT=wg[:, b, c, k * 128:(k + 1) * 128], rhs=hT[:, c, tt * 512:(tt + 1) * 512],
                            start=(c == 0), stop=(c == 7)),
                            r=[("wg", b, 0), ("hT", c, tt)], w=[("ps", tt)])
                S.act(lambda h, sb=sb: h.activation(out=sg[:, sb, :, :], in_=ps[:, 0:4, :], func=AF.Silu),
                      r=[("ps", 0), ("ps", 1), ("ps", 2), ("ps", 3)], w=[("sg", sb)])
                for c in range(8):
                    for tt in range(4):
                        S.pe(lambda h, b=b, k=k, c=c, tt=tt: h.matmul(
                            ps[:, 4 + tt, :], lhsT=wg[:, b, c, 256 + k * 128:256 + (k + 1) * 128],
                            rhs=hT[:, c, tt * 512:(tt + 1) * 512], start=(c == 0), stop=(c == 7)),
                            r=[("wg", b, 1), ("hT", c, tt)], w=[("ps", 4 + tt)])
                S.dve(lambda h, sb=sb, fi=fi: h.tensor_tensor(out=aT[:, fi, :, :], in0=ps[:, 4:8, :], in1=sg[:, sb, :, :],
                                                              op=ALU.mult),
                      r=[("ps", 4), ("ps", 5), ("ps", 6), ("ps", 7), ("sg", sb)], w=[("aT", fi)])
        for dp in range(4):
            b = di % 2
            di += 1
            S.dma(lambda h, b=b, dp=dp, half=half: h.dma_start(out=wd[:, b, :, :],
                                                                in_=wd_v[:, half * 11:(half + 1) * 11, dp * 256:(dp + 1) * 256]),
                  w=[("wd", b)], q="pool")
            for k in range(2):
                dc = dp * 2 + k
                bs = 4 * (dc % 2)
                for fi in range(11):
                    for tt in range(4):
                        S.pe(lambda h, b=b, k=k, fi=fi, tt=tt, bs=bs: h.matmul(
                            ps[:, bs + tt, :], lhsT=wd[:, b, fi, k * 128:(k + 1) * 128], rhs=aT[:, fi, tt, :],
                            start=(fi == 0), stop=(fi == 10)),
                            r=[("wd", b), ("aT", fi)], w=[("ps", bs + tt)])
                S.dve(lambda h, dc=dc, bs=bs: h.scalar_tensor_tensor(
                    out=xT[:, dc, :], in0=ps[:, bs:bs + 4, :].rearrange("p a b -> p (a b)"), scalar=0.5, in1=xT[:, dc, :],
                    op0=ALU.mult, op1=ALU.add),
                    r=[("ps", bs), ("ps", bs + 1), ("ps", bs + 2), ("ps", bs + 3)] + [("xT", dc, t) for t in range(4)],
                    w=[("xT", dc, t) for t in range(4)])


def alloc_common(nc, st, T):
    C = {}
    C["ps"] = st.enter_context(nc.psum_tensor("ps", [128, 8, 512], F32))
    C["ones_bf"] = st.enter_context(sbt(nc, "ones_bf", [128, 128], BF16))
    C["eps_col"] = st.enter_context(sbt(nc, "eps_col", [128, 1], F32))
    return C


def alloc_ffn(nc, st, C, T, with_x=True):
    if with_x:
        C["xT"] = st.enter_context(sbt(nc, "xT", [128, 8, T], F32))
    C["hT"] = st.enter_context(sbt(nc, "hT", [128, 8, T], BF16))
    C["aT"] = st.enter_context(sbt(nc, "aT", [128, 11, T // 512, 512], BF16))
    C["sg"] = st.enter_context(sbt(nc, "sg", [128, 2, T // 512, 512], BF16))
    C["sqb"] = st.enter_context(sbt(nc, "sqb", [128, 8, 512], BF16))
    C["rstd"] = st.enter_context(sbt(nc, "rstd", [128, 2, 512], F32))
    C["wg"] = st.enter_context(sbt(nc, "wg", [128, 2, 8, 512], BF16))
    C["wd"] = st.enter_context(sbt(nc, "wd", [128, 2, 11, 256], BF16))


def init_consts(S, C):
    S.pool(lambda h: h.memset(C["ones_bf"][:, :], 1.0), w=["ones_bf"])
    S.pool(lambda h: h.memset(C["eps_col"][:, :], EPS), w=["eps_col"])


def load_xT(S, C, x_dram, T):
    xv = x_dram.rearrange("(c p) t -> p c t", p=128)
    for tt in range(T // 512):
        S.dma(lambda h, tt=tt: h.dma_start(out=C["xT"][:, :, tt * 512:(tt + 1) * 512], in_=xv[:, :, tt * 512:(tt + 1) * 512]),
              w=[("xT", c, tt) for c in range(8)], q=("sp" if tt % 2 == 0 else "act"))


def store_T(S, C, key, sb, out_dram, T):
    ov = out_dram.rearrange("(c p) t -> p c t", p=128)
    outs = []
    for tt in range(T // 512):
        outs.append(S.dma(lambda h, tt=tt: h.dma_start(out=ov[:, :, tt * 512:(tt + 1) * 512], in_=sb[:, :, tt * 512:(tt + 1) * 512]),
                          r=[(key, c, tt) for c in range(8)], q="sp"))
    return outs


def build_L1(T=2048):
    import contextlib
    nc = bass.Bass("TRN2", target_bir_lowering=False)
    x = nc.dram_tensor("xT_in", [D, T], F32, kind="ExternalInput").ap()
    smalls = nc.dram_tensor("smalls", [128, 16], F32, kind="ExternalInput").ap()
    w_gu = nc.dram_tensor("w_gu", [D, 2 * DFF], F32, kind="ExternalInput").ap()
    w_down = nc.dram_tensor("w_down", [DFF, D], F32, kind="ExternalInput").ap()
    x_out = nc.dram_tensor("xT_out", [D, T], F32, kind="ExternalOutput").ap()
    h_out = nc.dram_tensor("hT_out", [D, T], BF16, kind="ExternalOutput").ap()
    with contextlib.ExitStack() as st:
        ss = SemState(nc, st)
        C = alloc_common(nc, st, T)
        alloc_ffn(nc, st, C, T)
        C["smalls"] = st.enter_context(sbt(nc, "smalls_sb", [128, 16], F32))
        S = Sched()
        init_consts(S, C)
        S.dma(lambda h: h.dma_start(out=C["smalls"][:, :], in_=smalls[:, :]), w=["smalls"], q="sp")
        load_xT(S, C, x, T)
        rmsnorm_fm(S, C, C["xT"], C["hT"], C["smalls"][:, 0:8], T, "n1")
        ffn_fm(S, C, C["xT"], C["hT"], w_gu, w_down, T, "f1")
        rmsnorm_fm(S, C, C["xT"], C["hT"], C["smalls"][:, 8:16], T, "n2")
        o1 = store_T(S, C, "xT", C["xT"], x_out, T)
        o2 = store_T(S, C, "hT", C["hT"], h_out, T)
        emit_phase(nc, S, ss, final_wait=o1 + o2)
    return nc


def col8(v):
    return np.ascontiguousarray(np.asarray(v, np.float32).reshape(8, 128).T)


def build_L2(TT=8192, cut=0):
    import contextlib
    nc = bass.Bass("TRN2", target_bir_lowering=False)
    hT_d = nc.dram_tensor("hT", [D, TT], BF16, kind="ExternalInput").ap()
    w_my = nc.dram_tensor("w_my", [D, 1028], F32, kind="ExternalInput").ap()
    sm_d = nc.dram_tensor("sm", [128, 32], F32, kind="ExternalInput").ap()
    cm_d = nc.dram_tensor("cm", [128, 4, 128], F32, kind="ExternalInput").ap()
    og_d = nc.dram_tensor("ogT", [256, TT], BF16, kind="ExternalOutput").ap()
    hv = hT_d.rearrange("(c p) t -> p c t", p=128)
    with contextlib.ExitStack() as st:
        ss = SemState(nc, st)
        ps = st.enter_context(nc.psum_tensor("ps", [128, 8, 512], F32))
        gdn_phase4(nc, ss, ps, lambda tg: hv[:, :, tg * 512:(tg + 1) * 512],
                  lambda hd, tg: og_d[hd * 128:(hd + 1) * 128, tg * 512:(tg + 1) * 512],
                  w_my, sm_d, cm_d, TT // 512, cut=cut)
    return nc


def gdn_phase(nc, ss, ps, h_src, og_dst, w_my, sm_d, cm_d, NG, cut=0, pre_fn=None, h_keys=()):
    import contextlib
    with contextlib.ExitStack() as st:
        sb = lambda name, shape, dt: st.enter_context(sbt(nc, name, shape, dt))
        ones_bf = sb("ones_bf", [128, 128], BF16)
        ones_f = sb("ones_f", [128, 128], F32)
        eps_col = sb("eps_col", [128, 1], F32)
        one_col = sb("one_col", [128, 1], F32)
        sm = sb("sm_sb", [128, 32], F32)
        cm = sb("cm_sb", [128, 4, 128], F32)
        negA = sb("negA", [128, 2], F32)
        wq = sb("wq", [128, 8, 1028], BF16)
        hTg = sb("hTg", [128, 2, 8, 512], BF16)
        pre = sb("pre", [128, 2, 3, 515], F32)
        cs = sb("cs", [128, 3, 512], F32)
        sqb = sb("sqb", [128, 512], BF16)
        rr = sb("rr", [128, 512], F32)
        qkn = sb("qkn", [128, 2, 512], F32)
        qkb = sb("qkb", [128, 2, 512], BF16)
        zT = sb("zT", [128, 512], BF16)
        bat = sb("bat", [128, 4, 4], F32)
        beta = sb("beta", [128, 4, 2], F32)
        esp = sb("esp", [128, 4, 2], F32)
        gg = sb("gg", [128, 4, 2], F32)
        gcs = sb("gcs", [128, 4, 2], F32)
        Rm = sb("Rm", [128, 4, 128], F32)
        tdm = sb("tdm", [128, 4, 128], F32)
        decI = sb("decI", [128, 4, 128], F32)
        egcB = sb("egcB", [128, 4, 128], F32)
        L0 = sb("L0", [128, 4, 128], F32)
        Am = sb("Am", [128, 4, 128], F32)
        X = sb("X", [128, 4, 2, 3, 128], F32)
        glr = sb("glr", [128, 4], F32)
        gl = sb("gl", [128, 4], F32)
        ekd = sb("ekd", [128, 4], F32)
        egc = sb("egc", [128, 4], F32)
        bg = sb("bg", [128, 4], F32)
        Ttb = sb("Ttb", [128, 4, 128], BF16)
        ATb = sb("ATb", [128, 4, 128], BF16)
        qdT = sb("qdT", [128, 4, 128], BF16)
        kd = sb("kd", [128, 4, 128], BF16)
        kbg = sb("kbg", [128, 4, 128], BF16)
        vb = sb("vb", [128, 4, 128], BF16)
        uu = sb("uu", [128, 4, 128], F32)
        wTb = sb("wTb", [128, 4, 128], BF16)
        vnb = sb("vnb", [128, 128], BF16)
        Sst = sb("Sst", [128, 2, 128], F32)
        Sb = sb("Sb", [128, 2, 128], BF16)
        oT = sb("oT", [128, 512], F32)
        on = sb("on", [128, 512], F32)
        sz = sb("sz", [128, 512], F32)
        og = sb("og", [128, 2, 512], BF16)

        IDN, TRIU, NMI, STR = 0, 1, 2, 3
        S = Sched()
        if pre_fn is not None:
            pre_fn(S)
        S.pool(lambda h: h.memset(ones_bf[:, :], 1.0), w=["ones_bf"])
        S.pool(lambda h: h.memset(ones_f[:, :], 1.0), w=["ones_f"])
        S.pool(lambda h: h.memset(eps_col[:, :], EPS), w=["eps_col"])
        S.pool(lambda h: h.memset(one_col[:, :], 1.0), w=["one_col"])
        S.pool(lambda h: h.memset(pre[:, :, :, :], 0.0), w=[("pre", a, b) for a in range(2) for b in range(3)])
        S.pool(lambda h: h.memset(Sst[:, :, :], 0.0), w=[("S", 0), ("S", 1)])
        S.pool(lambda h: h.memset(Sb[:, :, :], 0.0), w=[("Sb", 0), ("Sb", 1)])
        S.dma(lambda h: h.dma_start(out=sm[:, :], in_=sm_d[:, :]), w=["sm"])
        S.dma(lambda h: h.dma_start(out=cm[:, :, :], in_=cm_d[:, :, :]), w=["cm"])
        wv = w_my.rearrange("(c p) n -> p c n", p=128)
        for c in range(8):
            S.dma(lambda h, c=c: h.dma_start(out=wq[:, c, :], in_=wv[:, c, :]), w=[("wq", c)], q="pool")
        S.act(lambda h: h.activation(out=negA[:, :], in_=sm[:, 24:26], func=AF.Exp), r=["sm"], w=["negA"])
        S.dve(lambda h: h.tensor_scalar(out=negA[:, :], in0=negA[:, :], scalar1=-1.0, scalar2=None, op0=ALU.mult),
              r=["negA"], w=["negA"])
        WQ = [("wq", c) for c in range(8)]
        outs = []
        for tg in range(NG):
            b = tg % 2
            S.dma(lambda h, b=b, tg=tg: h.dma_start(out=hTg[:, b, :, :], in_=h_src(tg)),
                  r=list(h_keys), w=[("hTg", b)])
            for t in range(4):
                for c in range(8):
                    S.pe(lambda h, b=b, t=t, c=c: h.matmul(ps[:, 3, t * 4:(t + 1) * 4], lhsT=hTg[:, b, c, t * 128:(t + 1) * 128],
                                                            rhs=wq[:, c, 1024:1028], start=(c == 0), stop=(c == 7)),
                         r=[("hTg", b)] + WQ, w=["ps3_ba"])
            S.act(lambda h: h.copy(out=bat[:, :, :], in_=ps[:, 3, 0:16].rearrange("p (t f) -> p t f", f=4)),
                  r=["ps3_ba"], w=["bat"])
            S.act(lambda h: h.activation(out=beta[:, :, :], in_=bat[:, :, 0:2], func=AF.Sigmoid), r=["bat"], w=["beta"])
            for hd in range(2):
                S.act(lambda h, hd=hd: h.activation(out=esp[:, :, hd:hd + 1], in_=bat[:, :, 2 + hd:3 + hd], func=AF.Exp,
                                                    bias=sm[:, 26 + hd:27 + hd]), r=["bat", "sm"], w=[("esp", hd)])
                S.act(lambda h, hd=hd: h.activation(out=esp[:, :, hd:hd + 1], in_=esp[:, :, hd:hd + 1], func=AF.Ln,
                                                    bias=one_col[:, 0:1]), r=[("esp", hd), "one_col"], w=[("esp", hd)])
                S.dve(lambda h, hd=hd: h.tensor_scalar(out=gg[:, :, hd:hd + 1], in0=esp[:, :, hd:hd + 1],
                                                       scalar1=negA[:, hd:hd + 1], scalar2=None, op0=ALU.mult),
                      r=[("esp", hd), "negA"], w=[("gg", hd)])
            for t in range(4 if cut != 1 else 0):
                S.pe(lambda h, t=t: h.matmul(ps[:, 3, 32 + 2 * t:34 + 2 * t], lhsT=cm[:, TRIU, :], rhs=gg[:, t, :],
                                             start=True, stop=True),
                     r=["cm", ("gg", 0), ("gg", 1)], w=["ps3_gc"])
            S.act(lambda h: h.copy(out=gcs[:, :, :], in_=ps[:, 3, 32:40].rearrange("p (t f) -> p t f", f=2)),
                  r=["ps3_gc"], w=["gcs"])

            for hd in range(2):
                for j in range(4):
                    bank = j % 2
                    for c in range(8):
                        S.pe(lambda h, b=b, hd=hd, j=j, c=c, bank=bank: h.matmul(
                            ps[:, bank, :], lhsT=wq[:, c, hd * 512 + j * 128:hd * 512 + (j + 1) * 128], rhs=hTg[:, b, c, :],
                            start=(c == 0), stop=(c == 7)), r=[("hTg", b)] + WQ, w=[("ps", bank)])
                    if j < 3:
                        S.act(lambda h, hd=hd, j=j, bank=bank: h.copy(out=pre[:, hd, j, 3:515], in_=ps[:, bank, :]),
                              r=[("ps", bank)], w=[("pre", hd, j)])
                    else:
                        S.act(lambda h, bank=bank: h.copy(out=zT[:, :], in_=ps[:, bank, :]), r=[("ps", bank)], w=["zT"])
                for j in range(3):
                    cw = lambda k, hd=hd, j=j: sm[:, (hd * 3 + j) * 4 + k:(hd * 3 + j) * 4 + k + 1]
                    S.dve(lambda h, hd=hd, j=j, cw=cw: h.tensor_scalar(out=cs[:, j, :], in0=pre[:, hd, j, 3:515],
                                                                      scalar1=cw(3), scalar2=None, op0=ALU.mult),
                          r=[("pre", hd, j), "sm"], w=[("cs", j)])
                    for k in (2, 1, 0):
                        S.dve(lambda h, hd=hd, j=j, k=k, cw=cw: h.scalar_tensor_tensor(
                            out=cs[:, j, :], in0=pre[:, hd, j, k:k + 512], scalar=cw(k), in1=cs[:, j, :],
                            op0=ALU.mult, op1=ALU.add), r=[("pre", hd, j), "sm", ("cs", j)], w=[("cs", j)])
                    S.dve(lambda h, hd=hd, j=j: h.tensor_copy(out=pre[:, hd, j, 0:3], in_=pre[:, hd, j, 512:515]),
                          r=[("pre", hd, j)], w=[("pre", hd, j)])
                    S.act(lambda h, j=j: h.activation(out=cs[:, j, :], in_=cs[:, j, :], func=AF.Silu),
                          r=[("cs", j)], w=[("cs", j)])
                for j in range(2):
                    S.act(lambda h, j=j: h.activation(out=sqb[:, :], in_=cs[:, j, :], func=AF.Square), r=[("cs", j)], w=["sqb"])
                    S.pe(lambda h: h.matmul(ps[:, 2, :], lhsT=ones_bf[:, :], rhs=sqb[:, :], start=True, stop=True),
                         r=["sqb", "ones_bf"], w=[("ps", 2)])
                    S.act(lambda h: h.activation(out=rr[:, :], in_=ps[:, 2, :], func=AF.Sqrt, bias=eps_col[:, 0:1]),
                          r=[("ps", 2), "eps_col"], w=["rr"])
                    S.dve(lambda h: h.reciprocal(out=rr[:, :], in_=rr[:, :]), r=["rr"], w=["rr"])
                    sc = (128.0 ** -0.5) if j == 0 else 1.0
                    S.dve(lambda h, j=j, sc=sc: h.scalar_tensor_tensor(out=qkn[:, j, :], in0=cs[:, j, :], scalar=sc, in1=rr[:, :],
                                                                       op0=ALU.mult, op1=ALU.mult),
                          r=[("cs", j), "rr"], w=[("qkn", j)])
                    S.act(lambda h, j=j: h.copy(out=qkb[:, j, :], in_=qkn[:, j, :]), r=[("qkn", j)], w=[("qkb", j)])
                LV = {0: 9, 1: 0, 2: 1, 3: 2, 4: 3, 5: 4}[cut]
                CH = range(4)
                tsl = lambda ci: slice(ci * 128, (ci + 1) * 128)
                for ci in (CH if LV >= 2 else ()):
                    S.dve(lambda h, ci=ci, hd=hd: h.tensor_scalar(out=Rm[:, ci, :], in0=cm[:, TRIU, :],
                                                                  scalar1=gg[:, ci, hd:hd + 1], scalar2=None, op0=ALU.mult),
                          r=["cm", ("gg", hd)], w=[("Rm", ci)])
                    S.pe(lambda h, ci=ci: h.matmul(ps[:, 4, tsl(ci)], lhsT=ones_f[:, :], rhs=Rm[:, ci, :], start=True, stop=True),
                         r=[("Rm", ci), "ones_f"], w=[("ps4", ci)])
                for ci in (CH if LV >= 2 else ()):
                    S.dve(lambda h, ci=ci, hd=hd: h.scalar_tensor_tensor(
                        out=tdm[:, ci, :], in0=ps[:, 4, tsl(ci)], scalar=gcs[:, ci, hd:hd + 1], in1=cm[:, NMI, :],
                        op0=ALU.subtract, op1=ALU.max), r=[("ps4", ci), "gcs", "cm"], w=[("tdm", ci)])
                    S.act(lambda h, ci=ci: h.activation(out=decI[:, ci, :], in_=tdm[:, ci, :], func=AF.Exp, scale=-1.0),
                          r=[("tdm", ci)], w=[("decI", ci)])
                    S.act(lambda h, ci=ci: h.activation(out=egcB[:, ci, :], in_=ps[:, 4, tsl(ci)], func=AF.Exp),
                          r=[("ps4", ci)], w=[("egcB", ci)])
                    S.act(lambda h, ci=ci: h.copy(out=glr[:, ci:ci + 1], in_=ps[:, 4, ci * 128 + 127:ci * 128 + 128]),
                          r=[("ps4", ci)], w=[("glr", ci)])
                    S.act(lambda h, ci=ci: h.activation(out=gl[:, ci:ci + 1], in_=glr[:, ci:ci + 1], func=AF.Exp),
                          r=[("glr", ci)], w=[("gl", ci)])
                    S.act(lambda h, ci=ci, hd=hd: h.activation(out=ekd[:, ci:ci + 1], in_=gcs[:, ci, hd:hd + 1], func=AF.Exp,
                                                               scale=-1.0, bias=glr[:, ci:ci + 1]),
                          r=["gcs", ("glr", ci)], w=[("ekd", ci)])
                    S.act(lambda h, ci=ci, hd=hd: h.activation(out=egc[:, ci:ci + 1], in_=gcs[:, ci, hd:hd + 1], func=AF.Exp),
                          r=["gcs"], w=[("egc", ci)])
                    S.dve(lambda h, ci=ci, hd=hd: h.tensor_tensor(out=bg[:, ci:ci + 1], in0=egc[:, ci:ci + 1],
                                                                  in1=beta[:, ci, hd:hd + 1], op=ALU.mult),
                          r=[("egc", ci), "beta"], w=[("bg", ci)])
                    S.dve(lambda h, ci=ci: h.tensor_tensor(out=qdT[:, ci, :], in0=qkn[:, 0, tsl(ci)], in1=egcB[:, ci, :], op=ALU.mult),
                          r=[("qkn", 0), ("egcB", ci)], w=[("qdT", ci)])
                for ci in (CH if LV >= 3 else ()):
                    S.pe(lambda h, ci=ci: h.matmul(ps[:, 5, tsl(ci)], lhsT=qkb[:, 1, tsl(ci)], rhs=qkb[:, 1, tsl(ci)],
                                                   start=True, stop=True), r=[("qkb", 1)], w=[("ps5", ci)])
                    S.dve(lambda h, ci=ci, hd=hd: h.scalar_tensor_tensor(
                        out=L0[:, ci, :], in0=ps[:, 5, tsl(ci)], scalar=beta[:, ci, hd:hd + 1], in1=decI[:, ci, :],
                        op0=ALU.mult, op1=ALU.mult), r=[("ps5", ci), "beta", ("decI", ci)], w=[("L0", ci)])
                    S.dve(lambda h, ci=ci: h.scalar_tensor_tensor(
                        out=X[:, ci, 0, 0, :], in0=L0[:, ci, :], scalar=-1.0, in1=cm[:, STR, :], op0=ALU.mult, op1=ALU.mult),
                        r=[("L0", ci), "cm"], w=[("XP", ci, 0)])
                    S.pe(lambda h, ci=ci: h.matmul(ps[:, 5, tsl(ci)], lhsT=qkb[:, 0, tsl(ci)], rhs=qkb[:, 1, tsl(ci)],
                                                   start=True, stop=True), r=[("qkb", 0), ("qkb", 1)], w=[("ps5", ci)])
                    S.dve(lambda h, ci=ci: h.tensor_tensor(out=Am[:, ci, :], in0=ps[:, 5, tsl(ci)], in1=decI[:, ci, :], op=ALU.mult),
                          r=[("ps5", ci), ("decI", ci)], w=[("Am", ci)])
                for ci in (CH if LV >= 3 else ()):
                    S.pe(lambda h, ci=ci: h.matmul(ps[:, 5, tsl(ci)], lhsT=X[:, ci, 0, 0, :], rhs=cm[:, IDN, :], start=True, stop=True),
                         r=[("XP", ci, 0), "cm"], w=[("ps5", ci)])
                    S.act(lambda h, ci=ci: h.copy(out=X[:, ci, 0, 1, :], in_=ps[:, 5, tsl(ci)]), r=[("ps5", ci)], w=[("XQ", ci, 0)])
                    S.dve(lambda h, ci=ci: h.tensor_tensor(out=X[:, ci, 1, 2, :], in0=ps[:, 5, tsl(ci)], in1=cm[:, IDN, :], op=ALU.add),
                          r=[("ps5", ci), "cm"], w=[("XT", ci, 1)])
                    S.pe(lambda h, ci=ci: h.matmul(ps[:, 5, tsl(ci)], lhsT=Am[:, ci, :], rhs=cm[:, IDN, :], start=True, stop=True),
                         r=[("Am", ci), "cm"], w=[("ps5", ci)])
                    S.act(lambda h, ci=ci: h.copy(out=ATb[:, ci, :], in_=ps[:, 5, tsl(ci)]), r=[("ps5", ci)], w=[("ATb", ci)])
                for ci in (CH if LV >= 3 else ()):
                    S.pe(lambda h, ci=ci: h.matmul(ps[:, 5, tsl(ci)], lhsT=qkn[:, 1, tsl(ci)], rhs=cm[:, IDN, :], start=True, stop=True),
                         r=[("qkn", 1), "cm"], w=[("ps5", ci)])
                    S.dve(lambda h, ci=ci: h.tensor_scalar(out=kd[:, ci, :], in0=ps[:, 5, tsl(ci)], scalar1=ekd[:, ci:ci + 1],
                                                           scalar2=None, op0=ALU.mult), r=[("ps5", ci), ("ekd", ci)], w=[("kd", ci)])
                    S.dve(lambda h, ci=ci: h.tensor_scalar(out=kbg[:, ci, :], in0=ps[:, 5, tsl(ci)], scalar1=bg[:, ci:ci + 1],
                                                           scalar2=None, op0=ALU.mult), r=[("ps5", ci), ("bg", ci)], w=[("kbg", ci)])
                    S.pe(lambda h, ci=ci: h.matmul(ps[:, 5, tsl(ci)], lhsT=cs[:, 2, tsl(ci)], rhs=cm[:, IDN, :], start=True, stop=True),
                         r=[("cs", 2), "cm"], w=[("ps5", ci)])
                    S.dve(lambda h, ci=ci, hd=hd: h.tensor_scalar(out=vb[:, ci, :], in0=ps[:, 5, tsl(ci)], scalar1=beta[:, ci, hd:hd + 1],
                                                                  scalar2=None, op0=ALU.mult), r=[("ps5", ci), "beta"], w=[("vb", ci)])
                psA = lambda ci: ps[:, 6 + ci // 2, (ci % 2) * 256:(ci % 2) * 256 + 256]
                for m in (range(7) if LV >= 4 else ()):
                    cur, nxt = m % 2, (m + 1) % 2
                    for ci in CH:
                        if m == 0:
                            S.pe(lambda h, ci=ci: h.matmul(psA(ci)[:, 0:128], lhsT=X[:, ci, 0, 0, :], rhs=X[:, ci, 0, 1, :],
                                                           start=True, stop=True),
                                 r=[("XP", ci, 0), ("XQ", ci, 0)], w=[("psA", ci)])
                        elif m < 6:
                            S.pe(lambda h, ci=ci, cur=cur: h.matmul(psA(ci), lhsT=X[:, ci, cur, 0, :],
                                                                    rhs=X[:, ci, cur, 1:3, :].rearrange("p a b -> p (a b)"),
                                                                    start=True, stop=True),
                                 r=[("XP", ci, cur), ("XQ", ci, cur), ("XT", ci, cur)], w=[("psA", ci)])
                        else:
                            S.pe(lambda h, ci=ci, cur=cur: h.matmul(psA(ci)[:, 128:256], lhsT=X[:, ci, cur, 0, :], rhs=X[:, ci, cur, 2, :],
                                                                    start=True, stop=True),
                                 r=[("XP", ci, cur), ("XT", ci, cur)], w=[("psA", ci)])
                        if m < 6:
                            S.pe(lambda h, ci=ci, cur=cur: h.matmul(ps[:, 5, tsl(ci)], lhsT=X[:, ci, cur, 1, :], rhs=X[:, ci, cur, 0, :],
                                                                    start=True, stop=True),
                                 r=[("XP", ci, cur), ("XQ", ci, cur)], w=[("ps5", ci)])
                    for ci in CH:
                        if m < 6:
                            S.act(lambda h, ci=ci, nxt=nxt: h.copy(out=X[:, ci, nxt, 1, :], in_=psA(ci)[:, 0:128]),
                                  r=[("psA", ci)], w=[("XQ", ci, nxt)])
                            S.act(lambda h, ci=ci, nxt=nxt: h.copy(out=X[:, ci, nxt, 0, :], in_=ps[:, 5, tsl(ci)]),
                                  r=[("ps5", ci)], w=[("XP", ci, nxt)])
                        if 1 <= m < 6:
                            S.dve(lambda h, ci=ci, cur=cur, nxt=nxt: h.tensor_tensor(out=X[:, ci, nxt, 2, :], in0=psA(ci)[:, 128:256],
                                                                                    in1=X[:, ci, cur, 2, :], op=ALU.add),
                                  r=[("psA", ci), ("XT", ci, cur)], w=[("XT", ci, nxt)])
                        if m == 6:
                            S.dve(lambda h, ci=ci, cur=cur: h.tensor_tensor(out=Ttb[:, ci, :], in0=psA(ci)[:, 128:256],
                                                                           in1=X[:, ci, cur, 2, :], op=ALU.add),
                                  r=[("psA", ci), ("XT", ci, cur)], w=[("Ttb", ci)])
                for ci in (CH if LV >= 9 else ()):
                    S.pe(lambda h, ci=ci: h.matmul(ps[:, 5, tsl(ci)], lhsT=Ttb[:, ci, :], rhs=vb[:, ci, :], start=True, stop=True),
                         r=[("Ttb", ci), ("vb", ci)], w=[("ps5", ci)])
                    S.act(lambda h, ci=ci: h.copy(out=uu[:, ci, :], in_=ps[:, 5, tsl(ci)]), r=[("ps5", ci)], w=[("uu", ci)])
                    S.pe(lambda h, ci=ci: h.matmul(psA(ci)[:, 0:128], lhsT=kbg[:, ci, :], rhs=Ttb[:, ci, :], start=True, stop=True),
                         r=[("Ttb", ci), ("kbg", ci)], w=[("psA", ci)])
                    S.act(lambda h, ci=ci: h.copy(out=wTb[:, ci, :], in_=psA(ci)[:, 0:128]), r=[("psA", ci)], w=[("wTb", ci)])
                for ci in (CH if LV >= 9 else ()):
                    S.pe(lambda h, ci=ci, hd=hd: h.matmul(ps[:, 3, 128:256], lhsT=wTb[:, ci, :], rhs=Sb[:, hd, :], start=True, stop=True),
                         r=[("wTb", ci), ("Sb", hd)], w=["ps3_1"])
                    S.dve(lambda h, ci=ci: h.tensor_tensor(out=vnb[:, :], in0=uu[:, ci, :], in1=ps[:, 3, 128:256], op=ALU.subtract),
                          r=[("uu", ci), "ps3_1"], w=["vnb"])
                    S.pe(lambda h, ci=ci, hd=hd: h.matmul(ps[:, 3, 384:512], lhsT=Sb[:, hd, :], rhs=qdT[:, ci, :], start=True, stop=False),
                         r=[("Sb", hd), ("qdT", ci)], w=["ps3_o"])
                    S.pe(lambda h, ci=ci: h.matmul(ps[:, 3, 384:512], lhsT=vnb[:, :], rhs=ATb[:, ci, :], start=False, stop=True),
                         r=["vnb", ("ATb", ci)], w=["ps3_o"])
                    S.act(lambda h, ci=ci: h.copy(out=oT[:, tsl(ci)], in_=ps[:, 3, 384:512]), r=["ps3_o"], w=["oT"])
                    S.pe(lambda h, ci=ci: h.matmul(ps[:, 3, 256:384], lhsT=kd[:, ci, :], rhs=vnb[:, :], start=True, stop=True),
                         r=[("kd", ci), "vnb"], w=["ps3_s"])
                    S.dve(lambda h, ci=ci, hd=hd: h.scalar_tensor_tensor(out=Sst[:, hd, :], in0=Sst[:, hd, :], scalar=gl[:, ci:ci + 1],
                                                                         in1=ps[:, 3, 256:384], op0=ALU.mult, op1=ALU.add),
                          r=[("S", hd), ("gl", ci), "ps3_s"], w=[("S", hd)])
                    S.act(lambda h, hd=hd: h.copy(out=Sb[:, hd, :], in_=Sst[:, hd, :]), r=[("S", hd)], w=[("Sb", hd)])
                S.act(lambda h: h.activation(out=sqb[:, :], in_=oT[:, :], func=AF.Square), r=["oT"], w=["sqb"])
                S.pe(lambda h: h.matmul(ps[:, 2, :], lhsT=ones_bf[:, :], rhs=sqb[:, :], start=True, stop=True),
                     r=["sqb", "ones_bf"], w=[("ps", 2)])
                S.act(lambda h: h.activation(out=rr[:, :], in_=ps[:, 2, :], func=AF.Sqrt, scale=1.0 / 128, bias=eps_col[:, 0:1]),
                      r=[("ps", 2), "eps_col"], w=["rr"])
                S.dve(lambda h: h.reciprocal(out=rr[:, :], in_=rr[:, :]), r=["rr"], w=["rr"])
                S.dve(lambda h: h.scalar_tensor_tensor(out=on[:, :], in0=oT[:, :], scalar=sm[:, 28:29], in1=rr[:, :],
                                                       op0=ALU.mult, op1=ALU.mult), r=["oT", "sm", "rr"], w=["on"])
                S.act(lambda h: h.activation(out=sz[:, :], in_=zT[:, :], func=AF.Silu), r=["zT"], w=["sz"])
                ob = (tg * 2 + hd) % 2
                S.dve(lambda h, ob=ob: h.tensor_tensor(out=og[:, ob, :], in0=on[:, :], in1=sz[:, :], op=ALU.mult),
                      r=["on", "sz"], w=[("og", ob)])
                outs.append(S.dma(lambda h, ob=ob, hd=hd, tg=tg: h.dma_start(out=og_dst(hd, tg),
                                                                            in_=og[:, ob, :]), r=[("og", ob)]))
        emit_phase(nc, S, ss)


def gdn_phase2(nc, ss, ps, h_src, og_dst, w_my, sm_d, cm_d, NG, cut=0, pre_fn=None, h_keys=()):
    import contextlib
    from itertools import zip_longest
    with contextlib.ExitStack() as st:
        sb = lambda name, shape, dt: st.enter_context(sbt(nc, name, shape, dt))
        ones_bf = sb("ones_bf", [128, 128], BF16)
        ones_f = sb("ones_f", [128, 128], F32)
        eps_col = sb("eps_col", [128, 1], F32)
        one_col = sb("one_col", [128, 1], F32)
        sm = sb("sm_sb", [128, 32], F32)
        cm = sb("cm_sb", [128, 4, 128], F32)
        negA = sb("negA", [128, 2], F32)
        wq = sb("wq", [128, 8, 1028], BF16)
        hTg = sb("hTg", [128, 2, 8, 512], BF16)
        pre = sb("pre", [128, 2, 3, 515], F32)
        cs = sb("cs", [128, 2, 3, 512], F32)
        sqb = sb("sqb", [128, 2, 512], BF16)
        rr = sb("rr", [128, 2, 512], F32)
        qkn = sb("qkn", [128, 2, 2, 512], F32)
        qkb = sb("qkb", [128, 2, 2, 512], BF16)
        zT = sb("zT", [128, 2, 512], BF16)
        bat = sb("bat", [128, 4, 4], F32)
        beta = sb("beta", [128, 4, 2], F32)
        esp = sb("esp", [128, 4, 2], F32)
        gg = sb("gg", [128, 4, 2], F32)
        gcs = sb("gcs", [128, 4, 2], F32)
        Rm = sb("Rm", [128, 2, 4, 128], F32)
        tdm = sb("tdm", [128, 2, 4, 128], F32)
        egcB = sb("egcB", [128, 2, 4, 128], F32)
        W4 = sb("W4", [128, 2, 4, 128], F32)
        Am = sb("Am", [128, 2, 4, 128], F32)
        X = sb("X", [128, 2, 2, 4, 3, 128], F32)
        glr = sb("glr", [128, 2, 4], F32)
        gl = sb("gl", [128, 2, 4], F32)
        ekd = sb("ekd", [128, 2, 4], F32)
        egc = sb("egc", [128, 2, 4], F32)
        bg = sb("bg", [128, 2, 4], F32)
        Ttb = sb("Ttb", [128, 2, 4, 128], BF16)
        ATb = sb("ATb", [128, 2, 4, 128], BF16)
        qdT = sb("qdT", [128, 2, 4, 128], BF16)
        kd = sb("kd", [128, 2, 4, 128], BF16)
        kbg = sb("kbg", [128, 2, 4, 128], BF16)
        vb = sb("vb", [128, 2, 4, 128], BF16)
        uu = sb("uu", [128, 2, 4, 128], F32)
        wTb = sb("wTb", [128, 2, 4, 128], BF16)
        vnb = sb("vnb", [128, 2, 128], BF16)
        Sst = sb("Sst", [128, 2, 128], F32)
        Sb = sb("Sb", [128, 2, 128], BF16)
        oT = sb("oT", [128, 2, 512], F32)
        sz = sb("sz", [128, 2, 512], F32)
        og = sb("og", [128, 2, 512], BF16)

        IDN, TRIU, NMI, STR = 0, 1, 2, 3
        B4 = [128, 4, 128]
        mask4 = lambda mi: cm[:, mi, :].unsqueeze(1).to_broadcast(B4)
        PROJ, MISC = 0, 1
        XB = lambda hd: 2 + hd
        YB = lambda hd: 4 + 2 * hd
        psX = lambda hd: ps[:, XB(hd), :].rearrange("p (t c) -> p t c", c=128)
        psY = lambda hd: ps[:, YB(hd):YB(hd) + 2, :].rearrange("p b (t c) -> p (b t) c", c=256)
        KX = lambda hd: [("ps", XB(hd))]
        KY = lambda hd: [("ps", YB(hd)), ("ps", YB(hd) + 1)]
        S = Sched()
        if pre_fn is not None:
            pre_fn(S)
        S.pool(lambda h: h.memset(ones_bf[:, :], 1.0), w=["ones_bf"])
        S.pool(lambda h: h.memset(ones_f[:, :], 1.0), w=["ones_f"])
        S.pool(lambda h: h.memset(eps_col[:, :], EPS), w=["eps_col"])
        S.pool(lambda h: h.memset(one_col[:, :], 1.0), w=["one_col"])
        S.pool(lambda h: h.memset(pre[:, :, :, :], 0.0), w=[("pre", a, b) for a in range(2) for b in range(3)])
        S.pool(lambda h: h.memset(Sst[:, :, :], 0.0), w=[("S", 0), ("S", 1)])
        S.pool(lambda h: h.memset(Sb[:, :, :], 0.0), w=[("Sb", 0), ("Sb", 1)])
        S.dma(lambda h: h.dma_start(out=sm[:, :], in_=sm_d[:, :]), w=["sm"])
        S.dma(lambda h: h.dma_start(out=cm[:, :, :], in_=cm_d[:, :, :]), w=["cm"])
        wv = w_my.rearrange("(c p) n -> p c n", p=128)
        for c in range(8):
            S.dma(lambda h, c=c: h.dma_start(out=wq[:, c, :], in_=wv[:, c, :]), w=[("wq", c)], q="pool")
        S.act(lambda h: h.activation(out=negA[:, :], in_=sm[:, 24:26], func=AF.Exp), r=["sm"], w=["negA"])
        S.dve(lambda h: h.tensor_scalar(out=negA[:, :], in0=negA[:, :], scalar1=-1.0, scalar2=None, op0=ALU.mult),
              r=["negA"], w=["negA"])
        WQ = [("wq", c) for c in range(8)]
        tsl = lambda ci: slice(ci * 128, (ci + 1) * 128)

        def unit(tg, hd, b):
            pb = hd
            bcol = lambda t4: t4[:, :, hd:hd + 1].to_broadcast(B4)
            bvec = lambda v: v[:, hd, :].unsqueeze(2).to_broadcast(B4)
            for j in range(4):
                for c in range(8):
                    S.pe(lambda h, j=j, c=c: h.matmul(ps[:, PROJ, :], lhsT=wq[:, c, hd * 512 + j * 128:hd * 512 + (j + 1) * 128],
                                                      rhs=hTg[:, b, c, :], start=(c == 0), stop=(c == 7)),
                         r=[("hTg", b)] + WQ, w=[("ps", PROJ)])
                if j < 3:
                    S.act(lambda h, j=j: h.copy(out=pre[:, hd, j, 3:515], in_=ps[:, PROJ, :]), r=[("ps", PROJ)], w=[("pre", hd, j)])
                else:
                    S.act(lambda h: h.copy(out=zT[:, hd, :], in_=ps[:, PROJ, :]), r=[("ps", PROJ)], w=[("zT", hd)])
                yield
            for j in range(3):
                cw = lambda k, j=j: sm[:, (hd * 3 + j) * 4 + k:(hd * 3 + j) * 4 + k + 1]
                S.dve(lambda h, j=j, cw=cw: h.tensor_scalar(out=cs[:, hd, j, :], in0=pre[:, hd, j, 3:515], scalar1=cw(3), scalar2=None,
                                                           op0=ALU.mult), r=[("pre", hd, j), "sm"], w=[("cs", hd, j)])
                for k in (2, 1, 0):
                    S.dve(lambda h, j=j, k=k, cw=cw: h.scalar_tensor_tensor(out=cs[:, hd, j, :], in0=pre[:, hd, j, k:k + 512], scalar=cw(k),
                                                                          in1=cs[:, hd, j, :], op0=ALU.mult, op1=ALU.add),
                          r=[("pre", hd, j), "sm", ("cs", hd, j)], w=[("cs", hd, j)])
                S.dve(lambda h, j=j: h.tensor_copy(out=pre[:, hd, j, 0:3], in_=pre[:, hd, j, 512:515]),
                      r=[("pre", hd, j)], w=[("pre", hd, j)])
                S.act(lambda h, j=j: h.activation(out=cs[:, hd, j, :], in_=cs[:, hd, j, :], func=AF.Silu),
                      r=[("cs", hd, j)], w=[("cs", hd, j)])
                yield
            for j in range(2):
                S.act(lambda h, j=j: h.activation(out=sqb[:, hd, :], in_=cs[:, hd, j, :], func=AF.Square), r=[("cs", hd, j)], w=[("sqb", hd)])
                S.pe(lambda h: h.matmul(ps[:, PROJ, :], lhsT=ones_bf[:, :], rhs=sqb[:, hd, :], start=True, stop=True),
                     r=[("sqb", hd), "ones_bf"], w=[("ps", PROJ)])
                S.act(lambda h: h.activation(out=rr[:, hd, :], in_=ps[:, PROJ, :], func=AF.Sqrt, bias=eps_col[:, 0:1]),
                      r=[("ps", PROJ), "eps_col"], w=[("rr", hd)])
                S.dve(lambda h: h.reciprocal(out=rr[:, hd, :], in_=rr[:, hd, :]), r=[("rr", hd)], w=[("rr", hd)])
                sc = (128.0 ** -0.5) if j == 0 else 1.0
                S.dve(lambda h, j=j, sc=sc: h.scalar_tensor_tensor(out=qkn[:, hd, j, :], in0=cs[:, hd, j, :], scalar=sc, in1=rr[:, hd, :],
                                                                   op0=ALU.mult, op1=ALU.mult),
                      r=[("cs", hd, j), ("rr", hd)], w=[("qkn", hd, j)])
                S.act(lambda h, j=j: h.copy(out=qkb[:, hd, j, :], in_=qkn[:, hd, j, :]), r=[("qkn", hd, j)], w=[("qkb", hd, j)])
                yield
            if cut == 1:
                return
            q4 = qkn[:, hd, 0, :].rearrange("p (t c) -> p t c", c=128)
            S.dve(lambda h: h.tensor_tensor(out=Rm[:, hd, :, :], in0=mask4(TRIU), in1=bcol(gg), op=ALU.mult),
                  r=["cm", ("gg", hd)], w=[("Rm", hd)])
            S.pe(lambda h: h.matmul(ps[:, XB(hd), :], lhsT=ones_f[:, :], rhs=Rm[:, hd, :, :].rearrange("p t c -> p (t c)"),
                                    start=True, stop=True), r=[("Rm", hd), "ones_f"], w=KX(hd))
            yield
            S.dve(lambda h: h.tensor_tensor(out=tdm[:, hd, :, :], in0=psX(hd), in1=bcol(gcs), op=ALU.subtract),
                  r=KX(hd) + ["gcs"], w=[("tdm", hd)])
            S.act(lambda h: h.activation(out=egcB[:, hd, :, :], in_=psX(hd), func=AF.Exp), r=KX(hd), w=[("egcB", hd)])
            S.act(lambda h: h.copy(out=glr[:, hd, :].unsqueeze(2), in_=psX(hd)[:, :, 127:128]), r=KX(hd), w=[("glr", hd)])
            S.dve(lambda h: h.tensor_tensor(out=tdm[:, hd, :, :], in0=tdm[:, hd, :, :], in1=mask4(NMI), op=ALU.max),
                  r=[("tdm", hd), "cm"], w=[("tdm", hd)])
            S.act(lambda h: h.activation(out=tdm[:, hd, :, :], in_=tdm[:, hd, :, :], func=AF.Exp, scale=-1.0),
                  r=[("tdm", hd)], w=[("tdm", hd)])
            S.act(lambda h: h.activation(out=gl[:, hd, :], in_=glr[:, hd, :], func=AF.Exp), r=[("glr", hd)], w=[("gl", hd)])
            S.dve(lambda h: h.tensor_tensor(out=ekd[:, hd, :], in0=glr[:, hd, :], in1=gcs[:, :, hd], op=ALU.subtract),
                  r=[("glr", hd), "gcs"], w=[("ekd", hd)])
            S.act(lambda h: h.activation(out=ekd[:, hd, :], in_=ekd[:, hd, :], func=AF.Exp), r=[("ekd", hd)], w=[("ekd", hd)])
            S.act(lambda h: h.activation(out=egc[:, hd, :], in_=gcs[:, :, hd], func=AF.Exp), r=["gcs"], w=[("egc", hd)])
            S.dve(lambda h: h.tensor_tensor(out=bg[:, hd, :], in0=egc[:, hd, :], in1=beta[:, :, hd], op=ALU.mult),
                  r=[("egc", hd), "beta"], w=[("bg", hd)])
            S.dve(lambda h: h.tensor_tensor(out=qdT[:, hd, :, :], in0=q4, in1=egcB[:, hd, :, :], op=ALU.mult),
                  r=[("qkn", hd, 0), ("egcB", hd)], w=[("qdT", hd)])
            S.dve(lambda h: h.tensor_tensor(out=W4[:, hd, :, :], in0=tdm[:, hd, :, :], in1=mask4(STR), op=ALU.mult),
                  r=[("tdm", hd), "cm"], w=[("W4", hd)])
            S.dve(lambda h: h.scalar_tensor_tensor(out=W4[:, hd, :, :], in0=W4[:, hd, :, :], scalar=-1.0, in1=bcol(beta),
                                                   op0=ALU.mult, op1=ALU.mult), r=[("W4", hd), "beta"], w=[("W4", hd)])
            yield
            for ci in range(4):
                S.pe(lambda h, ci=ci: h.matmul(psX(hd)[:, ci, :], lhsT=qkb[:, hd, 1, tsl(ci)], rhs=qkb[:, hd, 1, tsl(ci)],
                                               start=True, stop=True), r=[("qkb", hd, 1)], w=KX(hd))
            S.dve(lambda h: h.tensor_tensor(out=X[:, hd, 0, :, 0, :], in0=psX(hd), in1=W4[:, hd, :, :], op=ALU.mult),
                  r=KX(hd) + [("W4", hd)], w=[("XP", hd, 0)])
            yield
            for ci in range(4):
                S.pe(lambda h, ci=ci: h.matmul(psX(hd)[:, ci, :], lhsT=qkb[:, hd, 0, tsl(ci)], rhs=qkb[:, hd, 1, tsl(ci)],
                                               start=True, stop=True), r=[("qkb", hd, 0), ("qkb", hd, 1)], w=KX(hd))
            S.dve(lambda h: h.tensor_tensor(out=Am[:, hd, :, :], in0=psX(hd), in1=tdm[:, hd, :, :], op=ALU.mult),
                  r=KX(hd) + [("tdm", hd)], w=[("Am", hd)])
            yield
            for ci in range(4):
                S.pe(lambda h, ci=ci: h.matmul(psX(hd)[:, ci, :], lhsT=X[:, hd, 0, ci, 0, :], rhs=cm[:, IDN, :], start=True, stop=True),
                     r=[("XP", hd, 0), "cm"], w=KX(hd))
            S.act(lambda h: h.copy(out=X[:, hd, 0, :, 1, :], in_=psX(hd)), r=KX(hd), w=[("XQ", hd, 0)])
            S.dve(lambda h: h.tensor_tensor(out=X[:, hd, 1, :, 2, :], in0=psX(hd), in1=mask4(IDN), op=ALU.add),
                  r=KX(hd) + ["cm"], w=[("XT", hd, 1)])
            yield
            for ci in range(4):
                S.pe(lambda h, ci=ci: h.matmul(psX(hd)[:, ci, :], lhsT=Am[:, hd, ci, :], rhs=cm[:, IDN, :], start=True, stop=True),
                     r=[("Am", hd), "cm"], w=KX(hd))
            S.act(lambda h: h.copy(out=ATb[:, hd, :, :], in_=psX(hd)), r=KX(hd), w=[("ATb", hd)])
            yield
            for ci in range(4):
                S.pe(lambda h, ci=ci: h.matmul(psX(hd)[:, ci, :], lhsT=qkn[:, hd, 1, tsl(ci)], rhs=cm[:, IDN, :], start=True, stop=True),
                     r=[("qkn", hd, 1), "cm"], w=KX(hd))
            S.dve(lambda h: h.tensor_tensor(out=kd[:, hd, :, :], in0=psX(hd), in1=bvec(ekd), op=ALU.mult),
                  r=KX(hd) + [("ekd", hd)], w=[("kd", hd)])
            S.dve(lambda h: h.tensor_tensor(out=kbg[:, hd, :, :], in0=psX(hd), in1=bvec(bg), op=ALU.mult),
                  r=KX(hd) + [("bg", hd)], w=[("kbg", hd)])
            yield
            for ci in range(4):
                S.pe(lambda h, ci=ci: h.matmul(psX(hd)[:, ci, :], lhsT=cs[:, hd, 2, tsl(ci)], rhs=cm[:, IDN, :], start=True, stop=True),
                     r=[("cs", hd, 2), "cm"], w=KX(hd))
            S.dve(lambda h: h.tensor_tensor(out=vb[:, hd, :, :], in0=psX(hd), in1=bcol(beta), op=ALU.mult),
                  r=KX(hd) + ["beta"], w=[("vb", hd)])
            yield
            for m in range(7):
                cur, nxt = m % 2, (m + 1) % 2
                for ci in range(4):
                    if m == 0:
                        S.pe(lambda h, ci=ci: h.matmul(psY(hd)[:, ci, 0:128], lhsT=X[:, hd, 0, ci, 0, :], rhs=X[:, hd, 0, ci, 1, :],
                                                       start=True, stop=True), r=[("XP", hd, 0), ("XQ", hd, 0)], w=KY(hd))
                    elif m < 6:
                        S.pe(lambda h, ci=ci, cur=cur: h.matmul(psY(hd)[:, ci, :], lhsT=X[:, hd, cur, ci, 0, :],
                                                                rhs=X[:, hd, cur, ci, 1:3, :].rearrange("p a b -> p (a b)"),
                                                                start=True, stop=True),
                             r=[("XP", hd, cur), ("XQ", hd, cur), ("XT", hd, cur)], w=KY(hd))
                    else:
                        S.pe(lambda h, ci=ci, cur=cur: h.matmul(psY(hd)[:, ci, 128:256], lhsT=X[:, hd, cur, ci, 0, :],
                                                                rhs=X[:, hd, cur, ci, 2, :], start=True, stop=True),
                             r=[("XP", hd, cur), ("XT", hd, cur)], w=KY(hd))
                if m < 6:
                    for ci in range(4):
                        S.pe(lambda h, ci=ci, cur=cur: h.matmul(psX(hd)[:, ci, :], lhsT=X[:, hd, cur, ci, 1, :], rhs=X[:, hd, cur, ci, 0, :],
                                                                start=True, stop=True), r=[("XP", hd, cur), ("XQ", hd, cur)], w=KX(hd))
                    S.act(lambda h, nxt=nxt: h.copy(out=X[:, hd, nxt, :, 1, :], in_=psY(hd)[:, :, 0:128]), r=KY(hd), w=[("XQ", hd, nxt)])
                    S.act(lambda h, nxt=nxt: h.copy(out=X[:, hd, nxt, :, 0, :], in_=psX(hd)), r=KX(hd), w=[("XP", hd, nxt)])
                if 1 <= m < 6:
                    S.dve(lambda h, cur=cur, nxt=nxt: h.tensor_tensor(out=X[:, hd, nxt, :, 2, :], in0=psY(hd)[:, :, 128:256],
                                                                      in1=X[:, hd, cur, :, 2, :], op=ALU.add),
                          r=KY(hd) + [("XT", hd, cur)], w=[("XT", hd, nxt)])
                if m == 6:
                    S.dve(lambda h, cur=cur: h.tensor_tensor(out=Ttb[:, hd, :, :], in0=psY(hd)[:, :, 128:256], in1=X[:, hd, cur, :, 2, :],
                                                             op=ALU.add), r=KY(hd) + [("XT", hd, cur)], w=[("Ttb", hd)])
                yield
            for ci in range(4):
                S.pe(lambda h, ci=ci: h.matmul(psX(hd)[:, ci, :], lhsT=Ttb[:, hd, ci, :], rhs=vb[:, hd, ci, :], start=True, stop=True),
                     r=[("Ttb", hd), ("vb", hd)], w=KX(hd))
            S.act(lambda h: h.copy(out=uu[:, hd, :, :], in_=psX(hd)), r=KX(hd), w=[("uu", hd)])
            for ci in range(4):
                S.pe(lambda h, ci=ci: h.matmul(psY(hd)[:, ci, 0:128], lhsT=kbg[:, hd, ci, :], rhs=Ttb[:, hd, ci, :], start=True, stop=True),
                     r=[("Ttb", hd), ("kbg", hd)], w=KY(hd))
            S.act(lambda h: h.copy(out=wTb[:, hd, :, :], in_=psY(hd)[:, :, 0:128]), r=KY(hd), w=[("wTb", hd)])
            yield
            if cut == 2:
                return
            sbk = YB(hd) + 1
            KS = [("ps", sbk)]
            for ci in range(4):
                S.pe(lambda h, ci=ci: h.matmul(ps[:, sbk, 0:128], lhsT=wTb[:, hd, ci, :], rhs=Sb[:, hd, :], start=True, stop=True),
                     r=[("wTb", hd), ("Sb", hd)], w=KS)
                S.dve(lambda h, ci=ci: h.tensor_tensor(out=vnb[:, hd, :], in0=uu[:, hd, ci, :], in1=ps[:, sbk, 0:128], op=ALU.subtract),
                      r=[("uu", hd)] + KS, w=[("vnb", hd)])
                S.pe(lambda h, ci=ci: h.matmul(ps[:, sbk, 256:384], lhsT=Sb[:, hd, :], rhs=qdT[:, hd, ci, :], start=True, stop=False),
                     r=[("Sb", hd), ("qdT", hd)], w=KS)
                S.pe(lambda h, ci=ci: h.matmul(ps[:, sbk, 256:384], lhsT=vnb[:, hd, :], rhs=ATb[:, hd, ci, :], start=False, stop=True),
                     r=[("vnb", hd), ("ATb", hd)], w=KS)
                S.pe(lambda h, ci=ci: h.matmul(ps[:, sbk, 128:256], lhsT=kd[:, hd, ci, :], rhs=vnb[:, hd, :], start=True, stop=True),
                     r=[("kd", hd), ("vnb", hd)], w=KS)
                S.act(lambda h, ci=ci: h.copy(out=oT[:, hd, tsl(ci)], in_=ps[:, sbk, 256:384]), r=KS, w=[("oT", hd)])
                S.dve(lambda h, ci=ci: h.scalar_tensor_tensor(out=Sst[:, hd, :], in0=Sst[:, hd, :], scalar=gl[:, hd, ci:ci + 1],
                                                              in1=ps[:, sbk, 128:256], op0=ALU.mult, op1=ALU.add),
                      r=[("S", hd), ("gl", hd)] + KS, w=[("S", hd)])
                S.act(lambda h: h.copy(out=Sb[:, hd, :], in_=Sst[:, hd, :]), r=[("S", hd)], w=[("Sb", hd)])
                yield
            S.act(lambda h: h.activation(out=sqb[:, hd, :], in_=oT[:, hd, :], func=AF.Square), r=[("oT", hd)], w=[("sqb", hd)])
            S.pe(lambda h: h.matmul(ps[:, PROJ, :], lhsT=ones_bf[:, :], rhs=sqb[:, hd, :], start=True, stop=True),
                 r=[("sqb", hd), "ones_bf"], w=[("ps", PROJ)])
            S.act(lambda h: h.activation(out=rr[:, hd, :], in_=ps[:, PROJ, :], func=AF.Sqrt, scale=1.0 / 128, bias=eps_col[:, 0:1]),
                  r=[("ps", PROJ), "eps_col"], w=[("rr", hd)])
            S.dve(lambda h: h.reciprocal(out=rr[:, hd, :], in_=rr[:, hd, :]), r=[("rr", hd)], w=[("rr", hd)])
            S.dve(lambda h: h.scalar_tensor_tensor(out=oT[:, hd, :], in0=oT[:, hd, :], scalar=sm[:, 28:29], in1=rr[:, hd, :],
                                                   op0=ALU.mult, op1=ALU.mult), r=[("oT", hd), "sm", ("rr", hd)], w=[("oT", hd)])
            S.act(lambda h: h.activation(out=sz[:, hd, :], in_=zT[:, hd, :], func=AF.Silu), r=[("zT", hd)], w=[("sz", hd)])
            S.dve(lambda h: h.tensor_tensor(out=og[:, hd, :], in0=oT[:, hd, :], in1=sz[:, hd, :], op=ALU.mult),
                  r=[("oT", hd), ("sz", hd)], w=[("og", hd)])
            S.dma(lambda h: h.dma_start(out=og_dst(hd, tg), in_=og[:, hd, :]), r=[("og", hd)])
            yield

        for tg in range(NG):
            b = tg % 2
            S.dma(lambda h, b=b, tg=tg: h.dma_start(out=hTg[:, b, :, :], in_=h_src(tg)), r=list(h_keys), w=[("hTg", b)])
            for t in range(4):
                for c in range(8):
                    S.pe(lambda h, b=b, t=t, c=c: h.matmul(ps[:, MISC, t * 4:(t + 1) * 4], lhsT=hTg[:, b, c, t * 128:(t + 1) * 128],
                                                            rhs=wq[:, c, 1024:1028], start=(c == 0), stop=(c == 7)),
                         r=[("hTg", b)] + WQ, w=[("ps", MISC)])
            S.act(lambda h: h.copy(out=bat[:, :, :], in_=ps[:, MISC, 0:16].rearrange("p (t f) -> p t f", f=4)),
                  r=[("ps", MISC)], w=["bat"])
            S.act(lambda h: h.activation(out=beta[:, :, :], in_=bat[:, :, 0:2], func=AF.Sigmoid), r=["bat"], w=["beta"])
            for hd in range(2):
                S.act(lambda h, hd=hd: h.activation(out=esp[:, :, hd:hd + 1], in_=bat[:, :, 2 + hd:3 + hd], func=AF.Exp,
                                                    bias=sm[:, 26 + hd:27 + hd]), r=["bat", "sm"], w=[("esp", hd)])
                S.act(lambda h, hd=hd: h.activation(out=esp[:, :, hd:hd + 1], in_=esp[:, :, hd:hd + 1], func=AF.Ln,
                                                    bias=one_col[:, 0:1]), r=[("esp", hd), "one_col"], w=[("esp", hd)])
                S.dve(lambda h, hd=hd: h.tensor_scalar(out=gg[:, :, hd:hd + 1], in0=esp[:, :, hd:hd + 1],
                                                       scalar1=negA[:, hd:hd + 1], scalar2=None, op0=ALU.mult),
                      r=[("esp", hd), "negA"], w=[("gg", hd)])
            for t in range(4):
                S.pe(lambda h, t=t: h.matmul(ps[:, MISC, 32 + 2 * t:34 + 2 * t], lhsT=cm[:, TRIU, :], rhs=gg[:, t, :],
                                             start=True, stop=True),
                     r=["cm", ("gg", 0), ("gg", 1)], w=[("ps", MISC)])
            S.act(lambda h: h.copy(out=gcs[:, :, :], in_=ps[:, MISC, 32:40].rearrange("p (t f) -> p t f", f=2)),
                  r=[("ps", MISC)], w=["gcs"])
            for _ in zip_longest(unit(tg, 0, b), unit(tg, 1, b)):
                pass
        emit_phase(nc, S, ss)


def gdn_phase3(nc, ss, ps, h_src, og_dst, w_my, sm_d, cm_d, NG, cut=0, pre_fn=None, h_keys=()):
    import contextlib
    from itertools import zip_longest
    with contextlib.ExitStack() as st:
        sb = lambda name, shape, dt: st.enter_context(sbt(nc, name, shape, dt))
        ones_bf = sb("ones_bf", [128, 128], BF16)
        ones_f = sb("ones_f", [128, 128], F32)
        eps_col = sb("eps_col", [128, 1], F32)
        one_col = sb("one_col", [128, 1], F32)
        sm = sb("sm_sb", [128, 32], F32)
        cm = sb("cm_sb", [128, 4, 128], F32)
        negA = sb("negA", [128, 2], F32)
        wq = sb("wq", [128, 8, 1028], BF16)
        hTg = sb("hTg", [128, 2, 8, 512], BF16)
        pre = sb("pre", [128, 2, 3, 515], F32)
        cs = sb("cs", [128, 2, 3, 512], F32)
        sqb = sb("sqb", [128, 2, 512], BF16)
        sqo = sb("sqo", [128, 2, 512], BF16)
        rro = sb("rro", [128, 2, 512], F32)
        rr = sb("rr", [128, 2, 512], F32)
        qkn = sb("qkn", [128, 2, 2, 512], F32)
        qkb = sb("qkb", [128, 2, 2, 512], BF16)
        zT = sb("zT", [128, 2, 2, 512], BF16)
        bat = sb("bat", [128, 4, 4], F32)
        beta = sb("beta", [128, 4, 2], F32)
        esp = sb("esp", [128, 4, 2], F32)
        gg = sb("gg", [128, 4, 2], F32)
        gcs = sb("gcs", [128, 4, 2], F32)
        Rm = sb("Rm", [128, 2, 4, 128], F32)
        tdm = sb("tdm", [128, 2, 4, 128], F32)
        egcB = sb("egcB", [128, 2, 4, 128], F32)
        W4 = sb("W4", [128, 2, 4, 128], F32)
        Am = sb("Am", [128, 2, 4, 128], F32)
        X = sb("X", [128, 2, 2, 4, 3, 128], F32)
        glr = sb("glr", [128, 2, 4], F32)
        gl = sb("gl", [128, 2, 2, 4], F32)
        ekd = sb("ekd", [128, 2, 4], F32)
        egc = sb("egc", [128, 2, 4], F32)
        bg = sb("bg", [128, 2, 4], F32)
        Ttb = sb("Ttb", [128, 2, 4, 128], BF16)
        ATb = sb("ATb", [128, 2, 2, 4, 128], BF16)
        qdT = sb("qdT", [128, 2, 2, 4, 128], BF16)
        kd = sb("kd", [128, 2, 2, 4, 128], BF16)
        kbg = sb("kbg", [128, 2, 4, 128], BF16)
        vb = sb("vb", [128, 2, 4, 128], BF16)
        uu = sb("uu", [128, 2, 2, 4, 128], F32)
        wTb = sb("wTb", [128, 2, 2, 4, 128], BF16)
        vnb = sb("vnb", [128, 2, 128], BF16)
        Sst = sb("Sst", [128, 2, 128], F32)
        Sb = sb("Sb", [128, 2, 128], BF16)
        oT = sb("oT", [128, 2, 512], F32)
        sz = sb("sz", [128, 2, 512], F32)
        og = sb("og", [128, 2, 512], BF16)

        IDN, TRIU, NMI, STR = 0, 1, 2, 3
        B4 = [128, 4, 128]
        mask4 = lambda mi: cm[:, mi, :].unsqueeze(1).to_broadcast(B4)
        PROJ, MISC = 0, 1
        XB = lambda hd: 2 + hd
        YB = lambda hd: 4 + 2 * hd
        psX = lambda hd: ps[:, XB(hd), :].rearrange("p (t c) -> p t c", c=128)
        psY = lambda hd: ps[:, YB(hd):YB(hd) + 2, :].rearrange("p b (t c) -> p (b t) c", c=256)
        KX = lambda hd: [("ps", XB(hd))]
        KY = lambda hd: [("ps", YB(hd)), ("ps", YB(hd) + 1)]
        S = Sched()
        if pre_fn is not None:
            pre_fn(S)
        S.pool(lambda h: h.memset(ones_bf[:, :], 1.0), w=["ones_bf"])
        S.pool(lambda h: h.memset(ones_f[:, :], 1.0), w=["ones_f"])
        S.pool(lambda h: h.memset(eps_col[:, :], EPS), w=["eps_col"])
        S.pool(lambda h: h.memset(one_col[:, :], 1.0), w=["one_col"])
        S.pool(lambda h: h.memset(pre[:, :, :, :], 0.0), w=[("pre", a, b) for a in range(2) for b in range(3)])
        S.pool(lambda h: h.memset(Sst[:, :, :], 0.0), w=[("S", 0), ("S", 1)])
        S.pool(lambda h: h.memset(Sb[:, :, :], 0.0), w=[("Sb", 0), ("Sb", 1)])
        S.dma(lambda h: h.dma_start(out=sm[:, :], in_=sm_d[:, :]), w=["sm"])
        S.dma(lambda h: h.dma_start(out=cm[:, :, :], in_=cm_d[:, :, :]), w=["cm"])
        wv = w_my.rearrange("(c p) n -> p c n", p=128)
        for c in range(8):
            S.dma(lambda h, c=c: h.dma_start(out=wq[:, c, :], in_=wv[:, c, :]), w=[("wq", c)], q="pool")
        S.act(lambda h: h.activation(out=negA[:, :], in_=sm[:, 24:26], func=AF.Exp), r=["sm"], w=["negA"])
        S.dve(lambda h: h.tensor_scalar(out=negA[:, :], in0=negA[:, :], scalar1=-1.0, scalar2=None, op0=ALU.mult),
              r=["negA"], w=["negA"])
        WQ = [("wq", c) for c in range(8)]
        tsl = lambda ci: slice(ci * 128, (ci + 1) * 128)

        def prep(tg, hd, b, gp):
            pb = hd
            bcol = lambda t4: t4[:, :, hd:hd + 1].to_broadcast(B4)
            bvec = lambda v: v[:, hd, :].unsqueeze(2).to_broadcast(B4)
            for j in range(4):
                for c in range(8):
                    S.pe(lambda h, j=j, c=c: h.matmul(ps[:, PROJ, :], lhsT=wq[:, c, hd * 512 + j * 128:hd * 512 + (j + 1) * 128],
                                                      rhs=hTg[:, b, c, :], start=(c == 0), stop=(c == 7)),
                         r=[("hTg", b)] + WQ, w=[("ps", PROJ)])
                if j < 3:
                    S.act(lambda h, j=j: h.copy(out=pre[:, hd, j, 3:515], in_=ps[:, PROJ, :]), r=[("ps", PROJ)], w=[("pre", hd, j)])
                else:
                    S.act(lambda h: h.copy(out=zT[:, hd, gp, :], in_=ps[:, PROJ, :]), r=[("ps", PROJ)], w=[("zT", hd, gp)])
                yield
            for j in range(3):
                cw = lambda k, j=j: sm[:, (hd * 3 + j) * 4 + k:(hd * 3 + j) * 4 + k + 1]
                S.dve(lambda h, j=j, cw=cw: h.tensor_scalar(out=cs[:, hd, j, :], in0=pre[:, hd, j, 3:515], scalar1=cw(3), scalar2=None,
                                                           op0=ALU.mult), r=[("pre", hd, j), "sm"], w=[("cs", hd, j)])
                for k in (2, 1, 0):
                    S.dve(lambda h, j=j, k=k, cw=cw: h.scalar_tensor_tensor(out=cs[:, hd, j, :], in0=pre[:, hd, j, k:k + 512], scalar=cw(k),
                                                                          in1=cs[:, hd, j, :], op0=ALU.mult, op1=ALU.add),
                          r=[("pre", hd, j), "sm", ("cs", hd, j)], w=[("cs", hd, j)])
                S.dve(lambda h, j=j: h.tensor_copy(out=pre[:, hd, j, 0:3], in_=pre[:, hd, j, 512:515]),
                      r=[("pre", hd, j)], w=[("pre", hd, j)])
                S.act(lambda h, j=j: h.activation(out=cs[:, hd, j, :], in_=cs[:, hd, j, :], func=AF.Silu),
                      r=[("cs", hd, j)], w=[("cs", hd, j)])
                yield
            for j in range(2):
                S.act(lambda h, j=j: h.activation(out=sqb[:, hd, :], in_=cs[:, hd, j, :], func=AF.Square), r=[("cs", hd, j)], w=[("sqb", hd)])
                S.pe(lambda h: h.matmul(ps[:, PROJ, :], lhsT=ones_bf[:, :], rhs=sqb[:, hd, :], start=True, stop=True),
                     r=[("sqb", hd), "ones_bf"], w=[("ps", PROJ)])
                S.act(lambda h: h.activation(out=rr[:, hd, :], in_=ps[:, PROJ, :], func=AF.Sqrt, bias=eps_col[:, 0:1]),
                      r=[("ps", PROJ), "eps_col"], w=[("rr", hd)])
                S.dve(lambda h: h.reciprocal(out=rr[:, hd, :], in_=rr[:, hd, :]), r=[("rr", hd)], w=[("rr", hd)])
                sc = (128.0 ** -0.5) if j == 0 else 1.0
                S.dve(lambda h, j=j, sc=sc: h.scalar_tensor_tensor(out=qkn[:, hd, j, :], in0=cs[:, hd, j, :], scalar=sc, in1=rr[:, hd, :],
                                                                   op0=ALU.mult, op1=ALU.mult),
                      r=[("cs", hd, j), ("rr", hd)], w=[("qkn", hd, j)])
                S.act(lambda h, j=j: h.copy(out=qkb[:, hd, j, :], in_=qkn[:, hd, j, :]), r=[("qkn", hd, j)], w=[("qkb", hd, j)])
                yield
            if cut == 1:
                return
            q4 = qkn[:, hd, 0, :].rearrange("p (t c) -> p t c", c=128)
            S.dve(lambda h: h.tensor_tensor(out=Rm[:, hd, :, :], in0=mask4(TRIU), in1=bcol(gg), op=ALU.mult),
                  r=["cm", ("gg", hd)], w=[("Rm", hd)])
            S.pe(lambda h: h.matmul(ps[:, XB(hd), :], lhsT=ones_f[:, :], rhs=Rm[:, hd, :, :].rearrange("p t c -> p (t c)"),
                                    start=True, stop=True), r=[("Rm", hd), "ones_f"], w=KX(hd))
            yield
            S.dve(lambda h: h.tensor_tensor(out=tdm[:, hd, :, :], in0=psX(hd), in1=bcol(gcs), op=ALU.subtract),
                  r=KX(hd) + ["gcs"], w=[("tdm", hd)])
            S.act(lambda h: h.activation(out=egcB[:, hd, :, :], in_=psX(hd), func=AF.Exp), r=KX(hd), w=[("egcB", hd)])
            S.act(lambda h: h.copy(out=glr[:, hd, :].unsqueeze(2), in_=psX(hd)[:, :, 127:128]), r=KX(hd), w=[("glr", hd)])
            S.dve(lambda h: h.tensor_tensor(out=tdm[:, hd, :, :], in0=tdm[:, hd, :, :], in1=mask4(NMI), op=ALU.max),
                  r=[("tdm", hd), "cm"], w=[("tdm", hd)])
            S.act(lambda h: h.activation(out=tdm[:, hd, :, :], in_=tdm[:, hd, :, :], func=AF.Exp, scale=-1.0),
                  r=[("tdm", hd)], w=[("tdm", hd)])
            S.act(lambda h: h.activation(out=gl[:, hd, gp, :], in_=glr[:, hd, :], func=AF.Exp), r=[("glr", hd)], w=[("gl", hd, gp)])
            S.dve(lambda h: h.tensor_tensor(out=ekd[:, hd, :], in0=glr[:, hd, :], in1=gcs[:, :, hd], op=ALU.subtract),
                  r=[("glr", hd), "gcs"], w=[("ekd", hd)])
            S.act(lambda h: h.activation(out=ekd[:, hd, :], in_=ekd[:, hd, :], func=AF.Exp), r=[("ekd", hd)], w=[("ekd", hd)])
            S.act(lambda h: h.activation(out=egc[:, hd, :], in_=gcs[:, :, hd], func=AF.Exp), r=["gcs"], w=[("egc", hd)])
            S.dve(lambda h: h.tensor_tensor(out=bg[:, hd, :], in0=egc[:, hd, :], in1=beta[:, :, hd], op=ALU.mult),
                  r=[("egc", hd), "beta"], w=[("bg", hd)])
            S.dve(lambda h: h.tensor_tensor(out=qdT[:, hd, gp, :, :], in0=q4, in1=egcB[:, hd, :, :], op=ALU.mult),
                  r=[("qkn", hd, 0), ("egcB", hd)], w=[("qdT", hd, gp)])
            S.dve(lambda h: h.tensor_tensor(out=W4[:, hd, :, :], in0=tdm[:, hd, :, :], in1=mask4(STR), op=ALU.mult),
                  r=[("tdm", hd), "cm"], w=[("W4", hd)])
            S.dve(lambda h: h.scalar_tensor_tensor(out=W4[:, hd, :, :], in0=W4[:, hd, :, :], scalar=-1.0, in1=bcol(beta),
                                                   op0=ALU.mult, op1=ALU.mult), r=[("W4", hd), "beta"], w=[("W4", hd)])
            yield
            for ci in range(4):
                S.pe(lambda h, ci=ci: h.matmul(psX(hd)[:, ci, :], lhsT=qkb[:, hd, 1, tsl(ci)], rhs=qkb[:, hd, 1, tsl(ci)],
                                               start=True, stop=True), r=[("qkb", hd, 1)], w=KX(hd))
            S.dve(lambda h: h.tensor_tensor(out=X[:, hd, 0, :, 0, :], in0=psX(hd), in1=W4[:, hd, :, :], op=ALU.mult),
                  r=KX(hd) + [("W4", hd)], w=[("XP", hd, 0)])
            yield
            for ci in range(4):
                S.pe(lambda h, ci=ci: h.matmul(psX(hd)[:, ci, :], lhsT=qkb[:, hd, 0, tsl(ci)], rhs=qkb[:, hd, 1, tsl(ci)],
                                               start=True, stop=True), r=[("qkb", hd, 0), ("qkb", hd, 1)], w=KX(hd))
            S.dve(lambda h: h.tensor_tensor(out=Am[:, hd, :, :], in0=psX(hd), in1=tdm[:, hd, :, :], op=ALU.mult),
                  r=KX(hd) + [("tdm", hd)], w=[("Am", hd)])
            yield
            for ci in range(4):
                S.pe(lambda h, ci=ci: h.matmul(psX(hd)[:, ci, :], lhsT=X[:, hd, 0, ci, 0, :], rhs=cm[:, IDN, :], start=True, stop=True),
                     r=[("XP", hd, 0), "cm"], w=KX(hd))
            S.act(lambda h: h.copy(out=X[:, hd, 0, :, 1, :], in_=psX(hd)), r=KX(hd), w=[("XQ", hd, 0)])
            S.dve(lambda h: h.tensor_tensor(out=X[:, hd, 1, :, 2, :], in0=psX(hd), in1=mask4(IDN), op=ALU.add),
                  r=KX(hd) + ["cm"], w=[("XT", hd, 1)])
            yield
            for ci in range(4):
                S.pe(lambda h, ci=ci: h.matmul(psX(hd)[:, ci, :], lhsT=Am[:, hd, ci, :], rhs=cm[:, IDN, :], start=True, stop=True),
                     r=[("Am", hd), "cm"], w=KX(hd))
            S.act(lambda h: h.copy(out=ATb[:, hd, gp, :, :], in_=psX(hd)), r=KX(hd), w=[("ATb", hd, gp)])
            yield
            for ci in range(4):
                S.pe(lambda h, ci=ci: h.matmul(psX(hd)[:, ci, :], lhsT=qkn[:, hd, 1, tsl(ci)], rhs=cm[:, IDN, :], start=True, stop=True),
                     r=[("qkn", hd, 1), "cm"], w=KX(hd))
            S.dve(lambda h: h.tensor_tensor(out=kd[:, hd, gp, :, :], in0=psX(hd), in1=bvec(ekd), op=ALU.mult),
                  r=KX(hd) + [("ekd", hd)], w=[("kd", hd, gp)])
            S.dve(lambda h: h.tensor_tensor(out=kbg[:, hd, :, :], in0=psX(hd), in1=bvec(bg), op=ALU.mult),
                  r=KX(hd) + [("bg", hd)], w=[("kbg", hd)])
            yield
            for ci in range(4):
                S.pe(lambda h, ci=ci: h.matmul(psX(hd)[:, ci, :], lhsT=cs[:, hd, 2, tsl(ci)], rhs=cm[:, IDN, :], start=True, stop=True),
                     r=[("cs", hd, 2), "cm"], w=KX(hd))
            S.dve(lambda h: h.tensor_tensor(out=vb[:, hd, :, :], in0=psX(hd), in1=bcol(beta), op=ALU.mult),
                  r=KX(hd) + ["beta"], w=[("vb", hd)])
            yield
            NL = 5
            for m in range(NL):
                cur, nxt = m % 2, (m + 1) % 2
                for ci in range(4):
                    if m == 0:
                        S.pe(lambda h, ci=ci: h.matmul(psY(hd)[:, ci, 0:128], lhsT=X[:, hd, 0, ci, 0, :], rhs=X[:, hd, 0, ci, 1, :],
                                                       start=True, stop=True), r=[("XP", hd, 0), ("XQ", hd, 0)], w=KY(hd))
                    elif m < NL - 1:
                        S.pe(lambda h, ci=ci, cur=cur: h.matmul(psY(hd)[:, ci, :], lhsT=X[:, hd, cur, ci, 0, :],
                                                                rhs=X[:, hd, cur, ci, 1:3, :].rearrange("p a b -> p (a b)"),
                                                                start=True, stop=True),
                             r=[("XP", hd, cur), ("XQ", hd, cur), ("XT", hd, cur)], w=KY(hd))
                    else:
                        S.pe(lambda h, ci=ci, cur=cur: h.matmul(psY(hd)[:, ci, 128:256], lhsT=X[:, hd, cur, ci, 0, :],
                                                                rhs=X[:, hd, cur, ci, 2, :], start=True, stop=True),
                             r=[("XP", hd, cur), ("XT", hd, cur)], w=KY(hd))
                if m < NL - 1:
                    for ci in range(4):
                        S.pe(lambda h, ci=ci, cur=cur: h.matmul(psX(hd)[:, ci, :], lhsT=X[:, hd, cur, ci, 1, :], rhs=X[:, hd, cur, ci, 0, :],
                                                                start=True, stop=True), r=[("XP", hd, cur), ("XQ", hd, cur)], w=KX(hd))
                    S.act(lambda h, nxt=nxt: h.copy(out=X[:, hd, nxt, :, 1, :], in_=psY(hd)[:, :, 0:128]), r=KY(hd), w=[("XQ", hd, nxt)])
                    S.act(lambda h, nxt=nxt: h.copy(out=X[:, hd, nxt, :, 0, :], in_=psX(hd)), r=KX(hd), w=[("XP", hd, nxt)])
                if 1 <= m < NL - 1:
                    S.dve(lambda h, cur=cur, nxt=nxt: h.tensor_tensor(out=X[:, hd, nxt, :, 2, :], in0=psY(hd)[:, :, 128:256],
                                                                      in1=X[:, hd, cur, :, 2, :], op=ALU.add),
                          r=KY(hd) + [("XT", hd, cur)], w=[("XT", hd, nxt)])
                if m == NL - 1:
                    S.dve(lambda h, cur=cur: h.tensor_tensor(out=Ttb[:, hd, :, :], in0=psY(hd)[:, :, 128:256], in1=X[:, hd, cur, :, 2, :],
                                                             op=ALU.add), r=KY(hd) + [("XT", hd, cur)], w=[("Ttb", hd)])
                yield
            for ci in range(4):
                S.pe(lambda h, ci=ci: h.matmul(psX(hd)[:, ci, :], lhsT=Ttb[:, hd, ci, :], rhs=vb[:, hd, ci, :], start=True, stop=True),
                     r=[("Ttb", hd), ("vb", hd)], w=KX(hd))
            S.act(lambda h: h.copy(out=uu[:, hd, gp, :, :], in_=psX(hd)), r=KX(hd), w=[("uu", hd, gp)])
            for ci in range(4):
                S.pe(lambda h, ci=ci: h.matmul(psY(hd)[:, ci, 0:128], lhsT=kbg[:, hd, ci, :], rhs=Ttb[:, hd, ci, :], start=True, stop=True),
                     r=[("Ttb", hd), ("kbg", hd)], w=KY(hd))
            S.act(lambda h: h.copy(out=wTb[:, hd, gp, :, :], in_=psY(hd)[:, :, 0:128]), r=KY(hd), w=[("wTb", hd, gp)])
            yield

        def scan_out(tg, hd, gp):
            sbk = MISC
            KS = [("ps", sbk)]
            for ci in range(4):
                S.pe(lambda h, ci=ci: h.matmul(ps[:, sbk, 128:256], lhsT=wTb[:, hd, gp, ci, :], rhs=Sb[:, hd, :], start=True, stop=True),
                     r=[("wTb", hd, gp), ("Sb", hd)], w=KS)
                S.dve(lambda h, ci=ci: h.tensor_tensor(out=vnb[:, hd, :], in0=uu[:, hd, gp, ci, :], in1=ps[:, sbk, 128:256], op=ALU.subtract),
                      r=[("uu", hd, gp)] + KS, w=[("vnb", hd)])
                S.pe(lambda h, ci=ci: h.matmul(ps[:, sbk, 384:512], lhsT=Sb[:, hd, :], rhs=qdT[:, hd, gp, ci, :], start=True, stop=False),
                     r=[("Sb", hd), ("qdT", hd, gp)], w=KS)
                S.pe(lambda h, ci=ci: h.matmul(ps[:, sbk, 384:512], lhsT=vnb[:, hd, :], rhs=ATb[:, hd, gp, ci, :], start=False, stop=True),
                     r=[("vnb", hd), ("ATb", hd, gp)], w=KS)
                S.pe(lambda h, ci=ci: h.matmul(ps[:, sbk, 256:384], lhsT=kd[:, hd, gp, ci, :], rhs=vnb[:, hd, :], start=True, stop=True),
                     r=[("kd", hd, gp), ("vnb", hd)], w=KS)
                S.act(lambda h, ci=ci: h.copy(out=oT[:, hd, tsl(ci)], in_=ps[:, sbk, 384:512]), r=KS, w=[("oT", hd)])
                S.dve(lambda h, ci=ci: h.scalar_tensor_tensor(out=Sst[:, hd, :], in0=Sst[:, hd, :], scalar=gl[:, hd, gp, ci:ci + 1],
                                                              in1=ps[:, sbk, 256:384], op0=ALU.mult, op1=ALU.add),
                      r=[("S", hd), ("gl", hd, gp)] + KS, w=[("S", hd)])
                S.act(lambda h: h.copy(out=Sb[:, hd, :], in_=Sst[:, hd, :]), r=[("S", hd)], w=[("Sb", hd)])
                yield
            S.act(lambda h: h.activation(out=sqo[:, hd, :], in_=oT[:, hd, :], func=AF.Square), r=[("oT", hd)], w=[("sqo", hd)])
            S.pe(lambda h: h.matmul(ps[:, PROJ, :], lhsT=ones_bf[:, :], rhs=sqo[:, hd, :], start=True, stop=True),
                 r=[("sqo", hd), "ones_bf"], w=[("ps", PROJ)])
            S.act(lambda h: h.activation(out=rro[:, hd, :], in_=ps[:, PROJ, :], func=AF.Sqrt, scale=1.0 / 128, bias=eps_col[:, 0:1]),
                  r=[("ps", PROJ), "eps_col"], w=[("rro", hd)])
            S.dve(lambda h: h.reciprocal(out=rro[:, hd, :], in_=rro[:, hd, :]), r=[("rro", hd)], w=[("rro", hd)])
            S.dve(lambda h: h.scalar_tensor_tensor(out=oT[:, hd, :], in0=oT[:, hd, :], scalar=sm[:, 28:29], in1=rro[:, hd, :],
                                                   op0=ALU.mult, op1=ALU.mult), r=[("oT", hd), "sm", ("rro", hd)], w=[("oT", hd)])
            S.act(lambda h: h.activation(out=sz[:, hd, :], in_=zT[:, hd, gp, :], func=AF.Silu), r=[("zT", hd, gp)], w=[("sz", hd)])
            S.dve(lambda h: h.tensor_tensor(out=og[:, hd, :], in0=oT[:, hd, :], in1=sz[:, hd, :], op=ALU.mult),
                  r=[("oT", hd), ("sz", hd)], w=[("og", hd)])
            S.dma(lambda h: h.dma_start(out=og_dst(hd, tg), in_=og[:, hd, :]), r=[("og", hd)])
            yield

        def preamble(tg):
            b = tg % 2
            S.dma(lambda h, b=b, tg=tg: h.dma_start(out=hTg[:, b, :, :], in_=h_src(tg)), r=list(h_keys), w=[("hTg", b)])
            for t in range(4):
                for c in range(8):
                    S.pe(lambda h, b=b, t=t, c=c: h.matmul(ps[:, MISC, t * 4:(t + 1) * 4], lhsT=hTg[:, b, c, t * 128:(t + 1) * 128],
                                                            rhs=wq[:, c, 1024:1028], start=(c == 0), stop=(c == 7)),
                         r=[("hTg", b)] + WQ, w=[("ps", MISC)])
            S.act(lambda h: h.copy(out=bat[:, :, :], in_=ps[:, MISC, 0:16].rearrange("p (t f) -> p t f", f=4)),
                  r=[("ps", MISC)], w=["bat"])
            S.act(lambda h: h.activation(out=beta[:, :, :], in_=bat[:, :, 0:2], func=AF.Sigmoid), r=["bat"], w=["beta"])
            for hd in range(2):
                S.act(lambda h, hd=hd: h.activation(out=esp[:, :, hd:hd + 1], in_=bat[:, :, 2 + hd:3 + hd], func=AF.Exp,
                                                    bias=sm[:, 26 + hd:27 + hd]), r=["bat", "sm"], w=[("esp", hd)])
                S.act(lambda h, hd=hd: h.activation(out=esp[:, :, hd:hd + 1], in_=esp[:, :, hd:hd + 1], func=AF.Ln,
                                                    bias=one_col[:, 0:1]), r=[("esp", hd), "one_col"], w=[("esp", hd)])
                S.dve(lambda h, hd=hd: h.tensor_scalar(out=gg[:, :, hd:hd + 1], in0=esp[:, :, hd:hd + 1],
                                                       scalar1=negA[:, hd:hd + 1], scalar2=None, op0=ALU.mult),
                      r=[("esp", hd), "negA"], w=[("gg", hd)])
            for t in range(4):
                S.pe(lambda h, t=t: h.matmul(ps[:, MISC, 32 + 2 * t:34 + 2 * t], lhsT=cm[:, TRIU, :], rhs=gg[:, t, :],
                                             start=True, stop=True),
                     r=["cm", ("gg", 0), ("gg", 1)], w=[("ps", MISC)])
            S.act(lambda h: h.copy(out=gcs[:, :, :], in_=ps[:, MISC, 32:40].rearrange("p (t f) -> p t f", f=2)),
                  r=[("ps", MISC)], w=["gcs"])

        def drive(fast, slow):
            live_f = list(fast)
            live_s = list(slow)
            rnd = 0
            while live_f or live_s:
                for gen in list(live_f):
                    if next(gen, StopIteration) is StopIteration:
                        live_f.remove(gen)
                if live_s and (rnd % 3 == 0 or not live_f):
                    for gen in list(live_s):
                        if next(gen, StopIteration) is StopIteration:
                            live_s.remove(gen)
                rnd += 1

        preamble(0)
        drive([prep(0, 0, 0, 0), prep(0, 1, 0, 0)], [])
        for tg in range(NG):
            gp = tg % 2
            if tg + 1 < NG:
                preamble(tg + 1)
                drive([prep(tg + 1, 0, (tg + 1) % 2, (tg + 1) % 2), prep(tg + 1, 1, (tg + 1) % 2, (tg + 1) % 2)],
                      [scan_out(tg, 0, gp), scan_out(tg, 1, gp)])
            else:
                drive([], [scan_out(tg, 0, gp), scan_out(tg, 1, gp)])
        emit_phase(nc, S, ss)


def gdn_phase4(nc, ss, ps, h_src, og_dst, w_my, sm_d, cm_d, NG, cut=0, pre_fn=None, h_keys=(), og_key=None, post_fn=None):
    import contextlib
    from itertools import zip_longest
    with contextlib.ExitStack() as st:
        sb = lambda name, shape, dt: st.enter_context(sbt(nc, name, shape, dt))
        ones_bf = sb("ones_bf", [128, 128], BF16)
        ones_f = sb("ones_f", [128, 128], F32)
        eps_col = sb("eps_col", [128, 1], F32)
        one_col = sb("one_col", [128, 1], F32)
        sm = sb("sm_sb", [128, 32], F32)
        cm = sb("cm_sb", [128, 4, 128], F32)
        negA = sb("negA", [128, 2], F32)
        wq = sb("wq", [128, 8, 1028], BF16)
        hTg = sb("hTg", [128, 3, 8, 512], BF16)
        pre = sb("pre", [128, 2, 3, 515], BF16)
        dgw = sb("dgw", [128, 24, 128], BF16)
        idb = sb("idb", [128, 128], BF16)
        vbf = sb("vbf", [128, 2, 512], BF16)
        cs = sb("cs", [128, 2, 3, 512], F32)
        sqb = sb("sqb", [128, 2, 512], BF16)
        sqo = sb("sqo", [128, 2, 512], BF16)
        rro = sb("rro", [128, 2, 512], F32)
        rr = sb("rr", [128, 2, 512], F32)
        qkn = sb("qkn", [128, 2, 2, 512], F32)
        qkb = sb("qkb", [128, 2, 2, 512], BF16)
        zT = sb("zT", [128, 2, 2, 512], BF16)
        bat = sb("bat", [128, 2, 4, 4], F32)
        beta = sb("beta", [128, 2, 4, 2], F32)
        esp = sb("esp", [128, 2, 4, 2], F32)
        gg = sb("gg", [128, 2, 4, 2], F32)
        gcs = sb("gcs", [128, 2, 4, 2], F32)
        Rm = sb("Rm", [128, 2, 4, 128], F32)
        tdm = sb("tdm", [128, 2, 4, 128], F32)
        egcB = sb("egcB", [128, 2, 4, 128], F32)
        W4 = sb("W4", [128, 2, 4, 128], F32)
        Am = sb("Am", [128, 2, 4, 128], BF16)
        X = sb("X", [128, 2, 2, 4, 3, 128], BF16)
        P0f = sb("P0f", [128, 2, 4, 128], F32)
        ILf = sb("ILf", [128, 2, 4, 128], F32)
        Ttf = sb("Ttf", [128, 2, 4, 128], F32)
        Ttq = sb("Ttq", [128, 2, 4, 128], BF16)
        RpT = sb("RpT", [128, 2, 4, 128], BF16)
        Tnb = sb("Tnb", [128, 2, 4, 128], BF16)
        glr = sb("glr", [128, 2, 4], F32)
        gl = sb("gl", [128, 2, 2, 4], F32)
        ekd = sb("ekd", [128, 2, 4], F32)
        egc = sb("egc", [128, 2, 4], F32)
        bg = sb("bg", [128, 2, 4], F32)
        Ttb = sb("Ttb", [128, 2, 4, 128], BF16)
        ATb = sb("ATb", [128, 2, 2, 4, 128], BF16)
        qdT = sb("qdT", [128, 2, 2, 4, 128], BF16)
        kd = sb("kd", [128, 2, 2, 4, 128], BF16)
        kbg = sb("kbg", [128, 2, 4, 128], BF16)
        vb = sb("vb", [128, 2, 4, 128], BF16)
        uu = sb("uu", [128, 2, 2, 4, 128], F32)
        wTb = sb("wTb", [128, 2, 2, 4, 128], BF16)
        vnb = sb("vnb", [128, 2, 128], BF16)
        Sst = sb("Sst", [128, 2, 128], F32)
        Sb = sb("Sb", [128, 2, 128], BF16)
        oT = sb("oT", [128, 2, 512], F32)
        sz = sb("sz", [128, 2, 512], F32)
        og = sb("og", [128, 2, 512], BF16)

        IDN, TRIU, NMI, STR = 0, 1, 2, 3
        B4 = [128, 4, 128]
        mask4 = lambda mi: cm[:, mi, :].unsqueeze(1).to_broadcast(B4)
        PROJ, MISC = 0, 1
        XB = lambda hd: 2 + hd
        YB = lambda hd: 4 + 2 * hd
        psX = lambda hd: ps[:, XB(hd), :].rearrange("p (t c) -> p t c", c=128)
        psY = lambda hd: ps[:, YB(hd):YB(hd) + 2, :].rearrange("p b (t c) -> p (b t) c", c=256)
        KX = lambda hd: [("ps", XB(hd))]
        KY = lambda hd: [("ps", YB(hd)), ("ps", YB(hd) + 1)]
        S = Sched()
        if pre_fn is not None:
            pre_fn(S)
        S.pool(lambda h: h.memset(ones_bf[:, :], 1.0), w=["ones_bf"])
        S.pool(lambda h: h.memset(ones_f[:, :], 1.0), w=["ones_f"])
        S.pool(lambda h: h.memset(eps_col[:, :], EPS), w=["eps_col"])
        S.pool(lambda h: h.memset(one_col[:, :], 1.0), w=["one_col"])
        S.pool(lambda h: h.memset(pre[:, :, :, :], 0.0), w=[("pre", a, b) for a in range(2) for b in range(3)])
        S.pool(lambda h: h.memset(Sst[:, :, :], 0.0), w=[("S", 0), ("S", 1)])
        S.pool(lambda h: h.memset(Sb[:, :, :], 0.0), w=[("Sb", 0), ("Sb", 1)])
        S.dma(lambda h: h.dma_start(out=sm[:, :], in_=sm_d[:, :]), w=["sm"])
        S.dma(lambda h: h.dma_start(out=cm[:, :, :], in_=cm_d[:, :, :]), w=["cm"])
        wv = w_my.rearrange("(c p) n -> p c n", p=128)
        for c in range(8):
            S.dma(lambda h, c=c: h.dma_start(out=wq[:, c, :], in_=wv[:, c, :]), w=[("wq", c)], q="pool")
        S.act(lambda h: h.activation(out=negA[:, :], in_=sm[:, 24:26], func=AF.Exp), r=["sm"], w=["negA"])
        S.dve(lambda h: h.tensor_scalar(out=negA[:, :], in0=negA[:, :], scalar1=-1.0, scalar2=None, op0=ALU.mult),
              r=["negA"], w=["negA"])
        for idx in range(24):
            S.dve(lambda h, idx=idx: h.tensor_scalar(out=dgw[:, idx, :], in0=cm[:, IDN, :], scalar1=sm[:, idx:idx + 1], scalar2=None,
                                                     op0=ALU.mult), r=["cm", "sm"], w=["dgw"])
        S.dve(lambda h: h.tensor_copy(out=idb[:, :], in_=cm[:, IDN, :]), r=["cm"], w=["idb"])
        WQ = [("wq", c) for c in range(8)]
        tsl = lambda ci: slice(ci * 128, (ci + 1) * 128)

        def prep(tg, hd, b, gp):
            beta_, gg_, gcs_ = beta[:, gp, :, :], gg[:, gp, :, :], gcs[:, gp, :, :]
            bcol = lambda t4: t4[:, :, hd:hd + 1].to_broadcast(B4)
            bvec = lambda v: v[:, hd, :].unsqueeze(2).to_broadcast(B4)
            for j in range(4):
                for c in range(8):
                    S.pe(lambda h, j=j, c=c: h.matmul(ps[:, PROJ, :], lhsT=wq[:, c, hd * 512 + j * 128:hd * 512 + (j + 1) * 128],
                                                      rhs=hTg[:, b, c, :], start=(c == 0), stop=(c == 7)),
                         r=[("hTg", b)] + WQ, w=[("ps", PROJ)])
                if j < 3:
                    S.act(lambda h, j=j: h.copy(out=pre[:, hd, j, 3:515], in_=ps[:, PROJ, :]), r=[("ps", PROJ)], w=[("pre", hd, j)])
                else:
                    S.act(lambda h: h.activation(out=zT[:, hd, gp, :], in_=ps[:, PROJ, :], func=AF.Silu), r=[("ps", PROJ)], w=[("zT", hd, gp)])
                yield
            for j in range(3):
                for k in range(4):
                    S.pe(lambda h, j=j, k=k: h.matmul(ps[:, MISC, :], lhsT=dgw[:, (hd * 3 + j) * 4 + k, :], rhs=pre[:, hd, j, k:k + 512],
                                                      start=(k == 0), stop=(k == 3)), r=[("pre", hd, j), "dgw"], w=[("ps", MISC)])
                S.pool(lambda h, j=j: h.tensor_copy(out=pre[:, hd, j, 0:3], in_=pre[:, hd, j, 512:515]),
                       r=[("pre", hd, j)], w=[("pre", hd, j)])
                if j < 2:
                    S.act(lambda h, j=j: h.activation(out=cs[:, hd, j, :], in_=ps[:, MISC, :], func=AF.Silu),
                          r=[("ps", MISC)], w=[("cs", hd, j)])
                else:
                    S.act(lambda h: h.activation(out=vbf[:, hd, :], in_=ps[:, MISC, :], func=AF.Silu),
                          r=[("ps", MISC)], w=[("vbf", hd)])
                yield
            for j in range(2):
                S.act(lambda h, j=j: h.activation(out=sqb[:, hd, :], in_=cs[:, hd, j, :], func=AF.Square), r=[("cs", hd, j)], w=[("sqb", hd)])
                S.pe(lambda h: h.matmul(ps[:, PROJ, :], lhsT=ones_bf[:, :], rhs=sqb[:, hd, :], start=True, stop=True),
                     r=[("sqb", hd), "ones_bf"], w=[("ps", PROJ)])
                S.act(lambda h: h.activation(out=rr[:, hd, :], in_=ps[:, PROJ, :], func=AF.Ln, bias=eps_col[:, 0:1]),
                      r=[("ps", PROJ), "eps_col"], w=[("rr", hd)])
                S.act(lambda h: h.activation(out=rr[:, hd, :], in_=rr[:, hd, :], func=AF.Exp, scale=-0.5), r=[("rr", hd)], w=[("rr", hd)])
                sc = (128.0 ** -0.5) if j == 0 else 1.0
                S.dve(lambda h, j=j, sc=sc: h.scalar_tensor_tensor(out=qkn[:, hd, j, :], in0=cs[:, hd, j, :], scalar=sc, in1=rr[:, hd, :],
                                                                   op0=ALU.mult, op1=ALU.mult),
                      r=[("cs", hd, j), ("rr", hd)], w=[("qkn", hd, j)])
                S.act(lambda h, j=j: h.copy(out=qkb[:, hd, j, :], in_=qkn[:, hd, j, :]), r=[("qkn", hd, j)], w=[("qkb", hd, j)])
                yield
            if cut == 1:
                return
            q4 = qkn[:, hd, 0, :].rearrange("p (t c) -> p t c", c=128)
            S.dve(lambda h: h.tensor_tensor(out=Rm[:, hd, :, :], in0=mask4(TRIU), in1=bcol(gg_), op=ALU.mult),
                  r=["cm", ("gg", hd, gp)], w=[("Rm", hd)])
            S.pe(lambda h: h.matmul(ps[:, XB(hd), :], lhsT=ones_f[:, :], rhs=Rm[:, hd, :, :].rearrange("p t c -> p (t c)"),
                                    start=True, stop=True), r=[("Rm", hd), "ones_f"], w=KX(hd))
            yield
            S.dve(lambda h: h.tensor_tensor(out=tdm[:, hd, :, :], in0=psX(hd), in1=bcol(gcs_), op=ALU.subtract),
                  r=KX(hd) + [("gcs", gp)], w=[("tdm", hd)])
            S.act(lambda h: h.activation(out=egcB[:, hd, :, :], in_=psX(hd), func=AF.Exp), r=KX(hd), w=[("egcB", hd)])
            S.act(lambda h: h.copy(out=glr[:, hd, :].unsqueeze(2), in_=psX(hd)[:, :, 127:128]), r=KX(hd), w=[("glr", hd)])
            S.dve(lambda h: h.tensor_tensor(out=tdm[:, hd, :, :], in0=tdm[:, hd, :, :], in1=mask4(NMI), op=ALU.max),
                  r=[("tdm", hd), "cm"], w=[("tdm", hd)])
            S.act(lambda h: h.activation(out=tdm[:, hd, :, :], in_=tdm[:, hd, :, :], func=AF.Exp, scale=-1.0),
                  r=[("tdm", hd)], w=[("tdm", hd)])
            S.act(lambda h: h.activation(out=gl[:, hd, gp, :], in_=glr[:, hd, :], func=AF.Exp), r=[("glr", hd)], w=[("gl", hd, gp)])
            S.dve(lambda h: h.tensor_tensor(out=ekd[:, hd, :], in0=glr[:, hd, :], in1=gcs_[:, :, hd], op=ALU.subtract),
                  r=[("glr", hd), ("gcs", gp)], w=[("ekd", hd)])
            S.act(lambda h: h.activation(out=ekd[:, hd, :], in_=ekd[:, hd, :], func=AF.Exp), r=[("ekd", hd)], w=[("ekd", hd)])
            S.act(lambda h: h.activation(out=egc[:, hd, :], in_=gcs_[:, :, hd], func=AF.Exp), r=[("gcs", gp)], w=[("egc", hd)])
            S.dve(lambda h: h.tensor_tensor(out=bg[:, hd, :], in0=egc[:, hd, :], in1=beta_[:, :, hd], op=ALU.mult),
                  r=[("egc", hd), ("beta", gp)], w=[("bg", hd)])
            S.dve(lambda h: h.tensor_tensor(out=qdT[:, hd, gp, :, :], in0=q4, in1=egcB[:, hd, :, :], op=ALU.mult),
                  r=[("qkn", hd, 0), ("egcB", hd)], w=[("qdT", hd, gp)])
            S.dve(lambda h: h.tensor_tensor(out=W4[:, hd, :, :], in0=tdm[:, hd, :, :], in1=mask4(STR), op=ALU.mult),
                  r=[("tdm", hd), "cm"], w=[("W4", hd)])
            S.dve(lambda h: h.scalar_tensor_tensor(out=W4[:, hd, :, :], in0=W4[:, hd, :, :], scalar=-1.0, in1=bcol(beta_),
                                                   op0=ALU.mult, op1=ALU.mult), r=[("W4", hd), ("beta", gp)], w=[("W4", hd)])
            yield
            for ci in range(4):
                S.pe(lambda h, ci=ci: h.matmul(psX(hd)[:, ci, :], lhsT=qkb[:, hd, 1, tsl(ci)], rhs=qkb[:, hd, 1, tsl(ci)],
                                               start=True, stop=True), r=[("qkb", hd, 1)], w=KX(hd))
            S.dve(lambda h: h.tensor_tensor(out=P0f[:, hd, :, :], in0=psX(hd), in1=W4[:, hd, :, :], op=ALU.mult),
                  r=KX(hd) + [("W4", hd)], w=[("P0f", hd)])
            S.act(lambda h: h.copy(out=X[:, hd, 0, :, 0, :], in_=P0f[:, hd, :, :]), r=[("P0f", hd)], w=[("XP", hd, 0)])
            S.dve(lambda h: h.scalar_tensor_tensor(out=ILf[:, hd, :, :], in0=P0f[:, hd, :, :], scalar=-1.0, in1=mask4(IDN),
                                                   op0=ALU.mult, op1=ALU.add), r=[("P0f", hd), "cm"], w=[("ILf", hd)])
            yield
            for ci in range(4):
                S.pe(lambda h, ci=ci: h.matmul(psX(hd)[:, ci, :], lhsT=qkb[:, hd, 0, tsl(ci)], rhs=qkb[:, hd, 1, tsl(ci)],
                                               start=True, stop=True), r=[("qkb", hd, 0), ("qkb", hd, 1)], w=KX(hd))
            S.dve(lambda h: h.tensor_tensor(out=Am[:, hd, :, :], in0=psX(hd), in1=tdm[:, hd, :, :], op=ALU.mult),
                  r=KX(hd) + [("tdm", hd)], w=[("Am", hd)])
            yield
            for ci in range(4):
                S.pe(lambda h, ci=ci: h.matmul(psX(hd)[:, ci, :], lhsT=P0f[:, hd, ci, :], rhs=cm[:, IDN, :], start=True, stop=True),
                     r=[("P0f", hd), "cm"], w=KX(hd))
            S.act(lambda h: h.copy(out=X[:, hd, 0, :, 1, :], in_=psX(hd)), r=KX(hd), w=[("XQ", hd, 0)])
            S.dve(lambda h: h.tensor_tensor(out=X[:, hd, 1, :, 2, :], in0=psX(hd), in1=mask4(IDN), op=ALU.add),
                  r=KX(hd) + ["cm"], w=[("XT", hd, 1)])
            yield
            for ci in range(4):
                S.pe(lambda h, ci=ci: h.matmul(psX(hd)[:, ci, :], lhsT=Am[:, hd, ci, :], rhs=idb[:, :], start=True, stop=True),
                     r=[("Am", hd), "idb"], w=KX(hd))
            S.act(lambda h: h.copy(out=ATb[:, hd, gp, :, :], in_=psX(hd)), r=KX(hd), w=[("ATb", hd, gp)])
            yield
            for ci in range(4):
                S.pe(lambda h, ci=ci: h.matmul(psX(hd)[:, ci, :], lhsT=qkb[:, hd, 1, tsl(ci)], rhs=idb[:, :], start=True, stop=True),
                     r=[("qkb", hd, 1), "idb"], w=KX(hd))
            S.dve(lambda h: h.tensor_tensor(out=kd[:, hd, gp, :, :], in0=psX(hd), in1=bvec(ekd), op=ALU.mult),
                  r=KX(hd) + [("ekd", hd)], w=[("kd", hd, gp)])
            S.dve(lambda h: h.tensor_tensor(out=kbg[:, hd, :, :], in0=psX(hd), in1=bvec(bg), op=ALU.mult),
                  r=KX(hd) + [("bg", hd)], w=[("kbg", hd)])
            yield
            for ci in range(4):
                S.pe(lambda h, ci=ci: h.matmul(psX(hd)[:, ci, :], lhsT=vbf[:, hd, tsl(ci)], rhs=idb[:, :], start=True, stop=True),
                     r=[("vbf", hd), "idb"], w=KX(hd))
            S.dve(lambda h: h.tensor_tensor(out=vb[:, hd, :, :], in0=psX(hd), in1=bcol(beta_), op=ALU.mult),
                  r=KX(hd) + [("beta", gp)], w=[("vb", hd)])
            yield
            NL = 5
            for m in range(NL):
                cur, nxt = m % 2, (m + 1) % 2
                for ci in range(4):
                    if m == 0:
                        S.pe(lambda h, ci=ci: h.matmul(psY(hd)[:, ci, 0:128], lhsT=X[:, hd, 0, ci, 0, :], rhs=X[:, hd, 0, ci, 1, :],
                                                       start=True, stop=True), r=[("XP", hd, 0), ("XQ", hd, 0)], w=KY(hd))
                    elif m < NL - 1:
                        S.pe(lambda h, ci=ci, cur=cur: h.matmul(psY(hd)[:, ci, :], lhsT=X[:, hd, cur, ci, 0, :],
                                                                rhs=X[:, hd, cur, ci, 1:3, :].rearrange("p a b -> p (a b)"),
                                                                start=True, stop=True),
                             r=[("XP", hd, cur), ("XQ", hd, cur), ("XT", hd, cur)], w=KY(hd))
                    else:
                        S.pe(lambda h, ci=ci, cur=cur: h.matmul(psY(hd)[:, ci, 128:256], lhsT=X[:, hd, cur, ci, 0, :],
                                                                rhs=X[:, hd, cur, ci, 2, :], start=True, stop=True),
                             r=[("XP", hd, cur), ("XT", hd, cur)], w=KY(hd))
                if m < NL - 1:
                    for ci in range(4):
                        S.pe(lambda h, ci=ci, cur=cur: h.matmul(psX(hd)[:, ci, :], lhsT=X[:, hd, cur, ci, 1, :], rhs=X[:, hd, cur, ci, 0, :],
                                                                start=True, stop=True), r=[("XP", hd, cur), ("XQ", hd, cur)], w=KX(hd))
                    S.act(lambda h, nxt=nxt: h.copy(out=X[:, hd, nxt, :, 1, :], in_=psY(hd)[:, :, 0:128]), r=KY(hd), w=[("XQ", hd, nxt)])
                    S.act(lambda h, nxt=nxt: h.copy(out=X[:, hd, nxt, :, 0, :], in_=psX(hd)), r=KX(hd), w=[("XP", hd, nxt)])
                if 1 <= m < NL - 1:
                    S.dve(lambda h, cur=cur, nxt=nxt: h.tensor_tensor(out=X[:, hd, nxt, :, 2, :], in0=psY(hd)[:, :, 128:256],
                                                                      in1=X[:, hd, cur, :, 2, :], op=ALU.add),
                          r=KY(hd) + [("XT", hd, cur)], w=[("XT", hd, nxt)])
                if m == NL - 1:
                    S.dve(lambda h, cur=cur: h.tensor_tensor(out=Ttf[:, hd, :, :], in0=psY(hd)[:, :, 128:256], in1=X[:, hd, cur, :, 2, :],
                                                             op=ALU.add), r=KY(hd) + [("XT", hd, cur)], w=[("Ttf", hd)])
                    S.act(lambda h: h.copy(out=Ttq[:, hd, :, :], in_=Ttf[:, hd, :, :]), r=[("Ttf", hd)], w=[("Ttq", hd)])
                yield
            for ci in range(4):
                S.pe(lambda h, ci=ci: h.matmul(psX(hd)[:, ci, :], lhsT=ILf[:, hd, ci, :], rhs=Ttf[:, hd, ci, :], start=True, stop=True),
                     r=[("ILf", hd), ("Ttf", hd)], w=KX(hd))
            for ci in range(4):
                S.pe(lambda h, ci=ci: h.matmul(psY(hd)[:, ci, 0:128], lhsT=Ttq[:, hd, ci, :], rhs=idb[:, :], start=True, stop=True),
                     r=[("Ttq", hd), "idb"], w=KY(hd))
            S.dve(lambda h: h.scalar_tensor_tensor(out=RpT[:, hd, :, :], in0=psX(hd), scalar=-1.0, in1=mask4(IDN),
                                                   op0=ALU.mult, op1=ALU.add), r=KX(hd) + ["cm"], w=[("RpT", hd)])
            S.act(lambda h: h.copy(out=Tnb[:, hd, :, :], in_=psY(hd)[:, :, 0:128]), r=KY(hd), w=[("Tnb", hd)])
            yield
            for ci in range(4):
                S.pe(lambda h, ci=ci: h.matmul(psX(hd)[:, ci, :], lhsT=Tnb[:, hd, ci, :], rhs=RpT[:, hd, ci, :], start=True, stop=True),
                     r=[("Tnb", hd), ("RpT", hd)], w=KX(hd))
            S.dve(lambda h: h.tensor_tensor(out=Ttb[:, hd, :, :], in0=psX(hd), in1=Ttf[:, hd, :, :], op=ALU.add),
                  r=KX(hd) + [("Ttf", hd)], w=[("Ttb", hd)])
            yield
            for ci in range(4):
                S.pe(lambda h, ci=ci: h.matmul(psX(hd)[:, ci, :], lhsT=Ttb[:, hd, ci, :], rhs=vb[:, hd, ci, :], start=True, stop=True),
                     r=[("Ttb", hd), ("vb", hd)], w=KX(hd))
            S.act(lambda h: h.copy(out=uu[:, hd, gp, :, :], in_=psX(hd)), r=KX(hd), w=[("uu", hd, gp)])
            for ci in range(4):
                S.pe(lambda h, ci=ci: h.matmul(psY(hd)[:, ci, 0:128], lhsT=kbg[:, hd, ci, :], rhs=Ttb[:, hd, ci, :], start=True, stop=True),
                     r=[("Ttb", hd), ("kbg", hd)], w=KY(hd))
            S.act(lambda h: h.copy(out=wTb[:, hd, gp, :, :], in_=psY(hd)[:, :, 0:128]), r=KY(hd), w=[("wTb", hd, gp)])
            yield

        def scan_out(tg, hd, gp):
            sbk = YB(hd) + 1
            KS = [("ps", sbk)]
            for ci in range(4):
                S.pe(lambda h, ci=ci: h.matmul(ps[:, sbk, 128:256], lhsT=wTb[:, hd, gp, ci, :], rhs=Sb[:, hd, :], start=True, stop=True),
                     r=[("wTb", hd, gp), ("Sb", hd)], w=KS)
                S.dve(lambda h, ci=ci: h.tensor_tensor(out=vnb[:, hd, :], in0=uu[:, hd, gp, ci, :], in1=ps[:, sbk, 128:256], op=ALU.subtract),
                      r=[("uu", hd, gp)] + KS, w=[("vnb", hd)])
                S.pe(lambda h, ci=ci: h.matmul(ps[:, sbk, 384:512], lhsT=Sb[:, hd, :], rhs=qdT[:, hd, gp, ci, :], start=True, stop=False),
                     r=[("Sb", hd), ("qdT", hd, gp)], w=KS)
                S.pe(lambda h, ci=ci: h.matmul(ps[:, sbk, 384:512], lhsT=vnb[:, hd, :], rhs=ATb[:, hd, gp, ci, :], start=False, stop=True),
                     r=[("vnb", hd), ("ATb", hd, gp)], w=KS)
                S.pe(lambda h, ci=ci: h.matmul(ps[:, sbk, 256:384], lhsT=kd[:, hd, gp, ci, :], rhs=vnb[:, hd, :], start=True, stop=True),
                     r=[("kd", hd, gp), ("vnb", hd)], w=KS)
                S.act(lambda h, ci=ci: h.copy(out=oT[:, hd, tsl(ci)], in_=ps[:, sbk, 384:512]), r=KS, w=[("oT", hd)])
                S.dve(lambda h, ci=ci: h.scalar_tensor_tensor(out=Sst[:, hd, :], in0=Sst[:, hd, :], scalar=gl[:, hd, gp, ci:ci + 1],
                                                              in1=ps[:, sbk, 256:384], op0=ALU.mult, op1=ALU.add),
                      r=[("S", hd), ("gl", hd, gp)] + KS, w=[("S", hd)])
                S.act(lambda h: h.copy(out=Sb[:, hd, :], in_=Sst[:, hd, :]), r=[("S", hd)], w=[("Sb", hd)])
                yield
            S.act(lambda h: h.activation(out=sqo[:, hd, :], in_=oT[:, hd, :], func=AF.Square), r=[("oT", hd)], w=[("sqo", hd)])
            S.pe(lambda h: h.matmul(ps[:, PROJ, :], lhsT=ones_bf[:, :], rhs=sqo[:, hd, :], start=True, stop=True),
                 r=[("sqo", hd), "ones_bf"], w=[("ps", PROJ)])
            S.act(lambda h: h.activation(out=rro[:, hd, :], in_=ps[:, PROJ, :], func=AF.Ln, scale=1.0 / 128, bias=eps_col[:, 0:1]),
                  r=[("ps", PROJ), "eps_col"], w=[("rro", hd)])
            S.act(lambda h: h.activation(out=rro[:, hd, :], in_=rro[:, hd, :], func=AF.Exp, scale=-0.5), r=[("rro", hd)], w=[("rro", hd)])
            S.dve(lambda h: h.scalar_tensor_tensor(out=oT[:, hd, :], in0=oT[:, hd, :], scalar=sm[:, 28:29], in1=rro[:, hd, :],
                                                   op0=ALU.mult, op1=ALU.mult), r=[("oT", hd), "sm", ("rro", hd)], w=[("oT", hd)])
            S.dve(lambda h: h.tensor_tensor(out=og[:, hd, :], in0=oT[:, hd, :], in1=zT[:, hd, gp, :], op=ALU.mult),
                  r=[("oT", hd), ("zT", hd, gp)], w=[("og", hd)])
            S.dma(lambda h: h.dma_start(out=og_dst(hd, tg), in_=og[:, hd, :]), r=[("og", hd)],
                  w=([og_key(tg)] if og_key is not None else []))
            done_units.add((tg, hd))
            if post_fn is not None and (tg, 0) in done_units and (tg, 1) in done_units:
                post_fn(S, tg)
            yield

        def load_h(tg):
            b = tg % 3
            hk = h_keys(tg) if callable(h_keys) else list(h_keys)
            S.dma(lambda h, b=b, tg=tg: h.dma_start(out=hTg[:, b, :, :], in_=h_src(tg)), r=hk, w=[("hTg", b)])

        def preamble(tg):
            b = tg % 3
            gq = tg % 2
            if tg == 0:
                load_h(0)
                if NG > 1:
                    load_h(1)
            if tg + 2 < NG:
                load_h(tg + 2)
            for t in range(4):
                for c in range(8):
                    S.pe(lambda h, b=b, t=t, c=c: h.matmul(ps[:, MISC, t * 4:(t + 1) * 4], lhsT=hTg[:, b, c, t * 128:(t + 1) * 128],
                                                            rhs=wq[:, c, 1024:1028], start=(c == 0), stop=(c == 7)),
                         r=[("hTg", b)] + WQ, w=[("ps", MISC)])
            S.act(lambda h: h.copy(out=bat[:, gq, :, :], in_=ps[:, MISC, 0:16].rearrange("p (t f) -> p t f", f=4)),
                  r=[("ps", MISC)], w=[("bat", gq)])
            S.act(lambda h: h.activation(out=beta[:, gq, :, :], in_=bat[:, gq, :, 0:2], func=AF.Exp, scale=-1.0), r=[("bat", gq)], w=[("beta", gq)])
            S.dve(lambda h: h.tensor_scalar(out=beta[:, gq, :, :], in0=beta[:, gq, :, :], scalar1=1.0, scalar2=None, op0=ALU.add),
                  r=[("beta", gq)], w=[("beta", gq)])
            S.dve(lambda h: h.reciprocal(out=beta[:, gq, :, :], in_=beta[:, gq, :, :]), r=[("beta", gq)], w=[("beta", gq)])
            for hd in range(2):
                S.act(lambda h, hd=hd: h.activation(out=esp[:, gq, :, hd:hd + 1], in_=bat[:, gq, :, 2 + hd:3 + hd], func=AF.Exp,
                                                    bias=sm[:, 26 + hd:27 + hd]), r=[("bat", gq), "sm"], w=[("esp", hd, gq)])
                S.act(lambda h, hd=hd: h.activation(out=esp[:, gq, :, hd:hd + 1], in_=esp[:, gq, :, hd:hd + 1], func=AF.Ln,
                                                    bias=one_col[:, 0:1]), r=[("esp", hd, gq), "one_col"], w=[("esp", hd, gq)])
                S.dve(lambda h, hd=hd: h.tensor_scalar(out=gg[:, gq, :, hd:hd + 1], in0=esp[:, gq, :, hd:hd + 1],
                                                       scalar1=negA[:, hd:hd + 1], scalar2=None, op0=ALU.mult),
                      r=[("esp", hd, gq), "negA"], w=[("gg", hd, gq)])
            for t in range(4):
                S.pe(lambda h, t=t: h.matmul(ps[:, MISC, 32 + 2 * t:34 + 2 * t], lhsT=cm[:, TRIU, :], rhs=gg[:, gq, t, :],
                                             start=True, stop=True),
                     r=["cm", ("gg", 0, gq), ("gg", 1, gq)], w=[("ps", MISC)])
            S.act(lambda h: h.copy(out=gcs[:, gq, :, :], in_=ps[:, MISC, 32:40].rearrange("p (t f) -> p t f", f=2)),
                  r=[("ps", MISC)], w=[("gcs", gq)])

        done_units = set()

        def stream(hd):
            for tg in range(NG):
                if hd == 0:
                    preamble(tg)
                    yield
                yield from prep(tg, hd, tg % 3, tg % 2)
                yield from scan_out(tg, hd, tg % 2)

        LAG = 14
        g0, g1 = stream(0), stream(1)
        alive0 = alive1 = True
        for _ in range(LAG):
            alive0 = next(g0, StopIteration) is not StopIteration
        while alive0 or alive1:
            if alive1:
                alive1 = next(g1, StopIteration) is not StopIteration
            if alive0:
                alive0 = next(g0, StopIteration) is not StopIteration
        emit_phase(nc, S, ss)


def gdn_masks():
    i = np.arange(128)
    idn = (i[:, None] == i[None, :]).astype(np.float32)
    triu = (i[:, None] <= i[None, :]).astype(np.float32)
    nmi = np.where(i[:, None] >= i[None, :], 0.0, 30000.0).astype(np.float32)
    strict = (i[:, None] > i[None, :]).astype(np.float32)
    return np.ascontiguousarray(np.stack([idn, triu, nmi, strict], axis=1))


def l2_inputs(hT_b, inp):
    w_in = np.asarray(inp["a_w_in"][0], np.float32)
    w_conv = np.asarray(inp["a_w_conv"][0], np.float32)
    cm = gdn_masks()
    maps = []
    for c in range(NCORES):
        bsel, hp = c // 4, c % 4
        cols = []
        sm = np.zeros((128, 32), np.float32)
        for hd in range(2):
            hh = hp * 2 + hd
            for j in range(3):
                cols.append(w_in[:, j * 1024 + hh * 128:j * 1024 + (hh + 1) * 128])
                for k in range(4):
                    sm[:, (hd * 3 + j) * 4 + k] = w_conv[k, j * 1024 + hh * 128:j * 1024 + (hh + 1) * 128]
            cols.append(w_in[:, 3072 + hh * 128:3072 + (hh + 1) * 128])
            sm[:, 24 + hd] = inp["a_A_log"][0][hh]
            sm[:, 26 + hd] = inp["a_dt_bias"][0][hh]
        sm[:, 28] = inp["a_out_norm"][0]
        for hd in range(2):
            cols.append(w_in[:, 4096 + hp * 2 + hd:4096 + hp * 2 + hd + 1])
        for hd in range(2):
            cols.append(w_in[:, 4104 + hp * 2 + hd:4104 + hp * 2 + hd + 1])
        w_my = np.ascontiguousarray(np.concatenate(cols, axis=1))
        assert w_my.shape == (1024, 1028)
        maps.append(dict(hT=hT_b[bsel], w_my=w_my, sm=sm, cm=cm))
    return maps


def proj_residual(S, C, w_dram, src, T, bias_cols=None, wkey="wg", extra_w=None):
    ps, xT, wg = C["ps"], C["xT"], C["wg"]
    wv = w_dram.rearrange("(c p) n -> p c n", p=128)
    for half in range(2):
        S.dma(lambda h, half=half: h.dma_start(out=wg[:, half, :, :], in_=wv[:, :, half * 512:(half + 1) * 512]),
              w=[(wkey, half, 0), (wkey, half, 1)], q="pool")
        for k in range(4):
            dc = half * 4 + k
            bs = 4 * (dc % 2)
            for kc in range(8):
                for tt in range(4):
                    S.pe(lambda h, half=half, k=k, kc=kc, tt=tt, bs=bs: h.matmul(
                        ps[:, bs + tt, :], lhsT=wg[:, half, kc, k * 128:(k + 1) * 128], rhs=src[:, kc, tt * 512:(tt + 1) * 512],
                        start=(kc == 0), stop=(kc == 7)),
                        r=[(wkey, half, 0), ("hT", kc, tt)], w=[("ps", bs + tt)] + list((extra_w or {}).get(bs + tt, [])))
            pv = ps[:, bs:bs + 4, :].rearrange("p a b -> p (a b)")
            rk = [("ps", bs + t) for t in range(4)] + [("xT", dc, t) for t in range(4)]
            wk = [("xT", dc, t) for t in range(4)]
            if bias_cols is None:
                S.dve(lambda h, dc=dc, pv=pv: h.tensor_tensor(out=xT[:, dc, :], in0=pv, in1=xT[:, dc, :], op=ALU.add), r=rk, w=wk)
            else:
                S.dve(lambda h, dc=dc, pv=pv: h.scalar_tensor_tensor(out=xT[:, dc, :], in0=pv, scalar=bias_cols[:, dc:dc + 1],
                                                                     in1=xT[:, dc, :], op0=ALU.add, op1=ALU.add),
                      r=rk + ["smalls"], w=wk)


def build_L3(T=2048):
    import contextlib
    nc = bass.Bass("TRN2", target_bir_lowering=False)
    x = nc.dram_tensor("xT_in", [D, T], F32, kind="ExternalInput").ap()
    og = nc.dram_tensor("ogT_in", [D, T], BF16, kind="ExternalInput").ap()
    smalls = nc.dram_tensor("smalls", [128, 24], F32, kind="ExternalInput").ap()
    w_out = nc.dram_tensor("w_out", [D, D], F32, kind="ExternalInput").ap()
    w_gu_a = nc.dram_tensor("w_gu_a", [D, 2 * DFF], F32, kind="ExternalInput").ap()
    w_down_a = nc.dram_tensor("w_down_a", [DFF, D], F32, kind="ExternalInput").ap()
    w_gu_b = nc.dram_tensor("w_gu_b", [D, 2 * DFF], F32, kind="ExternalInput").ap()
    w_down_b = nc.dram_tensor("w_down_b", [DFF, D], F32, kind="ExternalInput").ap()
    x_out = nc.dram_tensor("xT_out", [D, T], F32, kind="ExternalOutput").ap()
    h_out = nc.dram_tensor("hT_out", [D, T], BF16, kind="ExternalOutput").ap()
    with contextlib.ExitStack() as st:
        ss = SemState(nc, st)
        C = alloc_common(nc, st, T)
        alloc_ffn(nc, st, C, T)
        C["smalls"] = st.enter_context(sbt(nc, "smalls_sb", [128, 24], F32))
        S = Sched()
        init_consts(S, C)
        S.dma(lambda h: h.dma_start(out=C["smalls"][:, :], in_=smalls[:, :]), w=["smalls"], q="sp")
        load_xT(S, C, x, T)
        ogv = og.rearrange("(c p) t -> p c t", p=128)
        for tt in range(4):
            S.dma(lambda h, tt=tt: h.dma_start(out=C["hT"][:, :, tt * 512:(tt + 1) * 512], in_=ogv[:, :, tt * 512:(tt + 1) * 512]),
                  w=[("hT", c, tt) for c in range(8)], q="sp")
        proj_residual(S, C, w_out, C["hT"], T)
        rmsnorm_fm(S, C, C["xT"], C["hT"], C["smalls"][:, 0:8], T, "n1")
        ffn_fm(S, C, C["xT"], C["hT"], w_gu_a, w_down_a, T, "fa")
        rmsnorm_fm(S, C, C["xT"], C["hT"], C["smalls"][:, 8:16], T, "n2")
        ffn_fm(S, C, C["xT"], C["hT"], w_gu_b, w_down_b, T, "fb")
        rmsnorm_fm(S, C, C["xT"], C["hT"], C["smalls"][:, 16:24], T, "n3")
        o1 = store_T(S, C, "xT", C["xT"], x_out, T)
        o2 = store_T(S, C, "hT", C["hT"], h_out, T)
        emit_phase(nc, S, ss, final_wait=o1 + o2)
    return nc


def build_L4(T=2048):
    import contextlib
    nc = bass.Bass("TRN2", target_bir_lowering=False)
    TH = T + 128
    A = dict(
        x=nc.dram_tensor("xT_in", [D, T], F32, kind="ExternalInput").ap(),
        hh=nc.dram_tensor("hTh_in", [D, TH], BF16, kind="ExternalInput").ap(),
        smalls=nc.dram_tensor("smalls", [128, 40], F32, kind="ExternalInput").ap(),
        bq64=nc.dram_tensor("bq64", [64, 20], F32, kind="ExternalInput").ap(),
        bvb=nc.dram_tensor("bvb", [128, 256], F32, kind="ExternalInput").ap(),
        masks=nc.dram_tensor("masks", [128, 3, 512], BF16, kind="ExternalInput").ap(),
        w_in=nc.dram_tensor("w_in", [D, 1536], F32, kind="ExternalInput").ap(),
        w_out=nc.dram_tensor("w_out", [D, D], F32, kind="ExternalInput").ap(),
        w_gu=nc.dram_tensor("w_gu", [D, 2 * DFF], F32, kind="ExternalInput").ap(),
        w_down=nc.dram_tensor("w_down", [DFF, D], F32, kind="ExternalInput").ap(),
        y_out=nc.dram_tensor("yT_out", [D, T], F32, kind="ExternalOutput").ap())
    with contextlib.ExitStack() as st:
        ss = SemState(nc, st)
        C = alloc_common(nc, st, T)
        C["xT"] = st.enter_context(sbt(nc, "xT", [128, 8, T], F32))
        l4_phases(nc, ss, C, A, T, fused=False)
    return nc


def l4_phases(nc, ss, C, A, T, fused, pre_fn=None):
    import contextlib
    TH = T + 128
    NB = T // 128
    x, hh, smalls, bq64, bvb, masks = A.get("x"), A.get("hh"), A["smalls"], A["bq64"], A["bvb"], A["masks"]
    w_in, w_out, w_gu, w_down, y_out = A["w_in"], A["w_out"], A["w_gu"], A["w_down"], A["y_out"]
    if True:
        st4 = contextlib.ExitStack()
        C["smalls"] = st4.enter_context(sbt(nc, "smalls4_sb", [128, 40], F32))
        ps, xT, sm = C["ps"], C["xT"], C["smalls"]
        with contextlib.ExitStack() as sa:
            sb = lambda name, shape, dt: sa.enter_context(sbt(nc, name, shape, dt))
            hTh = sb("hTh", [128, 8, TH], BF16)
            OT = sb("OT", [128, 8, T], BF16)
            wk3 = sb("wk3", [128, 2, 8, 384], BF16)
            qT = sb("qT", [64, 4, T], BF16)
            kT = sb("kT", [64, TH], BF16)
            V = sb("V", [128, NB + 1, 2, 128], BF16)
            expT = sb("expT", [128, 2, 2, 512], BF16)
            mk = sb("mk", [128, 3, 512], BF16)
            bq = sb("bq", [64, 20], F32)
            bv = sb("bv", [128, 256], F32)
            esk = sb("esk", [128, 8], F32)
            onesLH = sb("onesLH", [128, 2, 128], BF16)
            rec = sb("rec", [128, 2, 128], F32)
            wg = sb("wgA", [128, 2, 8, 512], BF16)
            C["wg"] = wg
            S = Sched()
            init_consts(S, C)
            S.pool(lambda h: h.memset(V[:, :, :, :], 0.0), w=["V"])
            S.pool(lambda h: h.memset(onesLH[:, :, :], 0.0), w=["onesLH"])
            S.pool(lambda h: h.memset(onesLH[:, 0, 0:64], 1.0), r=["onesLH"], w=["onesLH"])
            S.pool(lambda h: h.memset(onesLH[:, 1, 64:128], 1.0), r=["onesLH"], w=["onesLH"])
            S.dma(lambda h: h.dma_start(out=sm[:, :], in_=smalls[:, :]), w=["smalls"])
            S.dma(lambda h: h.dma_start(out=bq[:, :], in_=bq64[:, :]), w=["bq"])
            if not fused:
                load_xT(S, C, x, T)
                hv = hh.rearrange("(c p) t -> p c t", p=128)
                S.dma(lambda h: h.dma_start(out=hTh[:, :, 0:1152], in_=hv[:, :, 0:1152]), w=["hTh"])
                S.dma(lambda h: h.dma_start(out=hTh[:, :, 1152:TH], in_=hv[:, :, 1152:TH]), w=["hTh2"])
                HR = ["hTh", "hTh2"]
            else:
                pre_fn(S)
                hv = A["h3_loc"].rearrange("(c p) t -> p c t", p=128)
                S.dma(lambda h: h.dma_start(out=hTh[:, :, 128:1152], in_=hv[:, :, 0:1024]), w=["hTh"])
                S.dma(lambda h: h.dma_start(out=hTh[:, :, 1152:TH], in_=hv[:, :, 1024:2048]), w=["hTh2"], q="act")
                hal = A["halo_all"].rearrange("(r c p) t -> p c r t", r=4, c=8, p=128)
                stg = wg[:, 0, :, :].rearrange("p c (r t) -> p c r t", t=128)
                for r_ in range(4):
                    S.dma(lambda h, r_=r_: h.dma_start(out=stg[:, :, r_, :], in_=hal[:, :, r_, :]), r=["halo_all"],
                          w=[("wgA", 0, 0), ("wgA", 0, 1)])
                S.dve(lambda h: h.tensor_scalar(out=hTh[:, :, 0:128], in0=stg[:, :, 0, :], scalar1=sm[:, 32:33], scalar2=None,
                                                op0=ALU.mult), r=[("wgA", 0, 0), "smalls"], w=["hTh3"])
                for r_ in range(1, 4):
                    S.dve(lambda h, r_=r_: h.scalar_tensor_tensor(out=hTh[:, :, 0:128], in0=stg[:, :, r_, :], scalar=sm[:, 32 + r_:33 + r_],
                                                                   in1=hTh[:, :, 0:128], op0=ALU.mult, op1=ALU.add),
                          r=[("wgA", 0, 0), "smalls", "hTh3"], w=["hTh3"])
                HR = ["hTh", "hTh2", "hTh3"]
            S.dma(lambda h: h.dma_start(out=bv[:, :], in_=bvb[:, :]), w=["bv"])
            S.dma(lambda h: h.dma_start(out=mk[:, :, :], in_=masks[:, :, :]), w=["mk"])
            S.act(lambda h: h.activation(out=esk[:, :], in_=sm[:, 24:32], func=AF.Exp), r=["smalls"], w=["esk"])
            wv = w_in.rearrange("(c p) n -> p c n", p=128)
            for kvh in range(4):
                wb = kvh % 2
                S.dma(lambda h, wb=wb, kvh=kvh: h.dma_start(out=wk3[:, wb, :, 0:256], in_=wv[:, :, kvh * 256:(kvh + 1) * 256]),
                      w=[("wk3", wb, 0)], q="pool")
                S.dma(lambda h, wb=wb, kvh=kvh: h.dma_start(out=wk3[:, wb, :, 256:320], in_=wv[:, :, 1024 + kvh * 64:1024 + (kvh + 1) * 64]),
                      w=[("wk3", wb, 1)], q="pool")
                S.dma(lambda h, wb=wb, kvh=kvh: h.dma_start(out=wk3[:, wb, :, 320:384], in_=wv[:, :, 1280 + kvh * 64:1280 + (kvh + 1) * 64]),
                      w=[("wk3", wb, 2)], q="pool")
                for g in range(4):
                    for tt in range(4):
                        bank = tt % 2
                        for c in range(8):
                            S.pe(lambda h, wb=wb, g=g, tt=tt, c=c, bank=bank: h.matmul(
                                ps[0:64, bank, :], lhsT=wk3[:, wb, c, g * 64:(g + 1) * 64], rhs=hTh[:, c, 128 + tt * 512:128 + (tt + 1) * 512],
                                start=(c == 0), stop=(c == 7)), r=[("wk3", wb, 0)] + HR[:2], w=[("ps", bank)])
                        S.act(lambda h, g=g, tt=tt, bank=bank, kvh=kvh: h.activation(
                            out=qT[:, g, tt * 512:(tt + 1) * 512], in_=ps[0:64, bank, :], func=AF.Identity,
                            bias=bq[:, kvh * 4 + g:kvh * 4 + g + 1]), r=[("ps", bank), "bq"], w=[("qT", g)])
                for tt in (1, 2, 3, 4, 0):
                    wdt = 512 if tt < 4 else 128
                    bank = tt % 2
                    for c in range(8):
                        S.pe(lambda h, wb=wb, tt=tt, c=c, bank=bank, wdt=wdt: h.matmul(
                            ps[0:64, bank, 0:wdt], lhsT=wk3[:, wb, c, 256:320], rhs=hTh[:, c, tt * 512:tt * 512 + wdt],
                            start=(c == 0), stop=(c == 7)), r=[("wk3", wb, 1)] + (HR if tt == 0 else HR[:2]), w=[("ps", bank)])
                    S.act(lambda h, tt=tt, bank=bank, wdt=wdt, kvh=kvh: h.activation(
                        out=kT[:, tt * 512:tt * 512 + wdt], in_=ps[0:64, bank, 0:wdt], func=AF.Identity,
                        bias=bq[:, 16 + kvh:17 + kvh]), r=[("ps", bank), "bq"], w=["kT"])
                for blk in list(range(1, NB + 1)) + [0]:
                    bank = 2 + blk % 2
                    for c in range(8):
                        S.pe(lambda h, wb=wb, blk=blk, c=c, bank=bank: h.matmul(
                            ps[:, bank, 0:64], lhsT=hTh[:, c, blk * 128:(blk + 1) * 128], rhs=wk3[:, wb, c, 320:384],
                            start=(c == 0), stop=(c == 7)), r=[("wk3", wb, 2)] + (HR if blk == 0 else HR[:2]), w=[("ps", bank)])
                    S.dve(lambda h, blk=blk, bank=bank, kvh=kvh: h.tensor_tensor(
                        out=V[:, blk, 0, 0:64], in0=ps[:, bank, 0:64], in1=bv[:, kvh * 64:(kvh + 1) * 64], op=ALU.add),
                        r=[("ps", bank), "bv", "V"], w=[("V", blk, 0)])
                    S.dve(lambda h, blk=blk, bank=bank, kvh=kvh: h.tensor_tensor(
                        out=V[:, blk, 1, 64:128], in0=ps[:, bank, 0:64], in1=bv[:, kvh * 64:(kvh + 1) * 64], op=ALU.add),
                        r=[("ps", bank), "bv", "V"], w=[("V", blk, 1)])
                def att_stage1(n, kvh=kvh):
                    eb = n % 2
                    for kb in range(2):
                        bank = 4 + 2 * eb + kb
                        kcol = (n + kb) * 128
                        S.pe(lambda h, n=n, bank=bank, kcol=kcol: h.matmul(
                            ps[:, bank, :], lhsT=kT[:, kcol:kcol + 128], rhs=qT[:, :, n * 128:(n + 1) * 128],
                            start=True, stop=True), r=["kT"] + [("qT", g) for g in range(4)], w=[("ps", bank)])
                        S.act(lambda h, eb=eb, kb=kb, bank=bank: h.activation(
                            out=expT[:, eb, kb, :], in_=ps[:, bank, :], func=AF.Exp, scale=0.125),
                            r=[("ps", bank)], w=[("expT", eb, kb)])
                        mi = (2 if n == 0 else 0) if kb == 0 else 1
                        S.dve(lambda h, eb=eb, kb=kb, mi=mi: h.tensor_tensor(
                            out=expT[:, eb, kb, :], in0=expT[:, eb, kb, :], in1=mk[:, mi, :], op=ALU.mult),
                            r=[("expT", eb, kb), "mk"], w=[("expT", eb, kb)])

                def att_stage2(n, kvh=kvh):
                    eb = n % 2
                    for pair in range(2):
                        ch = kvh * 2 + pair
                        bk = 2 + pair
                        pso = ps[:, bk, 128:256]
                        psd = ps[:, bk, 256:384]
                        key = ("ps", bk)
                        for (dst, lo) in ((pso, None), (psd, 0)):
                            i = 0
                            for kb in range(2):
                                for gi in range(2):
                                    g = pair * 2 + gi
                                    if lo is None:
                                        lh = V[:, n + kb, gi, :]
                                        rk = [("V", n + kb, gi)]
                                    else:
                                        lh = onesLH[:, gi, :]
                                        rk = ["onesLH"]
                                    S.pe(lambda h, dst=dst, lh=lh, eb=eb, kb=kb, g=g, i=i: h.matmul(
                                        dst, lhsT=lh, rhs=expT[:, eb, kb, g * 128:(g + 1) * 128], start=(i == 0), stop=(i == 3)),
                                        r=rk + [("expT", eb, kb)], w=[key])
                                    i += 1
                        S.act(lambda h, pair=pair, psd=psd, ch=ch: h.activation(out=rec[:, pair, :], in_=psd, func=AF.Ln,
                                                                                bias=esk[:, ch:ch + 1]),
                              r=[key, "esk"], w=[("rec", pair)])
                        S.act(lambda h, pair=pair: h.activation(out=rec[:, pair, :], in_=rec[:, pair, :], func=AF.Exp, scale=-1.0),
                              r=[("rec", pair)], w=[("rec", pair)])
                        S.dve(lambda h, pair=pair, pso=pso, ch=ch, n=n: h.tensor_tensor(
                            out=OT[:, ch, n * 128:(n + 1) * 128], in0=pso, in1=rec[:, pair, :], op=ALU.mult),
                            r=[key, ("rec", pair)], w=[("hT", ch, n // 4)])

                att_stage1(0)
                for n in range(NB):
                    if n + 1 < NB:
                        att_stage1(n + 1)
                    att_stage2(n)
            proj_residual(S, C, w_out, OT, T, bias_cols=sm[:, 16:24], wkey="wgA")
            emit_phase(nc, S, ss)
        with contextlib.ExitStack() as sb_:
            C["hT"] = sb_.enter_context(sbt(nc, "hT", [128, 8, T], BF16))
            C["aT"] = sb_.enter_context(sbt(nc, "aT", [128, 11, T // 512, 512], BF16))
            C["sg"] = sb_.enter_context(sbt(nc, "sg", [128, 2, T // 512, 512], BF16))
            C["sqb"] = sb_.enter_context(sbt(nc, "sqb", [128, 8, 512], BF16))
            C["rstd"] = sb_.enter_context(sbt(nc, "rstd", [128, 2, 512], F32))
            C["wg"] = sb_.enter_context(sbt(nc, "wg", [128, 2, 8, 512], BF16))
            C["wd"] = sb_.enter_context(sbt(nc, "wd", [128, 2, 11, 256], BF16))
            S = Sched()
            rmsnorm_fm(S, C, xT, C["hT"], sm[:, 0:8], T, "n1")
            ffn_fm(S, C, xT, C["hT"], w_gu, w_down, T, "f2")
            sqb, rstd, ones_bf = C["sqb"], C["rstd"], C["ones_bf"]
            for tt in range(4):
                sl = slice(tt * 512, (tt + 1) * 512)
                bank = tt % 2
                for c in range(8):
                    S.act(lambda h, c=c, sl=sl: h.activation(out=sqb[:, c, :], in_=xT[:, c, sl], func=AF.Square),
                          r=[("xT", c, tt)], w=[("sqb", c)])
                for c in range(8):
                    S.pe(lambda h, c=c, bank=bank: h.matmul(ps[:, bank, :], lhsT=ones_bf[:, :], rhs=sqb[:, c, :],
                                                             start=(c == 0), stop=(c == 7)), r=[("sqb", c), "ones_bf"], w=[("ps", bank)])
                S.act(lambda h, bank=bank: h.activation(out=rstd[:, bank, :], in_=ps[:, bank, :], func=AF.Ln, scale=1.0 / D,
                                                         bias=C["eps_col"][:, 0:1]), r=[("ps", bank), "eps_col"], w=[("rstd", bank)])
                S.act(lambda h, bank=bank: h.activation(out=rstd[:, bank, :], in_=rstd[:, bank, :], func=AF.Exp, scale=-0.5),
                      r=[("rstd", bank)], w=[("rstd", bank)])
                for c in range(8):
                    S.dve(lambda h, c=c, sl=sl, bank=bank: h.scalar_tensor_tensor(
                        out=xT[:, c, sl], in0=xT[:, c, sl], scalar=sm[:, 8 + c:9 + c], in1=rstd[:, bank, :], op0=ALU.mult, op1=ALU.mult),
                        r=[("xT", c, tt), ("rstd", bank), "smalls"], w=[("xT", c, tt)])
            o1 = store_T(S, C, "xT", xT, y_out, T)
            emit_phase(nc, S, ss, final_wait=o1)
        st4.close()


def l4_consts():
    kj = np.arange(128)[:, None]
    qi = np.arange(128)[None, :]
    mp = (kj > qi).astype(np.float32)
    mc = (kj <= qi).astype(np.float32)
    tile4 = lambda m: np.tile(m, (1, 4))
    return tile4(mp), tile4(mc)


def l4_inputs(x3, h3, inp):
    T = 2048
    bf = ml_dtypes.bfloat16
    cores = list(range(NCORES))
    sinks = np.asarray(inp["b_sinks"][0], np.float32)
    sk = np.zeros((128, 8), np.float32)
    for ch in range(8):
        sk[0:64, ch] = sinks[2 * ch]
        sk[64:128, ch] = sinks[2 * ch + 1]
    sm4 = np.concatenate([col8(inp["ffn2_norm"][1]), col8(inp["final_norm"]), col8(inp["b_b_out"][0]), sk, np.zeros((128, 8), np.float32)], axis=1)
    b_in = np.asarray(inp["b_b_in"][0], np.float32)
    bq64 = np.ascontiguousarray(b_in[:1280].reshape(20, 64).T)
    bvb = np.ascontiguousarray(np.tile(b_in[1280:1536][None, :], (128, 1)))
    mp, mc = l4_consts()
    maps = []
    for c in cores:
        own = np.asarray(h3[c])
        if c % 4 == 0:
            halo = np.zeros((D, 128), bf)
            m0 = np.zeros_like(mp)
        else:
            halo = np.asarray(h3[c - 1])[:, T - 128:]
            m0 = mp
        masks = np.ascontiguousarray(np.stack([mp, mc, m0], axis=1)).astype(bf)
        maps.append(dict(xT_in=x3[c], hTh_in=np.ascontiguousarray(np.concatenate([halo, own], axis=1)),
                         smalls=sm4, bq64=bq64, bvb=bvb, masks=masks, w_in=inp["b_w_in"][0], w_out=inp["b_w_out"][0],
                         w_gu=inp["ffn2_w_gu"][1], w_down=inp["ffn2_w_down"][1]))
    return maps


RG = [[0, 1, 2, 3], [4, 5, 6, 7]]


def build_fused(T=2048):
    import contextlib
    nc = bass.Bass("TRN2", target_bir_lowering=False)
    ext = lambda name, shape, dt: nc.dram_tensor(name, shape, dt, kind="ExternalInput").ap()
    x = ext("xT_in", [D, T], F32)
    sm1_d = ext("sm1", [128, 16], F32)
    sm2_d = ext("sm", [128, 32], F32)
    cm_d = ext("cm", [128, 4, 128], F32)
    w_my = ext("w_my", [D, 1028], F32)
    sm3_d = ext("sm3", [128, 28], F32)
    W = {}
    for l in range(2):
        for k in (1, 2):
            W["gu%d%d" % (l, k)] = ext("w_gu_%d_%d" % (l, k), [D, 2 * DFF], F32)
            W["dn%d%d" % (l, k)] = ext("w_down_%d_%d" % (l, k), [DFF, D], F32)
    a_w_out = ext("a_w_out", [D, D], F32)
    A = dict(smalls=ext("sm4", [128, 40], F32), bq64=ext("bq64", [64, 20], F32), bvb=ext("bvb", [128, 256], F32),
             masks=ext("masks", [128, 3, 512], BF16), w_in=ext("b_w_in", [D, 1536], F32), w_out=ext("b_w_out", [D, D], F32),
             w_gu=W["gu12"], w_down=W["dn12"],
             y_out=nc.dram_tensor("yT_out", [D, T], F32, kind="ExternalOutput").ap())
    h1_loc = [nc.dram_tensor("h1_loc%d" % i, [D, 512], BF16) for i in range(4)]
    h1_all = [nc.dram_tensor("h1_all%d" % i, [4 * D, 512], BF16) for i in range(4)]
    og_loc = [nc.dram_tensor("og_loc%d" % i, [D, 512], BF16) for i in range(4)]
    og_all = [nc.dram_tensor("og_all%d" % i, [4 * D, 512], BF16) for i in range(4)]
    h3_loc = nc.dram_tensor("h3_loc", [D, T], BF16)
    halo_loc = nc.dram_tensor("halo_loc", [D, 128], BF16)
    halo_all = nc.dram_tensor("halo_all", [4 * D, 128], BF16)
    A["h3_loc"] = h3_loc.ap()
    A["halo_all"] = halo_all.ap()

    def allgather(S, src, dst, wkey):
        return S.cc(lambda h: h.collective_compute("AllGather", ALU.bypass, replica_groups=RG, ins=[src.ap()], outs=[dst.ap()]),
                    w=[wkey])

    with contextlib.ExitStack() as st:
        ss = SemState(nc, st)
        C = alloc_common(nc, st, T)
        x1_loc = nc.dram_tensor("x1_loc", [D, T], F32)
        with contextlib.ExitStack() as p1:
            C["xT"] = p1.enter_context(sbt(nc, "xT", [128, 8, T], F32))
            xT = C["xT"]
            alloc_ffn(nc, p1, C, T, with_x=False)
            C["smalls"] = p1.enter_context(sbt(nc, "sm1_sb", [128, 16], F32))
            S = Sched()
            init_consts(S, C)
            S.dma(lambda h: h.dma_start(out=C["smalls"][:, :], in_=sm1_d[:, :]), w=["smalls"], q="sp")
            load_xT(S, C, x, T)
            rmsnorm_fm(S, C, xT, C["hT"], C["smalls"][:, 0:8], T, "n1")
            ffn_fm(S, C, xT, C["hT"], W["gu01"], W["dn01"], T, "f1")
            store_T(S, C, "xT", xT, x1_loc.ap(), T)
            rmsnorm_fm(S, C, xT, C["hT"], C["smalls"][:, 8:16], T, "n2")
            for tt in range(4):
                S.dma(lambda h, tt=tt: h.dma_start(out=h1_loc[tt].ap().rearrange("(c p) t -> p c t", p=128),
                                                   in_=C["hT"][:, :, tt * 512:(tt + 1) * 512]),
                      r=[("hT", c, tt) for c in range(8)], q="sp")
            emit_phase(nc, S, ss)
        h1v = [h1_all[i].ap().rearrange("(r c p) t -> p r c t", r=4, c=8, p=128) for i in range(4)]

        def pre2(S):
            for i in range(4):
                allgather(S, h1_loc[i], h1_all[i], ("h1_all", i))

        gdn_phase4(nc, ss, C["ps"],
                  lambda tg: h1v[tg % 4][:, tg // 4, :, :],
                  lambda hd, tg: og_loc[tg % 4].ap()[(tg // 4) * 256 + hd * 128:(tg // 4) * 256 + (hd + 1) * 128, :],
                  w_my, sm2_d, cm_d, 4 * T // 512,
                  pre_fn=pre2, h_keys=lambda tg: [("h1_all", tg % 4)],
                  og_key=lambda tg: ("og_loc", tg % 4),
                  post_fn=lambda S, tg: (S.cc(lambda h, i=tg % 4: h.collective_compute(
                      "AllGather", ALU.bypass, replica_groups=RG, ins=[og_loc[i].ap()], outs=[og_all[i].ap()]),
                      r=[("og_loc", tg % 4)], w=[("og_all_dram", tg % 4)]) if tg >= 12 else None))
        C["xT"] = st.enter_context(sbt(nc, "xT", [128, 8, T], F32))
        xT = C["xT"]
        with contextlib.ExitStack() as p3:
            alloc_ffn(nc, p3, C, T, with_x=False)
            C["smalls"] = p3.enter_context(sbt(nc, "sm3_sb", [128, 28], F32))
            sm3 = C["smalls"]
            hT, aT = C["hT"], C["aT"]
            S = Sched()
            S.dma(lambda h: h.dma_start(out=sm3[:, :], in_=sm3_d[:, :]), w=["smalls"], q="sp")
            for tt in range(4):
                ogv = og_all[tt].ap().rearrange("(r g q p) t -> p r g q t", r=4, g=4, q=2, p=128)
                sl = slice(tt * 512, (tt + 1) * 512)
                for g in range(4):
                    for r_ in range(4):
                        S.dma(lambda h, g=g, ogv=ogv, r_=r_: h.dma_start(out=aT[:, 2 * r_:2 * r_ + 2, g, :], in_=ogv[:, r_, g, :, :]),
                              w=[("aS", r_, g)], q=("sp" if (g + r_) % 2 == 0 else "act"))
                AR = [("aS", r_, g) for r_ in range(4) for g in range(4)]
                HW = [("hT", c, tt) for c in range(8)]
                S.dve(lambda h, sl=sl: h.tensor_scalar(out=hT[:, :, sl], in0=aT[:, 0:8, 0, :], scalar1=sm3[:, 24:25], scalar2=None,
                                                       op0=ALU.mult), r=AR + ["smalls"], w=HW)
                for g in range(1, 4):
                    S.dve(lambda h, sl=sl, g=g: h.scalar_tensor_tensor(out=hT[:, :, sl], in0=aT[:, 0:8, g, :], scalar=sm3[:, 24 + g:25 + g],
                                                                       in1=hT[:, :, sl], op0=ALU.mult, op1=ALU.add),
                          r=AR + ["smalls"] + HW, w=HW)
            load_xT(S, C, x1_loc.ap(), T)
            proj_residual(S, C, a_w_out, hT, T)
            rmsnorm_fm(S, C, xT, hT, sm3[:, 0:8], T, "n1")
            ffn_fm(S, C, xT, hT, W["gu02"], W["dn02"], T, "fa")
            rmsnorm_fm(S, C, xT, hT, sm3[:, 8:16], T, "n2")
            ffn_fm(S, C, xT, hT, W["gu11"], W["dn11"], T, "fb")
            rmsnorm_fm(S, C, xT, hT, sm3[:, 16:24], T, "n3")
            store_T(S, C, "hT", hT, h3_loc.ap(), T)
            S.dma(lambda h: h.dma_start(out=halo_loc.ap().rearrange("(c p) t -> p c t", p=128), in_=hT[:, :, T - 128:T]),
                  r=[("hT", c, 3) for c in range(8)], q="sp")
            emit_phase(nc, S, ss)
        l4_phases(nc, ss, C, A, T, fused=True, pre_fn=lambda S: allgather(S, halo_loc, halo_all, "halo_all"))
    return nc


_NC_CACHE = {}


def _get(name, fn):
    if name not in _NC_CACHE:
        _NC_CACHE[name] = fn()
    return _NC_CACHE[name]


def onehot_cols(idx):
    v = np.zeros((128, 4), np.float32)
    if 0 <= idx < 4:
        v[:, idx] = 1.0
    return v


def kernel(**inp):
    inp = {k: np.asarray(v) for k, v in inp.items()}
    T = 2048
    cores = list(range(NCORES))
    xs = inp["x"].reshape(2 * 8192, D)
    nc = _get("fused", build_fused)
    sm1 = np.concatenate([col8(inp["ffn1_norm"][0]), col8(inp["mix_norm"][0])], axis=1)
    l2 = l2_inputs([None, None], inp)
    l4 = l4_inputs([None] * NCORES, [np.zeros((D, T), ml_dtypes.bfloat16)] * NCORES, inp)
    maps = []
    for c in cores:
        s_ = c % 4
        sm3 = np.concatenate([col8(inp["ffn2_norm"][0]), col8(inp["ffn1_norm"][1]), col8(inp["mix_norm"][1]), onehot_cols(s_)], axis=1)
        sm4 = l4[c]["smalls"].copy()
        sm4[:, 32:36] = onehot_cols(s_ - 1)
        m = dict(xT_in=np.ascontiguousarray(xs[c * T:(c + 1) * T].T), sm1=sm1, sm=l2[c]["sm"], cm=l2[c]["cm"], w_my=l2[c]["w_my"],
                 sm3=np.ascontiguousarray(sm3), a_w_out=inp["a_w_out"][0], sm4=sm4, bq64=l4[c]["bq64"], bvb=l4[c]["bvb"],
                 masks=l4[c]["masks"], b_w_in=inp["b_w_in"][0], b_w_out=inp["b_w_out"][0])
        for l in range(2):
            m["w_gu_%d_1" % l] = inp["ffn1_w_gu"][l]
            m["w_down_%d_1" % l] = inp["ffn1_w_down"][l]
            m["w_gu_%d_2" % l] = inp["ffn2_w_gu"][l]
            m["w_down_%d_2" % l] = inp["ffn2_w_down"][l]
        maps.append(m)
    res = run_bass_kernel_spmd(nc, maps, core_ids=cores).results
    out = np.concatenate([np.asarray(res[c]["yT_out"]).T for c in cores], axis=0)
    return np.ascontiguousarray(out.reshape(2, 8192, D).astype(np.float32))
```

```python
import numpy as np
import ml_dtypes
import concourse.bass as bass
import concourse.mybir as mybir
from concourse.bass_utils import run_bass_kernel_spmd

F32 = mybir.dt.float32
BF16 = mybir.dt.bfloat16
AF = mybir.ActivationFunctionType
ALU = mybir.AluOpType

D = 1024
DFF = 2816
NFC = 22
EPS = 1e-6
NCORES = 8
ENGS = ["pe", "act", "dve", "pool", "sp"]
_UID = [0]


def sbt(nc, name, shape, dt):
    _UID[0] += 1
    return nc.sbuf_tensor("%s_u%d" % (name, _UID[0]), shape, dt)


def psum_bank_of(k):
    if isinstance(k, tuple):
        if k[0] == "ps":
            return k[1]
        if k[0] == "ps4":
            return 4
        if k[0] == "ps5":
            return 5
        if k[0] == "psA":
            return 6 + k[1] // 2
        if k[0] == "pso":
            return 2
        if k[0] == "psd":
            return 3
        return None
    if isinstance(k, str) and k.startswith("ps3_"):
        return 3
    return None


class Sched:
    def __init__(self):
        self.ops = {e: [] for e in ENGS}
        self.lastw = {}
        self.readers = {}
        self.lastx = {}

    def op(self, eng, fn, r=(), w=(), dma=False):
        idx = len(self.ops[eng])
        me = (eng, idx)
        deps = set()
        for k in r:
            lw = self.lastw.get(k)
            if lw is not None:
                deps.add(lw)
        for k in w:
            lw = self.lastw.get(k)
            if lw is not None:
                deps.add(lw)
            for rd in self.readers.get(k, ()):
                deps.add(rd)
        banks = set()
        for k in tuple(r) + tuple(w):
            b = psum_bank_of(k)
            if b is not None:
                banks.add(b)
        for b in banks:
            lx = self.lastx.setdefault(b, {})
            for e2, o2 in lx.items():
                if e2 != eng:
                    deps.add(o2)
            lx[eng] = me
        deps.discard(me)
        if eng == "pe":
            deps = {d for d in deps if d[0] != "pe"}
        self.ops[eng].append(dict(fn=fn, deps=deps, dma=dma, sig=False))
        for k in w:
            self.lastw[k] = me
            self.readers[k] = []
        for k in r:
            self.readers.setdefault(k, []).append(me)
        return me

    def pe(self, fn, r=(), w=()):
        return self.op("pe", fn, r, w)

    def act(self, fn, r=(), w=()):
        return self.op("act", fn, r, w)

    def dve(self, fn, r=(), w=()):
        return self.op("dve", fn, r, w)

    def pool(self, fn, r=(), w=()):
        return self.op("pool", fn, r, w)

    def dma(self, fn, r=(), w=(), q="sp"):
        return self.op(q, fn, r, w, dma=True)

    def cc(self, fn, r=(), w=()):
        me = self.op("pool", fn, r, w, dma=True)
        self.ops["pool"][me[1]]["cc"] = True
        return me


class SemState:
    def __init__(self, nc, stack, ndma=12):
        self.nc = nc
        self.stack = stack
        self.phase = 0
        self.eng_sem = {}
        self.eng_cnt = {}
        self.new_phase()
        self.dma_sem = {}
        self.dma_val = {}
        self.dma_rr = {}
        for q in ("sp", "pool", "act"):
            self.dma_sem[q] = [stack.enter_context(nc.semaphore("d_%s%d" % (q, i))) for i in range(ndma)]
            self.dma_val[q] = [0] * ndma
            self.dma_rr[q] = 0

    def new_phase(self):
        self.phase += 1
        for e in ENGS:
            self.eng_sem[e] = self.stack.enter_context(self.nc.semaphore("s%d_%s" % (self.phase, e)))
            self.eng_cnt[e] = 0

    def new_cc_sem(self):
        self.ncc = getattr(self, "ncc", 0) + 1
        return self.stack.enter_context(self.nc.semaphore("cc%d" % self.ncc))


def emit_phase(nc, sched, ss, final_wait=()):
    ops = sched.ops
    for e in ENGS:
        for op in ops[e]:
            for d in op["deps"]:
                ops[d[0]][d[1]]["sig"] = True
    for e in ENGS:
        for op in ops[e]:
            if op.get("cc"):
                op["prev"] = 0
                op["sem"] = ss.new_cc_sem()
                op["val"] = 1
            elif op["dma"]:
                j = ss.dma_rr[e]
                ss.dma_rr[e] = (j + 1) % len(ss.dma_sem[e])
                op["prev"] = ss.dma_val[e][j]
                ss.dma_val[e][j] += 16
                op["sem"] = ss.dma_sem[e][j]
                op["val"] = ss.dma_val[e][j]
            elif op["sig"]:
                ss.eng_cnt[e] += 1
                op["sem"] = ss.eng_sem[e]
                op["val"] = ss.eng_cnt[e]
    fw = [op for e in ENGS for op in ops[e] if op["dma"]]

    def run(e, h):
        waited = {}
        for op in ops[e]:
            need = {}
            for d in op["deps"]:
                dop = ops[d[0]][d[1]]
                k = id(dop["sem"])
                if need.get(k, (None, 0))[1] < dop["val"]:
                    need[k] = (dop["sem"], dop["val"])
            if op["dma"] and op["prev"] > 0:
                k = id(op["sem"])
                if need.get(k, (None, 0))[1] < op["prev"]:
                    need[k] = (op["sem"], op["prev"])
            for k, (s, v) in need.items():
                if waited.get(k, 0) < v:
                    h.wait_ge(s, v)
                    waited[k] = v
            inst = op["fn"](h)
            if op.get("cc"):
                inst.then_inc(op["sem"], 1)
            elif op["dma"]:
                inst.then_inc(op["sem"], 16)
            elif op["sig"]:
                inst.then_inc(op["sem"], 1)
        if e == "sp":
            for dop in fw:
                h.wait_ge(dop["sem"], dop["val"])

    with nc.Block() as block:
        @block.tensor
        def _(h):
            run("pe", h)

        @block.scalar
        def _(h):
            run("act", h)

        @block.vector
        def _(h):
            run("dve", h)

        @block.gpsimd
        def _(h):
            run("pool", h)

        @block.sync
        def _(h):
            run("sp", h)
    ss.new_phase()


def rmsnorm_fm(S, C, xT, hT, nwcol, T, tag):
    ps, ones_bf, sqb, rstd = C["ps"], C["ones_bf"], C["sqb"], C["rstd"]
    for tt in range(T // 512):
        sl = slice(tt * 512, (tt + 1) * 512)
        for c in range(8):
            S.act(lambda h, c=c, sl=sl: h.activation(out=sqb[:, c, :], in_=xT[:, c, sl], func=AF.Square),
                  r=[("xT", c, tt)], w=[("sqb", c)])
        bank = tt % 2
        for c in range(8):
            S.pe(lambda h, c=c, bank=bank: h.matmul(ps[:, bank, :], lhsT=ones_bf[:, :], rhs=sqb[:, c, :],
                                                     start=(c == 0), stop=(c == 7)),
                 r=[("sqb", c), "ones_bf"], w=[("ps", bank)])
        S.act(lambda h, bank=bank: h.activation(out=rstd[:, bank, :], in_=ps[:, bank, :], func=AF.Ln,
                                                 scale=1.0 / D, bias=C["eps_col"][:, 0:1]),
              r=[("ps", bank), "eps_col"], w=[("rstd", bank)])
        S.act(lambda h, bank=bank: h.activation(out=rstd[:, bank, :], in_=rstd[:, bank, :], func=AF.Exp, scale=-0.5),
              r=[("rstd", bank)], w=[("rstd", bank)])
        for c in range(8):
            S.dve(lambda h, c=c, sl=sl, bank=bank: h.scalar_tensor_tensor(
                out=hT[:, c, sl], in0=xT[:, c, sl], scalar=nwcol[:, c:c + 1], in1=rstd[:, bank, :],
                op0=ALU.mult, op1=ALU.mult),
                r=[("xT", c, tt), ("rstd", bank), "smalls"], w=[("hT", c, tt)])


def ffn_fm(S, C, xT, hT, w_gu, w_down, T, tag):
    ps, aT, sg, wg, wd = C["ps"], C["aT"], C["sg"], C["wg"], C["wd"]
    NT = T // 512
    assert NT == 4
    wgu_v = w_gu.rearrange("(c p) n -> p c n", p=128)
    wd_v = w_down.rearrange("(fc p) d -> p fc d", p=128)
    gi = 0
    di = 0
    for half in range(2):
        for fp in range(6):
            nf = 2 if fp < 5 else 1
            f0 = half * 11 + fp * 2
            b = gi % 2
            gi += 1
            S.dma(lambda h, b=b, f0=f0, nf=nf: h.dma_start(out=wg[:, b, :, 0:128 * nf],
                                                            in_=wgu_v[:, :, f0 * 128:(f0 + nf) * 128]),
                  w=[("wg", b, 0)], q="pool")
            S.dma(lambda h, b=b, f0=f0, nf=nf: h.dma_start(out=wg[:, b, :, 256:256 + 128 * nf],
                                                            in_=wgu_v[:, :, DFF + f0 * 128:DFF + (f0 + nf) * 128]),
                  w=[("wg", b, 1)], q="pool")
            for k in range(nf):
                fi = fp * 2 + k
                sb = fi % 2
                for c in range(8):
                    for tt in range(4):
                        S.pe(lambda h, b=b, k=k, c=c, tt=tt: h.matmul(
                            ps[:, tt, :], lhsT=wg[:, b, c, k * 128:(k + 1) * 128], rhs=hT[:, c, tt * 512:(tt + 1) * 512],
                            start=(c == 0), stop=(c == 7)),
                            r=[("wg", b, 0), ("hT", c, tt)], w=[("ps", tt)])
                S.act(lambda h, sb=sb: h.activation(out=sg[:, sb, :, :], in_=ps[:, 0:4, :], func=AF.Silu),
                      r=[("ps", 0), ("ps", 1), ("ps", 2), ("ps", 3)], w=[("sg", sb)])
                for c in range(8):
                    for tt in range(4):
                        S.pe(lambda h, b=b, k=k, c=c, tt=tt: h.matmul(
                            ps[:, 4 + tt, :], lhsT=wg[:, b, c, 256 + k * 128:256 + (k + 1) * 128],
                            rhs=hT[:, c, tt * 512:(tt + 1) * 512], start=(c == 0), stop=(c == 7)),
                            r=[("wg", b, 1), ("hT", c, tt)], w=[("ps", 4 + tt)])
                S.dve(lambda h, sb=sb, fi=fi: h.tensor_tensor(out=aT[:, fi, :, :], in0=ps[:, 4:8, :], in1=sg[:, sb, :, :],
                                                              op=ALU.mult),
                      r=[("ps", 4), ("ps", 5), ("ps", 6), ("ps", 7), ("sg", sb)], w=[("aT", fi)])
        for dp in range(4):
            b = di % 2
            di += 1
            S.dma(lambda h, b=b, dp=dp, half=half: h.dma_start(out=wd[:, b, :, :],
                                                                in_=wd_v[:, half * 11:(half + 1) * 11, dp * 256:(dp + 1) * 256]),
                  w=[("wd", b)], q="pool")
            for k in range(2):
                dc = dp * 2 + k
                bs = 4 * (dc % 2)
                for fi in range(11):
                    for tt in range(4):
                        S.pe(lambda h, b=b, k=k, fi=fi, tt=tt, bs=bs: h.matmul(
                            ps[:, bs + tt, :], lhsT=wd[:, b, fi, k * 128:(k + 1) * 128], rhs=aT[:, fi, tt, :],
                            start=(fi == 0), stop=(fi == 10)),
                            r=[("wd", b), ("aT", fi)], w=[("ps", bs + tt)])
                S.dve(lambda h, dc=dc, bs=bs: h.scalar_tensor_tensor(
                    out=xT[:, dc, :], in0=ps[:, bs:bs + 4, :].rearrange("p a b -> p (a b)"), scalar=0.5, in1=xT[:, dc, :],
                    op0=ALU.mult, op1=ALU.add),
                    r=[("ps", bs), ("ps", bs + 1), ("ps", bs + 2), ("ps", bs + 3)] + [("xT", dc, t) for t in range(4)],
                    w=[("xT", dc, t) for t in range(4)])


def alloc_common(nc, st, T):
    C = {}
    C["ps"] = st.enter_context(nc.psum_tensor("ps", [128, 8, 512], F32))
    C["ones_bf"] = st.enter_context(sbt(nc, "ones_bf", [128, 128], BF16))
    C["eps_col"] = st.enter_context(sbt(nc, "eps_col", [128, 1], F32))
    return C


def alloc_ffn(nc, st, C, T, with_x=True):
    if with_x:
        C["xT"] = st.enter_context(sbt(nc, "xT", [128, 8, T], F32))
    C["hT"] = st.enter_context(sbt(nc, "hT", [128, 8, T], BF16))
    C["aT"] = st.enter_context(sbt(nc, "aT", [128, 11, T // 512, 512], BF16))
    C["sg"] = st.enter_context(sbt(nc, "sg", [128, 2, T // 512, 512], BF16))
    C["sqb"] = st.enter_context(sbt(nc, "sqb", [128, 8, 512], BF16))
    C["rstd"] = st.enter_context(sbt(nc, "rstd", [128, 2, 512], F32))
    C["wg"] = st.enter_context(sbt(nc, "wg", [128, 2, 8, 512], BF16))
    C["wd"] = st.enter_context(sbt(nc, "wd", [128, 2, 11, 256], BF16))


def init_consts(S, C):
    S.pool(lambda h: h.memset(C["ones_bf"][:, :], 1.0), w=["ones_bf"])
    S.pool(lambda h: h.memset(C["eps_col"][:, :], EPS), w=["eps_col"])


def load_xT(S, C, x_dram, T):
    xv = x_dram.rearrange("(c p) t -> p c t", p=128)
    for tt in range(T // 512):
        S.dma(lambda h, tt=tt: h.dma_start(out=C["xT"][:, :, tt * 512:(tt + 1) * 512], in_=xv[:, :, tt * 512:(tt + 1) * 512]),
              w=[("xT", c, tt) for c in range(8)], q=("sp" if tt % 2 == 0 else "act"))


def store_T(S, C, key, sb, out_dram, T):
    ov = out_dram.rearrange("(c p) t -> p c t", p=128)
    outs = []
    for tt in range(T // 512):
        outs.append(S.dma(lambda h, tt=tt: h.dma_start(out=ov[:, :, tt * 512:(tt + 1) * 512], in_=sb[:, :, tt * 512:(tt + 1) * 512]),
                          r=[(key, c, tt) for c in range(8)], q="sp"))
    return outs


def build_L1(T=2048):
    import contextlib
    nc = bass.Bass("TRN2", target_bir_lowering=False)
    x = nc.dram_tensor("xT_in", [D, T], F32, kind="ExternalInput").ap()
    smalls = nc.dram_tensor("smalls", [128, 16], F32, kind="ExternalInput").ap()
    w_gu = nc.dram_tensor("w_gu", [D, 2 * DFF], F32, kind="ExternalInput").ap()
    w_down = nc.dram_tensor("w_down", [DFF, D], F32, kind="ExternalInput").ap()
    x_out = nc.dram_tensor("xT_out", [D, T], F32, kind="ExternalOutput").ap()
    h_out = nc.dram_tensor("hT_out", [D, T], BF16, kind="ExternalOutput").ap()
    with contextlib.ExitStack() as st:
        ss = SemState(nc, st)
        C = alloc_common(nc, st, T)
        alloc_ffn(nc, st, C, T)
        C["smalls"] = st.enter_context(sbt(nc, "smalls_sb", [128, 16], F32))
        S = Sched()
        init_consts(S, C)
        S.dma(lambda h: h.dma_start(out=C["smalls"][:, :], in_=smalls[:, :]), w=["smalls"], q="sp")
        load_xT(S, C, x, T)
        rmsnorm_fm(S, C, C["xT"], C["hT"], C["smalls"][:, 0:8], T, "n1")
        ffn_fm(S, C, C["xT"], C["hT"], w_gu, w_down, T, "f1")
        rmsnorm_fm(S, C, C["xT"], C["hT"], C["smalls"][:, 8:16], T, "n2")
        o1 = store_T(S, C, "xT", C["xT"], x_out, T)
        o2 = store_T(S, C, "hT", C["hT"], h_out, T)
        emit_phase(nc, S, ss, final_wait=o1 + o2)
    return nc


def col8(v):
    return np.ascontiguousarray(np.asarray(v, np.float32).reshape(8, 128).T)


def build_L2(TT=8192, cut=0):
    import contextlib
    nc = bass.Bass("TRN2", target_bir_lowering=False)
    hT_d = nc.dram_tensor("hT", [D, TT], BF16, kind="ExternalInput").ap()
    w_my = nc.dram_tensor("w_my", [D, 1028], F32, kind="ExternalInput").ap()
    sm_d = nc.dram_tensor("sm", [128, 32], F32, kind="ExternalInput").ap()
    cm_d = nc.dram_tensor("cm", [128, 4, 128], F32, kind="ExternalInput").ap()
    og_d = nc.dram_tensor("ogT", [256, TT], BF16, kind="ExternalOutput").ap()
    hv = hT_d.rearrange("(c p) t -> p c t", p=128)
    with contextlib.ExitStack() as st:
        ss = SemState(nc, st)
        ps = st.enter_context(nc.psum_tensor("ps", [128, 8, 512], F32))
        gdn_phase4(nc, ss, ps, lambda tg: hv[:, :, tg * 512:(tg + 1) * 512],
                  lambda hd, tg: og_d[hd * 128:(hd + 1) * 128, tg * 512:(tg + 1) * 512],
                  w_my, sm_d, cm_d, TT // 512, cut=cut)
    return nc


def gdn_phase(nc, ss, ps, h_src, og_dst, w_my, sm_d, cm_d, NG, cut=0, pre_fn=None, h_keys=()):
    import contextlib
    with contextlib.ExitStack() as st:
        sb = lambda name, shape, dt: st.enter_context(sbt(nc, name, shape, dt))
        ones_bf = sb("ones_bf", [128, 128], BF16)
        ones_f = sb("ones_f", [128, 128], F32)
        eps_col = sb("eps_col", [128, 1], F32)
        one_col = sb("one_col", [128, 1], F32)
        sm = sb("sm_sb", [128, 32], F32)
        cm = sb("cm_sb", [128, 4, 128], F32)
        negA = sb("negA", [128, 2], F32)
        wq = sb("wq", [128, 8, 1028], BF16)
        hTg = sb("hTg", [128, 2, 8, 512], BF16)
        pre = sb("pre", [128, 2, 3, 515], F32)
        cs = sb("cs", [128, 3, 512], F32)
        sqb = sb("sqb", [128, 512], BF16)
        rr = sb("rr", [128, 512], F32)
        qkn = sb("qkn", [128, 2, 512], F32)
        qkb = sb("qkb", [128, 2, 512], BF16)
        zT = sb("zT", [128, 512], BF16)
        bat = sb("bat", [128, 4, 4], F32)
        beta = sb("beta", [128, 4, 2], F32)
        esp = sb("esp", [128, 4, 2], F32)
        gg = sb("gg", [128, 4, 2], F32)
        gcs = sb("gcs", [128, 4, 2], F32)
        Rm = sb("Rm", [128, 4, 128], F32)
        tdm = sb("tdm", [128, 4, 128], F32)
        decI = sb("decI", [128, 4, 128], F32)
        egcB = sb("egcB", [128, 4, 128], F32)
        L0 = sb("L0", [128, 4, 128], F32)
        Am = sb("Am", [128, 4, 128], F32)
        X = sb("X", [128, 4, 2, 3, 128], F32)
        glr = sb("glr", [128, 4], F32)
        gl = sb("gl", [128, 4], F32)
        ekd = sb("ekd", [128, 4], F32)
        egc = sb("egc", [128, 4], F32)
        bg = sb("bg", [128, 4], F32)
        Ttb = sb("Ttb", [128, 4, 128], BF16)
        ATb = sb("ATb", [128, 4, 128], BF16)
        qdT = sb("qdT", [128, 4, 128], BF16)
        kd = sb("kd", [128, 4, 128], BF16)
        kbg = sb("kbg", [128, 4, 128], BF16)
        vb = sb("vb", [128, 4, 128], BF16)
        uu = sb("uu", [128, 4, 128], F32)
        wTb = sb("wTb", [128, 4, 128], BF16)
        vnb = sb("vnb", [128, 128], BF16)
        Sst = sb("Sst", [128, 2, 128], F32)
        Sb = sb("Sb", [128, 2, 128], BF16)
        oT = sb("oT", [128, 512], F32)
        on = sb("on", [128, 512], F32)
        sz = sb("sz", [128, 512], F32)
        og = sb("og", [128, 2, 512], BF16)

        IDN, TRIU, NMI, STR = 0, 1, 2, 3
        S = Sched()
        if pre_fn is not None:
            pre_fn(S)
        S.pool(lambda h: h.memset(ones_bf[:, :], 1.0), w=["ones_bf"])
        S.pool(lambda h: h.memset(ones_f[:, :], 1.0), w=["ones_f"])
        S.pool(lambda h: h.memset(eps_col[:, :], EPS), w=["eps_col"])
        S.pool(lambda h: h.memset(one_col[:, :], 1.0), w=["one_col"])
        S.pool(lambda h: h.memset(pre[:, :, :, :], 0.0), w=[("pre", a, b) for a in range(2) for b in range(3)])
        S.pool(lambda h: h.memset(Sst[:, :, :], 0.0), w=[("S", 0), ("S", 1)])
        S.pool(lambda h: h.memset(Sb[:, :, :], 0.0), w=[("Sb", 0), ("Sb", 1)])
        S.dma(lambda h: h.dma_start(out=sm[:, :], in_=sm_d[:, :]), w=["sm"])
        S.dma(lambda h: h.dma_start(out=cm[:, :, :], in_=cm_d[:, :, :]), w=["cm"])
        wv = w_my.rearrange("(c p) n -> p c n", p=128)
        for c in range(8):
            S.dma(lambda h, c=c: h.dma_start(out=wq[:, c, :], in_=wv[:, c, :]), w=[("wq", c)], q="pool")
        S.act(lambda h: h.activation(out=negA[:, :], in_=sm[:, 24:26], func=AF.Exp), r=["sm"], w=["negA"])
        S.dve(lambda h: h.tensor_scalar(out=negA[:, :], in0=negA[:, :], scalar1=-1.0, scalar2=None, op0=ALU.mult),
              r=["negA"], w=["negA"])
        WQ = [("wq", c) for c in range(8)]
        outs = []
        for tg in range(NG):
            b = tg % 2
            S.dma(lambda h, b=b, tg=tg: h.dma_start(out=hTg[:, b, :, :], in_=h_src(tg)),
                  r=list(h_keys), w=[("hTg", b)])
            for t in range(4):
                for c in range(8):
                    S.pe(lambda h, b=b, t=t, c=c: h.matmul(ps[:, 3, t * 4:(t + 1) * 4], lhsT=hTg[:, b, c, t * 128:(t + 1) * 128],
                                                            rhs=wq[:, c, 1024:1028], start=(c == 0), stop=(c == 7)),
                         r=[("hTg", b)] + WQ, w=["ps3_ba"])
            S.act(lambda h: h.copy(out=bat[:, :, :], in_=ps[:, 3, 0:16].rearrange("p (t f) -> p t f", f=4)),
                  r=["ps3_ba"], w=["bat"])
            S.act(lambda h: h.activation(out=beta[:, :, :], in_=bat[:, :, 0:2], func=AF.Sigmoid), r=["bat"], w=["beta"])
            for hd in range(2):
                S.act(lambda h, hd=hd: h.activation(out=esp[:, :, hd:hd + 1], in_=bat[:, :, 2 + hd:3 + hd], func=AF.Exp,
                                                    bias=sm[:, 26 + hd:27 + hd]), r=["bat", "sm"], w=[("esp", hd)])
                S.act(lambda h, hd=hd: h.activation(out=esp[:, :, hd:hd + 1], in_=esp[:, :, hd:hd + 1], func=AF.Ln,
                                                    bias=one_col[:, 0:1]), r=[("esp", hd), "one_col"], w=[("esp", hd)])
                S.dve(lambda h, hd=hd: h.tensor_scalar(out=gg[:, :, hd:hd + 1], in0=esp[:, :, hd:hd + 1],
                                                       scalar1=negA[:, hd:hd + 1], scalar2=None, op0=ALU.mult),
                      r=[("esp", hd), "negA"], w=[("gg", hd)])
            for t in range(4 if cut != 1 else 0):
                S.pe(lambda h, t=t: h.matmul(ps[:, 3, 32 + 2 * t:34 + 2 * t], lhsT=cm[:, TRIU, :], rhs=gg[:, t, :],
                                             start=True, stop=True),
                     r=["cm", ("gg", 0), ("gg", 1)], w=["ps3_gc"])
            S.act(lambda h: h.copy(out=gcs[:, :, :], in_=ps[:, 3, 32:40].rearrange("p (t f) -> p t f", f=2)),
                  r=["ps3_gc"], w=["gcs"])

            for hd in range(2):
                for j in range(4):
                    bank = j % 2
                    for c in range(8):
                        S.pe(lambda h, b=b, hd=hd, j=j, c=c, bank=bank: h.matmul(
                            ps[:, bank, :], lhsT=wq[:, c, hd * 512 + j * 128:hd * 512 + (j + 1) * 128], rhs=hTg[:, b, c, :],
                            start=(c == 0), stop=(c == 7)), r=[("hTg", b)] + WQ, w=[("ps", bank)])
                    if j < 3:
                        S.act(lambda h, hd=hd, j=j, bank=bank: h.copy(out=pre[:, hd, j, 3:515], in_=ps[:, bank, :]),
                              r=[("ps", bank)], w=[("pre", hd, j)])
                    else:
                        S.act(lambda h, bank=bank: h.copy(out=zT[:, :], in_=ps[:, bank, :]), r=[("ps", bank)], w=["zT"])
                for j in range(3):
                    cw = lambda k, hd=hd, j=j: sm[:, (hd * 3 + j) * 4 + k:(hd * 3 + j) * 4 + k + 1]
                    S.dve(lambda h, hd=hd, j=j, cw=cw: h.tensor_scalar(out=cs[:, j, :], in0=pre[:, hd, j, 3:515],
                                                                      scalar1=cw(3), scalar2=None, op0=ALU.mult),
                          r=[("pre", hd, j), "sm"], w=[("cs", j)])
                    for k in (2, 1, 0):
                        S.dve(lambda h, hd=hd, j=j, k=k, cw=cw: h.scalar_tensor_tensor(
                            out=cs[:, j, :], in0=pre[:, hd, j, k:k + 512], scalar=cw(k), in1=cs[:, j, :],
                            op0=ALU.mult, op1=ALU.add), r=[("pre", hd, j), "sm", ("cs", j)], w=[("cs", j)])
                    S.dve(lambda h, hd=hd, j=j: h.tensor_copy(out=pre[:, hd, j, 0:3], in_=pre[:, hd, j, 512:515]),
                          r=[("pre", hd, j)], w=[("pre", hd, j)])
                    S.act(lambda h, j=j: h.activation(out=cs[:, j, :], in_=cs[:, j, :], func=AF.Silu),
                          r=[("cs", j)], w=[("cs", j)])
                for j in range(2):
                    S.act(lambda h, j=j: h.activation(out=sqb[:, :], in_=cs[:, j, :], func=AF.Square), r=[("cs", j)], w=["sqb"])
                    S.pe(lambda h: h.matmul(ps[:, 2, :], lhsT=ones_bf[:, :], rhs=sqb[:, :], start=True, stop=True),
                         r=["sqb", "ones_bf"], w=[("ps", 2)])
                    S.act(lambda h: h.activation(out=rr[:, :], in_=ps[:, 2, :], func=AF.Sqrt, bias=eps_col[:, 0:1]),
                          r=[("ps", 2), "eps_col"], w=["rr"])
                    S.dve(lambda h: h.reciprocal(out=rr[:, :], in_=rr[:, :]), r=["rr"], w=["rr"])
                    sc = (128.0 ** -0.5) if j == 0 else 1.0
                    S.dve(lambda h, j=j, sc=sc: h.scalar_tensor_tensor(out=qkn[:, j, :], in0=cs[:, j, :], scalar=sc, in1=rr[:, :],
                                                                       op0=ALU.mult, op1=ALU.mult),
                          r=[("cs", j), "rr"], w=[("qkn", j)])
                    S.act(lambda h, j=j: h.copy(out=qkb[:, j, :], in_=qkn[:, j, :]), r=[("qkn", j)], w=[("qkb", j)])
                LV = {0: 9, 1: 0, 2: 1, 3: 2, 4: 3, 5: 4}[cut]
                CH = range(4)
                tsl = lambda ci: slice(ci * 128, (ci + 1) * 128)
                for ci in (CH if LV >= 2 else ()):
                    S.dve(lambda h, ci=ci, hd=hd: h.tensor_scalar(out=Rm[:, ci, :], in0=cm[:, TRIU, :],
                                                                  scalar1=gg[:, ci, hd:hd + 1], scalar2=None, op0=ALU.mult),
                          r=["cm", ("gg", hd)], w=[("Rm", ci)])
                    S.pe(lambda h, ci=ci: h.matmul(ps[:, 4, tsl(ci)], lhsT=ones_f[:, :], rhs=Rm[:, ci, :], start=True, stop=True),
                         r=[("Rm", ci), "ones_f"], w=[("ps4", ci)])
                for ci in (CH if LV >= 2 else ()):
                    S.dve(lambda h, ci=ci, hd=hd: h.scalar_tensor_tensor(
                        out=tdm[:, ci, :], in0=ps[:, 4, tsl(ci)], scalar=gcs[:, ci, hd:hd + 1], in1=cm[:, NMI, :],
                        op0=ALU.subtract, op1=ALU.max), r=[("ps4", ci), "gcs", "cm"], w=[("tdm", ci)])
                    S.act(lambda h, ci=ci: h.activation(out=decI[:, ci, :], in_=tdm[:, ci, :], func=AF.Exp, scale=-1.0),
                          r=[("tdm", ci)], w=[("decI", ci)])
                    S.act(lambda h, ci=ci: h.activation(out=egcB[:, ci, :], in_=ps[:, 4, tsl(ci)], func=AF.Exp),
                          r=[("ps4", ci)], w=[("egcB", ci)])
                    S.act(lambda h, ci=ci: h.copy(out=glr[:, ci:ci + 1], in_=ps[:, 4, ci * 128 + 127:ci * 128 + 128]),
                          r=[("ps4", ci)], w=[("glr", ci)])
                    S.act(lambda h, ci=ci: h.activation(out=gl[:, ci:ci + 1], in_=glr[:, ci:ci + 1], func=AF.Exp),
                          r=[("glr", ci)], w=[("gl", ci)])
                    S.act(lambda h, ci=ci, hd=hd: h.activation(out=ekd[:, ci:ci + 1], in_=gcs[:, ci, hd:hd + 1], func=AF.Exp,
                                                               scale=-1.0, bias=glr[:, ci:ci + 1]),
                          r=["gcs", ("glr", ci)], w=[("ekd", ci)])
                    S.act(lambda h, ci=ci, hd=hd: h.activation(out=egc[:, ci:ci + 1], in_=gcs[:, ci, hd:hd + 1], func=AF.Exp),
                          r=["gcs"], w=[("egc", ci)])
                    S.dve(lambda h, ci=ci, hd=hd: h.tensor_tensor(out=bg[:, ci:ci + 1], in0=egc[:, ci:ci + 1],
                                                                  in1=beta[:, ci, hd:hd + 1], op=ALU.mult),
                          r=[("egc", ci), "beta"], w=[("bg", ci)])
                    S.dve(lambda h, ci=ci: h.tensor_tensor(out=qdT[:, ci, :], in0=qkn[:, 0, tsl(ci)], in1=egcB[:, ci, :], op=ALU.mult),
                          r=[("qkn", 0), ("egcB", ci)], w=[("qdT", ci)])
                for ci in (CH if LV >= 3 else ()):
                    S.pe(lambda h, ci=ci: h.matmul(ps[:, 5, tsl(ci)], lhsT=qkb[:, 1, tsl(ci)], rhs=qkb[:, 1, tsl(ci)],
                                                   start=True, stop=True), r=[("qkb", 1)], w=[("ps5", ci)])
                    S.dve(lambda h, ci=ci, hd=hd: h.scalar_tensor_tensor(
                        out=L0[:, ci, :], in0=ps[:, 5, tsl(ci)], scalar=beta[:, ci, hd:hd + 1], in1=decI[:, ci, :],
                        op0=ALU.mult, op1=ALU.mult), r=[("ps5", ci), "beta", ("decI", ci)], w=[("L0", ci)])
                    S.dve(lambda h, ci=ci: h.scalar_tensor_tensor(
                        out=X[:, ci, 0, 0, :], in0=L0[:, ci, :], scalar=-1.0, in1=cm[:, STR, :], op0=ALU.mult, op1=ALU.mult),
                        r=[("L0", ci), "cm"], w=[("XP", ci, 0)])
                    S.pe(lambda h, ci=ci: h.matmul(ps[:, 5, tsl(ci)], lhsT=qkb[:, 0, tsl(ci)], rhs=qkb[:, 1, tsl(ci)],
                                                   start=True, stop=True), r=[("qkb", 0), ("qkb", 1)], w=[("ps5", ci)])
                    S.dve(lambda h, ci=ci: h.tensor_tensor(out=Am[:, ci, :], in0=ps[:, 5, tsl(ci)], in1=decI[:, ci, :], op=ALU.mult),
                          r=[("ps5", ci), ("decI", ci)], w=[("Am", ci)])
                for ci in (CH if LV >= 3 else ()):
                    S.pe(lambda h, ci=ci: h.matmul(ps[:, 5, tsl(ci)], lhsT=X[:, ci, 0, 0, :], rhs=cm[:, IDN, :], start=True, stop=True),
                         r=[("XP", ci, 0), "cm"], w=[("ps5", ci)])
                    S.act(lambda h, ci=ci: h.copy(out=X[:, ci, 0, 1, :], in_=ps[:, 5, tsl(ci)]), r=[("ps5", ci)], w=[("XQ", ci, 0)])
                    S.dve(lambda h, ci=ci: h.tensor_tensor(out=X[:, ci, 1, 2, :], in0=ps[:, 5, tsl(ci)], in1=cm[:, IDN, :], op=ALU.add),
                          r=[("ps5", ci), "cm"], w=[("XT", ci, 1)])
                    S.pe(lambda h, ci=ci: h.matmul(ps[:, 5, tsl(ci)], lhsT=Am[:, ci, :], rhs=cm[:, IDN, :], start=True, stop=True),
                         r=[("Am", ci), "cm"], w=[("ps5", ci)])
                    S.act(lambda h, ci=ci: h.copy(out=ATb[:, ci, :], in_=ps[:, 5, tsl(ci)]), r=[("ps5", ci)], w=[("ATb", ci)])
                for ci in (CH if LV >= 3 else ()):
                    S.pe(lambda h, ci=ci: h.matmul(ps[:, 5, tsl(ci)], lhsT=qkn[:, 1, tsl(ci)], rhs=cm[:, IDN, :], start=True, stop=True),
                         r=[("qkn", 1), "cm"], w=[("ps5", ci)])
                    S.dve(lambda h, ci=ci: h.tensor_scalar(out=kd[:, ci, :], in0=ps[:, 5, tsl(ci)], scalar1=ekd[:, ci:ci + 1],
                                                           scalar2=None, op0=ALU.mult), r=[("ps5", ci), ("ekd", ci)], w=[("kd", ci)])
                    S.dve(lambda h, ci=ci: h.tensor_scalar(out=kbg[:, ci, :], in0=ps[:, 5, tsl(ci)], scalar1=bg[:, ci:ci + 1],
                                                           scalar2=None, op0=ALU.mult), r=[("ps5", ci), ("bg", ci)], w=[("kbg", ci)])
                    S.pe(lambda h, ci=ci: h.matmul(ps[:, 5, tsl(ci)], lhsT=cs[:, 2, tsl(ci)], rhs=cm[:, IDN, :], start=True, stop=True),
                         r=[("cs", 2), "cm"], w=[("ps5", ci)])
                    S.dve(lambda h, ci=ci, hd=hd: h.tensor_scalar(out=vb[:, ci, :], in0=ps[:, 5, tsl(ci)], scalar1=beta[:, ci, hd:hd + 1],
                                                                  scalar2=None, op0=ALU.mult), r=[("ps5", ci), "beta"], w=[("vb", ci)])
                psA = lambda ci: ps[:, 6 + ci // 2, (ci % 2) * 256:(ci % 2) * 256 + 256]
                for m in (range(7) if LV >= 4 else ()):
                    cur, nxt = m % 2, (m + 1) % 2
                    for ci in CH:
                        if m == 0:
                            S.pe(lambda h, ci=ci: h.matmul(psA(ci)[:, 0:128], lhsT=X[:, ci, 0, 0, :], rhs=X[:, ci, 0, 1, :],
                                                           start=True, stop=True),
                                 r=[("XP", ci, 0), ("XQ", ci, 0)], w=[("psA", ci)])
                        elif m < 6:
                            S.pe(lambda h, ci=ci, cur=cur: h.matmul(psA(ci), lhsT=X[:, ci, cur, 0, :],
                                                                    rhs=X[:, ci, cur, 1:3, :].rearrange("p a b -> p (a b)"),
                                                                    start=True, stop=True),
                                 r=[("XP", ci, cur), ("XQ", ci, cur), ("XT", ci, cur)], w=[("psA", ci)])
                        else:
                            S.pe(lambda h, ci=ci, cur=cur: h.matmul(psA(ci)[:, 128:256], lhsT=X[:, ci, cur, 0, :], rhs=X[:, ci, cur, 2, :],
                                                                    start=True, stop=True),
                                 r=[("XP", ci, cur), ("XT", ci, cur)], w=[("psA", ci)])
                        if m < 6:
                            S.pe(lambda h, ci=ci, cur=cur: h.matmul(ps[:, 5, tsl(ci)], lhsT=X[:, ci, cur, 1, :], rhs=X[:, ci, cur, 0, :],
                                                                    start=True, stop=True),
                                 r=[("XP", ci, cur), ("XQ", ci, cur)], w=[("ps5", ci)])
                    for ci in CH:
                        if m < 6:
                            S.act(lambda h, ci=ci, nxt=nxt: h.copy(out=X[:, ci, nxt, 1, :], in_=psA(ci)[:, 0:128]),
                                  r=[("psA", ci)], w=[("XQ", ci, nxt)])
                            S.act(lambda h, ci=ci, nxt=nxt: h.copy(out=X[:, ci, nxt, 0, :], in_=ps[:, 5, tsl(ci)]),
                                  r=[("ps5", ci)], w=[("XP", ci, nxt)])
                        if 1 <= m < 6:
                            S.dve(lambda h, ci=ci, cur=cur, nxt=nxt: h.tensor_tensor(out=X[:, ci, nxt, 2, :], in0=psA(ci)[:, 128:256],
                                                                                    in1=X[:, ci, cur, 2, :], op=ALU.add),
                                  r=[("psA", ci), ("XT", ci, cur)], w=[("XT", ci, nxt)])
                        if m == 6:
                            S.dve(lambda h, ci=ci, cur=cur: h.tensor_tensor(out=Ttb[:, ci, :], in0=psA(ci)[:, 128:256],
                                                                           in1=X[:, ci, cur, 2, :], op=ALU.add),
                                  r=[("psA", ci), ("XT", ci, cur)], w=[("Ttb", ci)])
                for ci in (CH if LV >= 9 else ()):
                    S.pe(lambda h, ci=ci: h.matmul(ps[:, 5, tsl(ci)], lhsT=Ttb[:, ci, :], rhs=vb[:, ci, :], start=True, stop=True),
                         r=[("Ttb", ci), ("vb", ci)], w=[("ps5", ci)])
                    S.act(lambda h, ci=ci: h.copy(out=uu[:, ci, :], in_=ps[:, 5, tsl(ci)]), r=[("ps5", ci)], w=[("uu", ci)])
                    S.pe(lambda h, ci=ci: h.matmul(psA(ci)[:, 0:128], lhsT=kbg[:, ci, :], rhs=Ttb[:, ci, :], start=True, stop=True),
                         r=[("Ttb", ci), ("kbg", ci)], w=[("psA", ci)])
                    S.act(lambda h, ci=ci: h.copy(out=wTb[:, ci, :], in_=psA(ci)[:, 0:128]), r=[("psA", ci)], w=[("wTb", ci)])
                for ci in (CH if LV >= 9 else ()):
                    S.pe(lambda h, ci=ci, hd=hd: h.matmul(ps[:, 3, 128:256], lhsT=wTb[:, ci, :], rhs=Sb[:, hd, :], start=True, stop=True),
                         r=[("wTb", ci), ("Sb", hd)], w=["ps3_1"])
                    S.dve(lambda h, ci=ci: h.tensor_tensor(out=vnb[:, :], in0=uu[:, ci, :], in1=ps[:, 3, 128:256], op=ALU.subtract),
                          r=[("uu", ci), "ps3_1"], w=["vnb"])
                    S.pe(lambda h, ci=ci, hd=hd: h.matmul(ps[:, 3, 384:512], lhsT=Sb[:, hd, :], rhs=qdT[:, ci, :], start=True, stop=False),
                         r=[("Sb", hd), ("qdT", ci)], w=["ps3_o"])
                    S.pe(lambda h, ci=ci: h.matmul(ps[:, 3, 384:512], lhsT=vnb[:, :], rhs=ATb[:, ci, :], start=False, stop=True),
                         r=["vnb", ("ATb", ci)], w=["ps3_o"])
                    S.act(lambda h, ci=ci: h.copy(out=oT[:, tsl(ci)], in_=ps[:, 3, 384:512]), r=["ps3_o"], w=["oT"])
                    S.pe(lambda h, ci=ci: h.matmul(ps[:, 3, 256:384], lhsT=kd[:, ci, :], rhs=vnb[:, :], start=True, stop=True),
                         r=[("kd", ci), "vnb"], w=["ps3_s"])
                    S.dve(lambda h, ci=ci, hd=hd: h.scalar_tensor_tensor(out=Sst[:, hd, :], in0=Sst[:, hd, :], scalar=gl[:, ci:ci + 1],
                                                                         in1=ps[:, 3, 256:384], op0=ALU.mult, op1=ALU.add),
                          r=[("S", hd), ("gl", ci), "ps3_s"], w=[("S", hd)])
                    S.act(lambda h, hd=hd: h.copy(out=Sb[:, hd, :], in_=Sst[:, hd, :]), r=[("S", hd)], w=[("Sb", hd)])
                S.act(lambda h: h.activation(out=sqb[:, :], in_=oT[:, :], func=AF.Square), r=["oT"], w=["sqb"])
                S.pe(lambda h: h.matmul(ps[:, 2, :], lhsT=ones_bf[:, :], rhs=sqb[:, :], start=True, stop=True),
                     r=["sqb", "ones_bf"], w=[("ps", 2)])
                S.act(lambda h: h.activation(out=rr[:, :], in_=ps[:, 2, :], func=AF.Sqrt, scale=1.0 / 128, bias=eps_col[:, 0:1]),
                      r=[("ps", 2), "eps_col"], w=["rr"])
                S.dve(lambda h: h.reciprocal(out=rr[:, :], in_=rr[:, :]), r=["rr"], w=["rr"])
                S.dve(lambda h: h.scalar_tensor_tensor(out=on[:, :], in0=oT[:, :], scalar=sm[:, 28:29], in1=rr[:, :],
                                                       op0=ALU.mult, op1=ALU.mult), r=["oT", "sm", "rr"], w=["on"])
                S.act(lambda h: h.activation(out=sz[:, :], in_=zT[:, :], func=AF.Silu), r=["zT"], w=["sz"])
                ob = (tg * 2 + hd) % 2
                S.dve(lambda h, ob=ob: h.tensor_tensor(out=og[:, ob, :], in0=on[:, :], in1=sz[:, :], op=ALU.mult),
                      r=["on", "sz"], w=[("og", ob)])
                outs.append(S.dma(lambda h, ob=ob, hd=hd, tg=tg: h.dma_start(out=og_dst(hd, tg),
                                                                            in_=og[:, ob, :]), r=[("og", ob)]))
        emit_phase(nc, S, ss)


def gdn_phase2(nc, ss, ps, h_src, og_dst, w_my, sm_d, cm_d, NG, cut=0, pre_fn=None, h_keys=()):
    import contextlib
    from itertools import zip_longest
    with contextlib.ExitStack() as st:
        sb = lambda name, shape, dt: st.enter_context(sbt(nc, name, shape, dt))
        ones_bf = sb("ones_bf", [128, 128], BF16)
        ones_f = sb("ones_f", [128, 128], F32)
        eps_col = sb("eps_col", [128, 1], F32)
        one_col = sb("one_col", [128, 1], F32)
        sm = sb("sm_sb", [128, 32], F32)
        cm = sb("cm_sb", [128, 4, 128], F32)
        negA = sb("negA", [128, 2], F32)
        wq = sb("wq", [128, 8, 1028], BF16)
        hTg = sb("hTg", [128, 2, 8, 512], BF16)
        pre = sb("pre", [128, 2, 3, 515], F32)
        cs = sb("cs", [128, 2, 3, 512], F32)
        sqb = sb("sqb", [128, 2, 512], BF16)
        rr = sb("rr", [128, 2, 512], F32)
        qkn = sb("qkn", [128, 2, 2, 512], F32)
        qkb = sb("qkb", [128, 2, 2, 512], BF16)
        zT = sb("zT", [128, 2, 512], BF16)
        bat = sb("bat", [128, 4, 4], F32)
        beta = sb("beta", [128, 4, 2], F32)
        esp = sb("esp", [128, 4, 2], F32)
        gg = sb("gg", [128, 4, 2], F32)
        gcs = sb("gcs", [128, 4, 2], F32)
        Rm = sb("Rm", [128, 2, 4, 128], F32)
        tdm = sb("tdm", [128, 2, 4, 128], F32)
        egcB = sb("egcB", [128, 2, 4, 128], F32)
        W4 = sb("W4", [128, 2, 4, 128], F32)
        Am = sb("Am", [128, 2, 4, 128], F32)
        X = sb("X", [128, 2, 2, 4, 3, 128], F32)
        glr = sb("glr", [128, 2, 4], F32)
        gl = sb("gl", [128, 2, 4], F32)
        ekd = sb("ekd", [128, 2, 4], F32)
        egc = sb("egc", [128, 2, 4], F32)
        bg = sb("bg", [128, 2, 4], F32)
        Ttb = sb("Ttb", [128, 2, 4, 128], BF16)
        ATb = sb("ATb", [128, 2, 4, 128], BF16)
        qdT = sb("qdT", [128, 2, 4, 128], BF16)
        kd = sb("kd", [128, 2, 4, 128], BF16)
        kbg = sb("kbg", [128, 2, 4, 128], BF16)
        vb = sb("vb", [128, 2, 4, 128], BF16)
        uu = sb("uu", [128, 2, 4, 128], F32)
        wTb = sb("wTb", [128, 2, 4, 128], BF16)
        vnb = sb("vnb", [128, 2, 128], BF16)
        Sst = sb("Sst", [128, 2, 128], F32)
        Sb = sb("Sb", [128, 2, 128], BF16)
        oT = sb("oT", [128, 2, 512], F32)
        sz = sb("sz", [128, 2, 512], F32)
        og = sb("og", [128, 2, 512], BF16)

        IDN, TRIU, NMI, STR = 0, 1, 2, 3
        B4 = [128, 4, 128]
        mask4 = lambda mi: cm[:, mi, :].unsqueeze(1).to_broadcast(B4)
        PROJ, MISC = 0, 1
        XB = lambda hd: 2 + hd
        YB = lambda hd: 4 + 2 * hd
        psX = lambda hd: ps[:, XB(hd), :].rearrange("p (t c) -> p t c", c=128)
        psY = lambda hd: ps[:, YB(hd):YB(hd) + 2, :].rearrange("p b (t c) -> p (b t) c", c=256)
        KX = lambda hd: [("ps", XB(hd))]
        KY = lambda hd: [("ps", YB(hd)), ("ps", YB(hd) + 1)]
        S = Sched()
        if pre_fn is not None:
            pre_fn(S)
        S.pool(lambda h: h.memset(ones_bf[:, :], 1.0), w=["ones_bf"])
        S.pool(lambda h: h.memset(ones_f[:, :], 1.0), w=["ones_f"])
        S.pool(lambda h: h.memset(eps_col[:, :], EPS), w=["eps_col"])
        S.pool(lambda h: h.memset(one_col[:, :], 1.0), w=["one_col"])
        S.pool(lambda h: h.memset(pre[:, :, :, :], 0.0), w=[("pre", a, b) for a in range(2) for b in range(3)])
        S.pool(lambda h: h.memset(Sst[:, :, :], 0.0), w=[("S", 0), ("S", 1)])
        S.pool(lambda h: h.memset(Sb[:, :, :], 0.0), w=[("Sb", 0), ("Sb", 1)])
        S.dma(lambda h: h.dma_start(out=sm[:, :], in_=sm_d[:, :]), w=["sm"])
        S.dma(lambda h: h.dma_start(out=cm[:, :, :], in_=cm_d[:, :, :]), w=["cm"])
        wv = w_my.rearrange("(c p) n -> p c n", p=128)
        for c in range(8):
            S.dma(lambda h, c=c: h.dma_start(out=wq[:, c, :], in_=wv[:, c, :]), w=[("wq", c)], q="pool")
        S.act(lambda h: h.activation(out=negA[:, :], in_=sm[:, 24:26], func=AF.Exp), r=["sm"], w=["negA"])
        S.dve(lambda h: h.tensor_scalar(out=negA[:, :], in0=negA[:, :], scalar1=-1.0, scalar2=None, op0=ALU.mult),
              r=["negA"], w=["negA"])
        WQ = [("wq", c) for c in range(8)]
        tsl = lambda ci: slice(ci * 128, (ci + 1) * 128)

        def unit(tg, hd, b):
            pb = hd
            bcol = lambda t4: t4[:, :, hd:hd + 1].to_broadcast(B4)
            bvec = lambda v: v[:, hd, :].unsqueeze(2).to_broadcast(B4)
            for j in range(4):
                for c in range(8):
                    S.pe(lambda h, j=j, c=c: h.matmul(ps[:, PROJ, :], lhsT=wq[:, c, hd * 512 + j * 128:hd * 512 + (j + 1) * 128],
                                                      rhs=hTg[:, b, c, :], start=(c == 0), stop=(c == 7)),
                         r=[("hTg", b)] + WQ, w=[("ps", PROJ)])
                if j < 3:
                    S.act(lambda h, j=j: h.copy(out=pre[:, hd, j, 3:515], in_=ps[:, PROJ, :]), r=[("ps", PROJ)], w=[("pre", hd, j)])
                else:
                    S.act(lambda h: h.copy(out=zT[:, hd, :], in_=ps[:, PROJ, :]), r=[("ps", PROJ)], w=[("zT", hd)])
                yield
            for j in range(3):
                cw = lambda k, j=j: sm[:, (hd * 3 + j) * 4 + k:(hd * 3 + j) * 4 + k + 1]
                S.dve(lambda h, j=j, cw=cw: h.tensor_scalar(out=cs[:, hd, j, :], in0=pre[:, hd, j, 3:515], scalar1=cw(3), scalar2=None,
                                                           op0=ALU.mult), r=[("pre", hd, j), "sm"], w=[("cs", hd, j)])
                for k in (2, 1, 0):
                    S.dve(lambda h, j=j, k=k, cw=cw: h.scalar_tensor_tensor(out=cs[:, hd, j, :], in0=pre[:, hd, j, k:k + 512], scalar=cw(k),
                                                                          in1=cs[:, hd, j, :], op0=ALU.mult, op1=ALU.add),
                          r=[("pre", hd, j), "sm", ("cs", hd, j)], w=[("cs", hd, j)])
                S.dve(lambda h, j=j: h.tensor_copy(out=pre[:, hd, j, 0:3], in_=pre[:, hd, j, 512:515]),
                      r=[("pre", hd, j)], w=[("pre", hd, j)])
                S.act(lambda h, j=j: h.activation(out=cs[:, hd, j, :], in_=cs[:, hd, j, :], func=AF.Silu),
                      r=[("cs", hd, j)], w=[("cs", hd, j)])
                yield
            for j in range(2):
                S.act(lambda h, j=j: h.activation(out=sqb[:, hd, :], in_=cs[:, hd, j, :], func=AF.Square), r=[("cs", hd, j)], w=[("sqb", hd)])
                S.pe(lambda h: h.matmul(ps[:, PROJ, :], lhsT=ones_bf[:, :], rhs=sqb[:, hd, :], start=True, stop=True),
                     r=[("sqb", hd), "ones_bf"], w=[("ps", PROJ)])
                S.act(lambda h: h.activation(out=rr[:, hd, :], in_=ps[:, PROJ, :], func=AF.Sqrt, bias=eps_col[:, 0:1]),
                      r=[("ps", PROJ), "eps_col"], w=[("rr", hd)])
                S.dve(lambda h: h.reciprocal(out=rr[:, hd, :], in_=rr[:, hd, :]), r=[("rr", hd)], w=[("rr", hd)])
                sc = (128.0 ** -0.5) if j == 0 else 1.0
                S.dve(lambda h, j=j, sc=sc: h.scalar_tensor_tensor(out=qkn[:, hd, j, :], in0=cs[:, hd, j, :], scalar=sc, in1=rr[:, hd, :],
                                                                   op0=ALU.mult, op1=ALU.mult),
                      r=[("cs", hd, j), ("rr", hd)], w=[("qkn", hd, j)])
                S.act(lambda h, j=j: h.copy(out=qkb[:, hd, j, :], in_=qkn[:, hd, j, :]), r=[("qkn", hd, j)], w=[("qkb", hd, j)])
                yield
            if cut == 1:
                return
            q4 = qkn[:, hd, 0, :].rearrange("p (t c) -> p t c", c=128)
            S.dve(lambda h: h.tensor_tensor(out=Rm[:, hd, :, :], in0=mask4(TRIU), in1=bcol(gg), op=ALU.mult),
                  r=["cm", ("gg", hd)], w=[("Rm", hd)])
            S.pe(lambda h: h.matmul(ps[:, XB(hd), :], lhsT=ones_f[:, :], rhs=Rm[:, hd, :, :].rearrange("p t c -> p (t c)"),
                                    start=True, stop=True), r=[("Rm", hd), "ones_f"], w=KX(hd))
            yield
            S.dve(lambda h: h.tensor_tensor(out=tdm[:, hd, :, :], in0=psX(hd), in1=bcol(gcs), op=ALU.subtract),
                  r=KX(hd) + ["gcs"], w=[("tdm", hd)])
            S.act(lambda h: h.activation(out=egcB[:, hd, :, :], in_=psX(hd), func=AF.Exp), r=KX(hd), w=[("egcB", hd)])
            S.act(lambda h: h.copy(out=glr[:, hd, :].unsqueeze(2), in_=psX(hd)[:, :, 127:128]), r=KX(hd), w=[("glr", hd)])
            S.dve(lambda h: h.tensor_tensor(out=tdm[:, hd, :, :], in0=tdm[:, hd, :, :], in1=mask4(NMI), op=ALU.max),
                  r=[("tdm", hd), "cm"], w=[("tdm", hd)])
            S.act(lambda h: h.activation(out=tdm[:, hd, :, :], in_=tdm[:, hd, :, :], func=AF.Exp, scale=-1.0),
                  r=[("tdm", hd)], w=[("tdm", hd)])
            S.act(lambda h: h.activation(out=gl[:, hd, :], in_=glr[:, hd, :], func=AF.Exp), r=[("glr", hd)], w=[("gl", hd)])
            S.dve(lambda h: h.tensor_tensor(out=ekd[:, hd, :], in0=glr[:, hd, :], in1=gcs[:, :, hd], op=ALU.subtract),
                  r=[("glr", hd), "gcs"], w=[("ekd", hd)])
            S.act(lambda h: h.activation(out=ekd[:, hd, :], in_=ekd[:, hd, :], func=AF.Exp), r=[("ekd", hd)], w=[("ekd", hd)])
            S.act(lambda h: h.activation(out=egc[:, hd, :], in_=gcs[:, :, hd], func=AF.Exp), r=["gcs"], w=[("egc", hd)])
            S.dve(lambda h: h.tensor_tensor(out=bg[:, hd, :], in0=egc[:, hd, :], in1=beta[:, :, hd], op=ALU.mult),
                  r=[("egc", hd), "beta"], w=[("bg", hd)])
            S.dve(lambda h: h.tensor_tensor(out=qdT[:, hd, :, :], in0=q4, in1=egcB[:, hd, :, :], op=ALU.mult),
                  r=[("qkn", hd, 0), ("egcB", hd)], w=[("qdT", hd)])
            S.dve(lambda h: h.tensor_tensor(out=W4[:, hd, :, :], in0=tdm[:, hd, :, :], in1=mask4(STR), op=ALU.mult),
                  r=[("tdm", hd), "cm"], w=[("W4", hd)])
            S.dve(lambda h: h.scalar_tensor_tensor(out=W4[:, hd, :, :], in0=W4[:, hd, :, :], scalar=-1.0, in1=bcol(beta),
                                                   op0=ALU.mult, op1=ALU.mult), r=[("W4", hd), "beta"], w=[("W4", hd)])
            yield
            for ci in range(4):
                S.pe(lambda h, ci=ci: h.matmul(psX(hd)[:, ci, :], lhsT=qkb[:, hd, 1, tsl(ci)], rhs=qkb[:, hd, 1, tsl(ci)],
                                               start=True, stop=True), r=[("qkb", hd, 1)], w=KX(hd))
            S.dve(lambda h: h.tensor_tensor(out=X[:, hd, 0, :, 0, :], in0=psX(hd), in1=W4[:, hd, :, :], op=ALU.mult),
                  r=KX(hd) + [("W4", hd)], w=[("XP", hd, 0)])
            yield
            for ci in range(4):
                S.pe(lambda h, ci=ci: h.matmul(psX(hd)[:, ci, :], lhsT=qkb[:, hd, 0, tsl(ci)], rhs=qkb[:, hd, 1, tsl(ci)],
                                               start=True, stop=True), r=[("qkb", hd, 0), ("qkb", hd, 1)], w=KX(hd))
            S.dve(lambda h: h.tensor_tensor(out=Am[:, hd, :, :], in0=psX(hd), in1=tdm[:, hd, :, :], op=ALU.mult),
                  r=KX(hd) + [("tdm", hd)], w=[("Am", hd)])
            yield
            for ci in range(4):
                S.pe(lambda h, ci=ci: h.matmul(psX(hd)[:, ci, :], lhsT=X[:, hd, 0, ci, 0, :], rhs=cm[:, IDN, :], start=True, stop=True),
                     r=[("XP", hd, 0), "cm"], w=KX(hd))
            S.act(lambda h: h.copy(out=X[:, hd, 0, :, 1, :], in_=psX(hd)), r=KX(hd), w=[("XQ", hd, 0)])
            S.dve(lambda h: h.tensor_tensor(out=X[:, hd, 1, :, 2, :], in0=psX(hd), in1=mask4(IDN), op=ALU.add),
                  r=KX(hd) + ["cm"], w=[("XT", hd, 1)])
            yield
            for ci in range(4):
                S.pe(lambda h, ci=ci: h.matmul(psX(hd)[:, ci, :], lhsT=Am[:, hd, ci, :], rhs=cm[:, IDN, :], start=True, stop=True),
                     r=[("Am", hd), "cm"], w=KX(hd))
            S.act(lambda h: h.copy(out=ATb[:, hd, :, :], in_=psX(hd)), r=KX(hd), w=[("ATb", hd)])
            yield
            for ci in range(4):
                S.pe(lambda h, ci=ci: h.matmul(psX(hd)[:, ci, :], lhsT=qkn[:, hd, 1, tsl(ci)], rhs=cm[:, IDN, :], start=True, stop=True),
                     r=[("qkn", hd, 1), "cm"], w=KX(hd))
            S.dve(lambda h: h.tensor_tensor(out=kd[:, hd, :, :], in0=psX(hd), in1=bvec(ekd), op=ALU.mult),
                  r=KX(hd) + [("ekd", hd)], w=[("kd", hd)])
            S.dve(lambda h: h.tensor_tensor(out=kbg[:, hd, :, :], in0=psX(hd), in1=bvec(bg), op=ALU.mult),
                  r=KX(hd) + [("bg", hd)], w=[("kbg", hd)])
            yield
            for ci in range(4):
                S.pe(lambda h, ci=ci: h.matmul(psX(hd)[:, ci, :], lhsT=cs[:, hd, 2, tsl(ci)], rhs=cm[:, IDN, :], start=True, stop=True),
                     r=[("cs", hd, 2), "cm"], w=KX(hd))
            S.dve(lambda h: h.tensor_tensor(out=vb[:, hd, :, :], in0=psX(hd), in1=bcol(beta), op=ALU.mult),
                  r=KX(hd) + ["beta"], w=[("vb", hd)])
            yield
            for m in range(7):
                cur, nxt = m % 2, (m + 1) % 2
                for ci in range(4):
                    if m == 0:
                        S.pe(lambda h, ci=ci: h.matmul(psY(hd)[:, ci, 0:128], lhsT=X[:, hd, 0, ci, 0, :], rhs=X[:, hd, 0, ci, 1, :],
                                                       start=True, stop=True), r=[("XP", hd, 0), ("XQ", hd, 0)], w=KY(hd))
                    elif m < 6:
                        S.pe(lambda h, ci=ci, cur=cur: h.matmul(psY(hd)[:, ci, :], lhsT=X[:, hd, cur, ci, 0, :],
                                                                rhs=X[:, hd, cur, ci, 1:3, :].rearrange("p a b -> p (a b)"),
                                                                start=True, stop=True),
                             r=[("XP", hd, cur), ("XQ", hd, cur), ("XT", hd, cur)], w=KY(hd))
                    else:
                        S.pe(lambda h, ci=ci, cur=cur: h.matmul(psY(hd)[:, ci, 128:256], lhsT=X[:, hd, cur, ci, 0, :],
                                                                rhs=X[:, hd, cur, ci, 2, :], start=True, stop=True),
                             r=[("XP", hd, cur), ("XT", hd, cur)], w=KY(hd))
                if m < 6:
                    for ci in range(4):
                        S.pe(lambda h, ci=ci, cur=cur: h.matmul(psX(hd)[:, ci, :], lhsT=X[:, hd, cur, ci, 1, :], rhs=X[:, hd, cur, ci, 0, :],
                                                                start=True, stop=True), r=[("XP", hd, cur), ("XQ", hd, cur)], w=KX(hd))
                    S.act(lambda h, nxt=nxt: h.copy(out=X[:, hd, nxt, :, 1, :], in_=psY(hd)[:, :, 0:128]), r=KY(hd), w=[("XQ", hd, nxt)])
                    S.act(lambda h, nxt=nxt: h.copy(out=X[:, hd, nxt, :, 0, :], in_=psX(hd)), r=KX(hd), w=[("XP", hd, nxt)])
                if 1 <= m < 6:
                    S.dve(lambda h, cur=cur, nxt=nxt: h.tensor_tensor(out=X[:, hd, nxt, :, 2, :], in0=psY(hd)[:, :, 128:256],
                                                                      in1=X[:, hd, cur, :, 2, :], op=ALU.add),
                          r=KY(hd) + [("XT", hd, cur)], w=[("XT", hd, nxt)])
                if m == 6:
                    S.dve(lambda h, cur=cur: h.tensor_tensor(out=Ttb[:, hd, :, :], in0=psY(hd)[:, :, 128:256], in1=X[:, hd, cur, :, 2, :],
                                                             op=ALU.add), r=KY(hd) + [("XT", hd, cur)], w=[("Ttb", hd)])
                yield
            for ci in range(4):
                S.pe(lambda h, ci=ci: h.matmul(psX(hd)[:, ci, :], lhsT=Ttb[:, hd, ci, :], rhs=vb[:, hd, ci, :], start=True, stop=True),
                     r=[("Ttb", hd), ("vb", hd)], w=KX(hd))
            S.act(lambda h: h.copy(out=uu[:, hd, :, :], in_=psX(hd)), r=KX(hd), w=[("uu", hd)])
            for ci in range(4):
                S.pe(lambda h, ci=ci: h.matmul(psY(hd)[:, ci, 0:128], lhsT=kbg[:, hd, ci, :], rhs=Ttb[:, hd, ci, :], start=True, stop=True),
                     r=[("Ttb", hd), ("kbg", hd)], w=KY(hd))
            S.act(lambda h: h.copy(out=wTb[:, hd, :, :], in_=psY(hd)[:, :, 0:128]), r=KY(hd), w=[("wTb", hd)])
            yield
            if cut == 2:
                return
            sbk = YB(hd) + 1
            KS = [("ps", sbk)]
            for ci in range(4):
                S.pe(lambda h, ci=ci: h.matmul(ps[:, sbk, 0:128], lhsT=wTb[:, hd, ci, :], rhs=Sb[:, hd, :], start=True, stop=True),
                     r=[("wTb", hd), ("Sb", hd)], w=KS)
                S.dve(lambda h, ci=ci: h.tensor_tensor(out=vnb[:, hd, :], in0=uu[:, hd, ci, :], in1=ps[:, sbk, 0:128], op=ALU.subtract),
                      r=[("uu", hd)] + KS, w=[("vnb", hd)])
                S.pe(lambda h, ci=ci: h.matmul(ps[:, sbk, 256:384], lhsT=Sb[:, hd, :], rhs=qdT[:, hd, ci, :], start=True, stop=False),
                     r=[("Sb", hd), ("qdT", hd)], w=KS)
                S.pe(lambda h, ci=ci: h.matmul(ps[:, sbk, 256:384], lhsT=vnb[:, hd, :], rhs=ATb[:, hd, ci, :], start=False, stop=True),
                     r=[("vnb", hd), ("ATb", hd)], w=KS)
                S.pe(lambda h, ci=ci: h.matmul(ps[:, sbk, 128:256], lhsT=kd[:, hd, ci, :], rhs=vnb[:, hd, :], start=True, stop=True),
                     r=[("kd", hd), ("vnb", hd)], w=KS)
                S.act(lambda h, ci=ci: h.copy(out=oT[:, hd, tsl(ci)], in_=ps[:, sbk, 256:384]), r=KS, w=[("oT", hd)])
                S.dve(lambda h, ci=ci: h.scalar_tensor_tensor(out=Sst[:, hd, :], in0=Sst[:, hd, :], scalar=gl[:, hd, ci:ci + 1],
                                                              in1=ps[:, sbk, 128:256], op0=ALU.mult, op1=ALU.add),
                      r=[("S", hd), ("gl", hd)] + KS, w=[("S", hd)])
                S.act(lambda h: h.copy(out=Sb[:, hd, :], in_=Sst[:, hd, :]), r=[("S", hd)], w=[("Sb", hd)])
                yield
            S.act(lambda h: h.activation(out=sqb[:, hd, :], in_=oT[:, hd, :], func=AF.Square), r=[("oT", hd)], w=[("sqb", hd)])
            S.pe(lambda h: h.matmul(ps[:, PROJ, :], lhsT=ones_bf[:, :], rhs=sqb[:, hd, :], start=True, stop=True),
                 r=[("sqb", hd), "ones_bf"], w=[("ps", PROJ)])
            S.act(lambda h: h.activation(out=rr[:, hd, :], in_=ps[:, PROJ, :], func=AF.Sqrt, scale=1.0 / 128, bias=eps_col[:, 0:1]),
                  r=[("ps", PROJ), "eps_col"], w=[("rr", hd)])
            S.dve(lambda h: h.reciprocal(out=rr[:, hd, :], in_=rr[:, hd, :]), r=[("rr", hd)], w=[("rr", hd)])
            S.dve(lambda h: h.scalar_tensor_tensor(out=oT[:, hd, :], in0=oT[:, hd, :], scalar=sm[:, 28:29], in1=rr[:, hd, :],
                                                   op0=ALU.mult, op1=ALU.mult), r=[("oT", hd), "sm", ("rr", hd)], w=[("oT", hd)])
            S.act(lambda h: h.activation(out=sz[:, hd, :], in_=zT[:, hd, :], func=AF.Silu), r=[("zT", hd)], w=[("sz", hd)])
            S.dve(lambda h: h.tensor_tensor(out=og[:, hd, :], in0=oT[:, hd, :], in1=sz[:, hd, :], op=ALU.mult),
                  r=[("oT", hd), ("sz", hd)], w=[("og", hd)])
            S.dma(lambda h: h.dma_start(out=og_dst(hd, tg), in_=og[:, hd, :]), r=[("og", hd)])
            yield

        for tg in range(NG):
            b = tg % 2
            S.dma(lambda h, b=b, tg=tg: h.dma_start(out=hTg[:, b, :, :], in_=h_src(tg)), r=list(h_keys), w=[("hTg", b)])
            for t in range(4):
                for c in range(8):
                    S.pe(lambda h, b=b, t=t, c=c: h.matmul(ps[:, MISC, t * 4:(t + 1) * 4], lhsT=hTg[:, b, c, t * 128:(t + 1) * 128],
                                                            rhs=wq[:, c, 1024:1028], start=(c == 0), stop=(c == 7)),
                         r=[("hTg", b)] + WQ, w=[("ps", MISC)])
            S.act(lambda h: h.copy(out=bat[:, :, :], in_=ps[:, MISC, 0:16].rearrange("p (t f) -> p t f", f=4)),
                  r=[("ps", MISC)], w=["bat"])
            S.act(lambda h: h.activation(out=beta[:, :, :], in_=bat[:, :, 0:2], func=AF.Sigmoid), r=["bat"], w=["beta"])
            for hd in range(2):
                S.act(lambda h, hd=hd: h.activation(out=esp[:, :, hd:hd + 1], in_=bat[:, :, 2 + hd:3 + hd], func=AF.Exp,
                                                    bias=sm[:, 26 + hd:27 + hd]), r=["bat", "sm"], w=[("esp", hd)])
                S.act(lambda h, hd=hd: h.activation(out=esp[:, :, hd:hd + 1], in_=esp[:, :, hd:hd + 1], func=AF.Ln,
                                                    bias=one_col[:, 0:1]), r=[("esp", hd), "one_col"], w=[("esp", hd)])
                S.dve(lambda h, hd=hd: h.tensor_scalar(out=gg[:, :, hd:hd + 1], in0=esp[:, :, hd:hd + 1],
                                                       scalar1=negA[:, hd:hd + 1], scalar2=None, op0=ALU.mult),
                      r=[("esp", hd), "negA"], w=[("gg", hd)])
            for t in range(4):
                S.pe(lambda h, t=t: h.matmul(ps[:, MISC, 32 + 2 * t:34 + 2 * t], lhsT=cm[:, TRIU, :], rhs=gg[:, t, :],
                                             start=True, stop=True),
                     r=["cm", ("gg", 0), ("gg", 1)], w=[("ps", MISC)])
            S.act(lambda h: h.copy(out=gcs[:, :, :], in_=ps[:, MISC, 32:40].rearrange("p (t f) -> p t f", f=2)),
                  r=[("ps", MISC)], w=["gcs"])
            for _ in zip_longest(unit(tg, 0, b), unit(tg, 1, b)):
                pass
        emit_phase(nc, S, ss)


def gdn_phase3(nc, ss, ps, h_src, og_dst, w_my, sm_d, cm_d, NG, cut=0, pre_fn=None, h_keys=()):
    import contextlib
    from itertools import zip_longest
    with contextlib.ExitStack() as st:
        sb = lambda name, shape, dt: st.enter_context(sbt(nc, name, shape, dt))
        ones_bf = sb("ones_bf", [128, 128], BF16)
        ones_f = sb("ones_f", [128, 128], F32)
        eps_col = sb("eps_col", [128, 1], F32)
        one_col = sb("one_col", [128, 1], F32)
        sm = sb("sm_sb", [128, 32], F32)
        cm = sb("cm_sb", [128, 4, 128], F32)
        negA = sb("negA", [128, 2], F32)
        wq = sb("wq", [128, 8, 1028], BF16)
        hTg = sb("hTg", [128, 2, 8, 512], BF16)
        pre = sb("pre", [128, 2, 3, 515], F32)
        cs = sb("cs", [128, 2, 3, 512], F32)
        sqb = sb("sqb", [128, 2, 512], BF16)
        sqo = sb("sqo", [128, 2, 512], BF16)
        rro = sb("rro", [128, 2, 512], F32)
        rr = sb("rr", [128, 2, 512], F32)
        qkn = sb("qkn", [128, 2, 2, 512], F32)
        qkb = sb("qkb", [128, 2, 2, 512], BF16)
        zT = sb("zT", [128, 2, 2, 512], BF16)
        bat = sb("bat", [128, 4, 4], F32)
        beta = sb("beta", [128, 4, 2], F32)
        esp = sb("esp", [128, 4, 2], F32)
        gg = sb("gg", [128, 4, 2], F32)
        gcs = sb("gcs", [128, 4, 2], F32)
        Rm = sb("Rm", [128, 2, 4, 128], F32)
        tdm = sb("tdm", [128, 2, 4, 128], F32)
        egcB = sb("egcB", [128, 2, 4, 128], F32)
        W4 = sb("W4", [128, 2, 4, 128], F32)
        Am = sb("Am", [128, 2, 4, 128], F32)
        X = sb("X", [128, 2, 2, 4, 3, 128], F32)
        glr = sb("glr", [128, 2, 4], F32)
        gl = sb("gl", [128, 2, 2, 4], F32)
        ekd = sb("ekd", [128, 2, 4], F32)
        egc = sb("egc", [128, 2, 4], F32)
        bg = sb("bg", [128, 2, 4], F32)
        Ttb = sb("Ttb", [128, 2, 4, 128], BF16)
        ATb = sb("ATb", [128, 2, 2, 4, 128], BF16)
        qdT = sb("qdT", [128, 2, 2, 4, 128], BF16)
        kd = sb("kd", [128, 2, 2, 4, 128], BF16)
        kbg = sb("kbg", [128, 2, 4, 128], BF16)
        vb = sb("vb", [128, 2, 4, 128], BF16)
        uu = sb("uu", [128, 2, 2, 4, 128], F32)
        wTb = sb("wTb", [128, 2, 2, 4, 128], BF16)
        vnb = sb("vnb", [128, 2, 128], BF16)
        Sst = sb("Sst", [128, 2, 128], F32)
        Sb = sb("Sb", [128, 2, 128], BF16)
        oT = sb("oT", [128, 2, 512], F32)
        sz = sb("sz", [128, 2, 512], F32)
        og = sb("og", [128, 2, 512], BF16)

        IDN, TRIU, NMI, STR = 0, 1, 2, 3
        B4 = [128, 4, 128]
        mask4 = lambda mi: cm[:, mi, :].unsqueeze(1).to_broadcast(B4)
        PROJ, MISC = 0, 1
        XB = lambda hd: 2 + hd
        YB = lambda hd: 4 + 2 * hd
        psX = lambda hd: ps[:, XB(hd), :].rearrange("p (t c) -> p t c", c=128)
        psY = lambda hd: ps[:, YB(hd):YB(hd) + 2, :].rearrange("p b (t c) -> p (b t) c", c=256)
        KX = lambda hd: [("ps", XB(hd))]
        KY = lambda hd: [("ps", YB(hd)), ("ps", YB(hd) + 1)]
        S = Sched()
        if pre_fn is not None:
            pre_fn(S)
        S.pool(lambda h: h.memset(ones_bf[:, :], 1.0), w=["ones_bf"])
        S.pool(lambda h: h.memset(ones_f[:, :], 1.0), w=["ones_f"])
        S.pool(lambda h: h.memset(eps_col[:, :], EPS), w=["eps_col"])
        S.pool(lambda h: h.memset(one_col[:, :], 1.0), w=["one_col"])
        S.pool(lambda h: h.memset(pre[:, :, :, :], 0.0), w=[("pre", a, b) for a in range(2) for b in range(3)])
        S.pool(lambda h: h.memset(Sst[:, :, :], 0.0), w=[("S", 0), ("S", 1)])
        S.pool(lambda h: h.memset(Sb[:, :, :], 0.0), w=[("Sb", 0), ("Sb", 1)])
        S.dma(lambda h: h.dma_start(out=sm[:, :], in_=sm_d[:, :]), w=["sm"])
        S.dma(lambda h: h.dma_start(out=cm[:, :, :], in_=cm_d[:, :, :]), w=["cm"])
        wv = w_my.rearrange("(c p) n -> p c n", p=128)
        for c in range(8):
            S.dma(lambda h, c=c: h.dma_start(out=wq[:, c, :], in_=wv[:, c, :]), w=[("wq", c)], q="pool")
        S.act(lambda h: h.activation(out=negA[:, :], in_=sm[:, 24:26], func=AF.Exp), r=["sm"], w=["negA"])
        S.dve(lambda h: h.tensor_scalar(out=negA[:, :], in0=negA[:, :], scalar1=-1.0, scalar2=None, op0=ALU.mult),
              r=["negA"], w=["negA"])
        WQ = [("wq", c) for c in range(8)]
        tsl = lambda ci: slice(ci * 128, (ci + 1) * 128)

        def prep(tg, hd, b, gp):
            pb = hd
            bcol = lambda t4: t4[:, :, hd:hd + 1].to_broadcast(B4)
            bvec = lambda v: v[:, hd, :].unsqueeze(2).to_broadcast(B4)
            for j in range(4):
                for c in range(8):
                    S.pe(lambda h, j=j, c=c: h.matmul(ps[:, PROJ, :], lhsT=wq[:, c, hd * 512 + j * 128:hd * 512 + (j + 1) * 128],
                                                      rhs=hTg[:, b, c, :], start=(c == 0), stop=(c == 7)),
                         r=[("hTg", b)] + WQ, w=[("ps", PROJ)])
                if j < 3:
                    S.act(lambda h, j=j: h.copy(out=pre[:, hd, j, 3:515], in_=ps[:, PROJ, :]), r=[("ps", PROJ)], w=[("pre", hd, j)])
                else:
                    S.act(lambda h: h.copy(out=zT[:, hd, gp, :], in_=ps[:, PROJ, :]), r=[("ps", PROJ)], w=[("zT", hd, gp)])
                yield
            for j in range(3):
                cw = lambda k, j=j: sm[:, (hd * 3 + j) * 4 + k:(hd * 3 + j) * 4 + k + 1]
                S.dve(lambda h, j=j, cw=cw: h.tensor_scalar(out=cs[:, hd, j, :], in0=pre[:, hd, j, 3:515], scalar1=cw(3), scalar2=None,
                                                           op0=ALU.mult), r=[("pre", hd, j), "sm"], w=[("cs", hd, j)])
                for k in (2, 1, 0):
                    S.dve(lambda h, j=j, k=k, cw=cw: h.scalar_tensor_tensor(out=cs[:, hd, j, :], in0=pre[:, hd, j, k:k + 512], scalar=cw(k),
                                                                          in1=cs[:, hd, j, :], op0=ALU.mult, op1=ALU.add),
                          r=[("pre", hd, j), "sm", ("cs", hd, j)], w=[("cs", hd, j)])
                S.dve(lambda h, j=j: h.tensor_copy(out=pre[:, hd, j, 0:3], in_=pre[:, hd, j, 512:515]),
                      r=[("pre", hd, j)], w=[("pre", hd, j)])
                S.act(lambda h, j=j: h.activation(out=cs[:, hd, j, :], in_=cs[:, hd, j, :], func=AF.Silu),
                      r=[("cs", hd, j)], w=[("cs", hd, j)])
                yield
            for j in range(2):
                S.act(lambda h, j=j: h.activation(out=sqb[:, hd, :], in_=cs[:, hd, j, :], func=AF.Square), r=[("cs", hd, j)], w=[("sqb", hd)])
                S.pe(lambda h: h.matmul(ps[:, PROJ, :], lhsT=ones_bf[:, :], rhs=sqb[:, hd, :], start=True, stop=True),
                     r=[("sqb", hd), "ones_bf"], w=[("ps", PROJ)])
                S.act(lambda h: h.activation(out=rr[:, hd, :], in_=ps[:, PROJ, :], func=AF.Sqrt, bias=eps_col[:, 0:1]),
                      r=[("ps", PROJ), "eps_col"], w=[("rr", hd)])
                S.dve(lambda h: h.reciprocal(out=rr[:, hd, :], in_=rr[:, hd, :]), r=[("rr", hd)], w=[("rr", hd)])
                sc = (128.0 ** -0.5) if j == 0 else 1.0
                S.dve(lambda h, j=j, sc=sc: h.scalar_tensor_tensor(out=qkn[:, hd, j, :], in0=cs[:, hd, j, :], scalar=sc, in1=rr[:, hd, :],
                                                                   op0=ALU.mult, op1=ALU.mult),
                      r=[("cs", hd, j), ("rr", hd)], w=[("qkn", hd, j)])
                S.act(lambda h, j=j: h.copy(out=qkb[:, hd, j, :], in_=qkn[:, hd, j, :]), r=[("qkn", hd, j)], w=[("qkb", hd, j)])
                yield
            if cut == 1:
                return
            q4 = qkn[:, hd, 0, :].rearrange("p (t c) -> p t c", c=128)
            S.dve(lambda h: h.tensor_tensor(out=Rm[:, hd, :, :], in0=mask4(TRIU), in1=bcol(gg), op=ALU.mult),
                  r=["cm", ("gg", hd)], w=[("Rm", hd)])
            S.pe(lambda h: h.matmul(ps[:, XB(hd), :], lhsT=ones_f[:, :], rhs=Rm[:, hd, :, :].rearrange("p t c -> p (t c)"),
                                    start=True, stop=True), r=[("Rm", hd), "ones_f"], w=KX(hd))
            yield
            S.dve(lambda h: h.tensor_tensor(out=tdm[:, hd, :, :], in0=psX(hd), in1=bcol(gcs), op=ALU.subtract),
                  r=KX(hd) + ["gcs"], w=[("tdm", hd)])
            S.act(lambda h: h.activation(out=egcB[:, hd, :, :], in_=psX(hd), func=AF.Exp), r=KX(hd), w=[("egcB", hd)])
            S.act(lambda h: h.copy(out=glr[:, hd, :].unsqueeze(2), in_=psX(hd)[:, :, 127:128]), r=KX(hd), w=[("glr", hd)])
            S.dve(lambda h: h.tensor_tensor(out=tdm[:, hd, :, :], in0=tdm[:, hd, :, :], in1=mask4(NMI), op=ALU.max),
                  r=[("tdm", hd), "cm"], w=[("tdm", hd)])
            S.act(lambda h: h.activation(out=tdm[:, hd, :, :], in_=tdm[:, hd, :, :], func=AF.Exp, scale=-1.0),
                  r=[("tdm", hd)], w=[("tdm", hd)])
            S.act(lambda h: h.activation(out=gl[:, hd, gp, :], in_=glr[:, hd, :], func=AF.Exp), r=[("glr", hd)], w=[("gl", hd, gp)])
            S.dve(lambda h: h.tensor_tensor(out=ekd[:, hd, :], in0=glr[:, hd, :], in1=gcs[:, :, hd], op=ALU.subtract),
                  r=[("glr", hd), "gcs"], w=[("ekd", hd)])
            S.act(lambda h: h.activation(out=ekd[:, hd, :], in_=ekd[:, hd, :], func=AF.Exp), r=[("ekd", hd)], w=[("ekd", hd)])
            S.act(lambda h: h.activation(out=egc[:, hd, :], in_=gcs[:, :, hd], func=AF.Exp), r=["gcs"], w=[("egc", hd)])
            S.dve(lambda h: h.tensor_tensor(out=bg[:, hd, :], in0=egc[:, hd, :], in1=beta[:, :, hd], op=ALU.mult),
                  r=[("egc", hd), "beta"], w=[("bg", hd)])
            S.dve(lambda h: h.tensor_tensor(out=qdT[:, hd, gp, :, :], in0=q4, in1=egcB[:, hd, :, :], op=ALU.mult),
                  r=[("qkn", hd, 0), ("egcB", hd)], w=[("qdT", hd, gp)])
            S.dve(lambda h: h.tensor_tensor(out=W4[:, hd, :, :], in0=tdm[:, hd, :, :], in1=mask4(STR), op=ALU.mult),
                  r=[("tdm", hd), "cm"], w=[("W4", hd)])
            S.dve(lambda h: h.scalar_tensor_tensor(out=W4[:, hd, :, :], in0=W4[:, hd, :, :], scalar=-1.0, in1=bcol(beta),
                                                   op0=ALU.mult, op1=ALU.mult), r=[("W4", hd), "beta"], w=[("W4", hd)])
            yield
            for ci in range(4):
                S.pe(lambda h, ci=ci: h.matmul(psX(hd)[:, ci, :], lhsT=qkb[:, hd, 1, tsl(ci)], rhs=qkb[:, hd, 1, tsl(ci)],
                                               start=True, stop=True), r=[("qkb", hd, 1)], w=KX(hd))
            S.dve(lambda h: h.tensor_tensor(out=X[:, hd, 0, :, 0, :], in0=psX(hd), in1=W4[:, hd, :, :], op=ALU.mult),
                  r=KX(hd) + [("W4", hd)], w=[("XP", hd, 0)])
            yield
            for ci in range(4):
                S.pe(lambda h, ci=ci: h.matmul(psX(hd)[:, ci, :], lhsT=qkb[:, hd, 0, tsl(ci)], rhs=qkb[:, hd, 1, tsl(ci)],
                                               start=True, stop=True), r=[("qkb", hd, 0), ("qkb", hd, 1)], w=KX(hd))
            S.dve(lambda h: h.tensor_tensor(out=Am[:, hd, :, :], in0=psX(hd), in1=tdm[:, hd, :, :], op=ALU.mult),
                  r=KX(hd) + [("tdm", hd)], w=[("Am", hd)])
            yield
            for ci in range(4):
                S.pe(lambda h, ci=ci: h.matmul(psX(hd)[:, ci, :], lhsT=X[:, hd, 0, ci, 0, :], rhs=cm[:, IDN, :], start=True, stop=True),
                     r=[("XP", hd, 0), "cm"], w=KX(hd))
            S.act(lambda h: h.copy(out=X[:, hd, 0, :, 1, :], in_=psX(hd)), r=KX(hd), w=[("XQ", hd, 0)])
            S.dve(lambda h: h.tensor_tensor(out=X[:, hd, 1, :, 2, :], in0=psX(hd), in1=mask4(IDN), op=ALU.add),
                  r=KX(hd) + ["cm"], w=[("XT", hd, 1)])
            yield
            for ci in range(4):
                S.pe(lambda h, ci=ci: h.matmul(psX(hd)[:, ci, :], lhsT=Am[:, hd, ci, :], rhs=cm[:, IDN, :], start=True, stop=True),
                     r=[("Am", hd), "cm"], w=KX(hd))
            S.act(lambda h: h.copy(out=ATb[:, hd, gp, :, :], in_=psX(hd)), r=KX(hd), w=[("ATb", hd, gp)])
            yield
            for ci in range(4):
                S.pe(lambda h, ci=ci: h.matmul(psX(hd)[:, ci, :], lhsT=qkn[:, hd, 1, tsl(ci)], rhs=cm[:, IDN, :], start=True, stop=True),
                     r=[("qkn", hd, 1), "cm"], w=KX(hd))
            S.dve(lambda h: h.tensor_tensor(out=kd[:, hd, gp, :, :], in0=psX(hd), in1=bvec(ekd), op=ALU.mult),
                  r=KX(hd) + [("ekd", hd)], w=[("kd", hd, gp)])
            S.dve(lambda h: h.tensor_tensor(out=kbg[:, hd, :, :], in0=psX(hd), in1=bvec(bg), op=ALU.mult),
                  r=KX(hd) + [("bg", hd)], w=[("kbg", hd)])
            yield
            for ci in range(4):
                S.pe(lambda h, ci=ci: h.matmul(psX(hd)[:, ci, :], lhsT=cs[:, hd, 2, tsl(ci)], rhs=cm[:, IDN, :], start=True, stop=True),
                     r=[("cs", hd, 2), "cm"], w=KX(hd))
            S.dve(lambda h: h.tensor_tensor(out=vb[:, hd, :, :], in0=psX(hd), in1=bcol(beta), op=ALU.mult),
                  r=KX(hd) + ["beta"], w=[("vb", hd)])
            yield
            NL = 5
            for m in range(NL):
                cur, nxt = m % 2, (m + 1) % 2
                for ci in range(4):
                    if m == 0:
                        S.pe(lambda h, ci=ci: h.matmul(psY(hd)[:, ci, 0:128], lhsT=X[:, hd, 0, ci, 0, :], rhs=X[:, hd, 0, ci, 1, :],
                                                       start=True, stop=True), r=[("XP", hd, 0), ("XQ", hd, 0)], w=KY(hd))
                    elif m < NL - 1:
                        S.pe(lambda h, ci=ci, cur=cur: h.matmul(psY(hd)[:, ci, :], lhsT=X[:, hd, cur, ci, 0, :],
                                                                rhs=X[:, hd, cur, ci, 1:3, :].rearrange("p a b -> p (a b)"),
                                                                start=True, stop=True),
                             r=[("XP", hd, cur), ("XQ", hd, cur), ("XT", hd, cur)], w=KY(hd))
                    else:
                        S.pe(lambda h, ci=ci, cur=cur: h.matmul(psY(hd)[:, ci, 128:256], lhsT=X[:, hd, cur, ci, 0, :],
                                                                rhs=X[:, hd, cur, ci, 2, :], start=True, stop=True),
                             r=[("XP", hd, cur), ("XT", hd, cur)], w=KY(hd))
                if m < NL - 1:
                    for ci in range(4):
                        S.pe(lambda h, ci=ci, cur=cur: h.matmul(psX(hd)[:, ci, :], lhsT=X[:, hd, cur, ci, 1, :], rhs=X[:, hd, cur, ci, 0, :],
                                                                start=True, stop=True), r=[("XP", hd, cur), ("XQ", hd, cur)], w=KX(hd))
                    S.act(lambda h, nxt=nxt: h.copy(out=X[:, hd, nxt, :, 1, :], in_=psY(hd)[:, :, 0:128]), r=KY(hd), w=[("XQ", hd, nxt)])
                    S.act(lambda h, nxt=nxt: h.copy(out=X[:, hd, nxt, :, 0, :], in_=psX(hd)), r=KX(hd), w=[("XP", hd, nxt)])
                if 1 <= m < NL - 1:
                    S.dve(lambda h, cur=cur, nxt=nxt: h.tensor_tensor(out=X[:, hd, nxt, :, 2, :], in0=psY(hd)[:, :, 128:256],
                                                                      in1=X[:, hd, cur, :, 2, :], op=ALU.add),
                          r=KY(hd) + [("XT", hd, cur)], w=[("XT", hd, nxt)])
                if m == NL - 1:
                    S.dve(lambda h, cur=cur: h.tensor_tensor(out=Ttb[:, hd, :, :], in0=psY(hd)[:, :, 128:256], in1=X[:, hd, cur, :, 2, :],
                                                             op=ALU.add), r=KY(hd) + [("XT", hd, cur)], w=[("Ttb", hd)])
                yield
            for ci in range(4):
                S.pe(lambda h, ci=ci: h.matmul(psX(hd)[:, ci, :], lhsT=Ttb[:, hd, ci, :], rhs=vb[:, hd, ci, :], start=True, stop=True),
                     r=[("Ttb", hd), ("vb", hd)], w=KX(hd))
            S.act(lambda h: h.copy(out=uu[:, hd, gp, :, :], in_=psX(hd)), r=KX(hd), w=[("uu", hd, gp)])
            for ci in range(4):
                S.pe(lambda h, ci=ci: h.matmul(psY(hd)[:, ci, 0:128], lhsT=kbg[:, hd, ci, :], rhs=Ttb[:, hd, ci, :], start=True, stop=True),
                     r=[("Ttb", hd), ("kbg", hd)], w=KY(hd))
            S.act(lambda h: h.copy(out=wTb[:, hd, gp, :, :], in_=psY(hd)[:, :, 0:128]), r=KY(hd), w=[("wTb", hd, gp)])
            yield

        def scan_out(tg, hd, gp):
            sbk = MISC
            KS = [("ps", sbk)]
            for ci in range(4):
                S.pe(lambda h, ci=ci: h.matmul(ps[:, sbk, 128:256], lhsT=wTb[:, hd, gp, ci, :], rhs=Sb[:, hd, :], start=True, stop=True),
                     r=[("wTb", hd, gp), ("Sb", hd)], w=KS)
                S.dve(lambda h, ci=ci: h.tensor_tensor(out=vnb[:, hd, :], in0=uu[:, hd, gp, ci, :], in1=ps[:, sbk, 128:256], op=ALU.subtract),
                      r=[("uu", hd, gp)] + KS, w=[("vnb", hd)])
                S.pe(lambda h, ci=ci: h.matmul(ps[:, sbk, 384:512], lhsT=Sb[:, hd, :], rhs=qdT[:, hd, gp, ci, :], start=True, stop=False),
                     r=[("Sb", hd), ("qdT", hd, gp)], w=KS)
                S.pe(lambda h, ci=ci: h.matmul(ps[:, sbk, 384:512], lhsT=vnb[:, hd, :], rhs=ATb[:, hd, gp, ci, :], start=False, stop=True),
                     r=[("vnb", hd), ("ATb", hd, gp)], w=KS)
                S.pe(lambda h, ci=ci: h.matmul(ps[:, sbk, 256:384], lhsT=kd[:, hd, gp, ci, :], rhs=vnb[:, hd, :], start=True, stop=True),
                     r=[("kd", hd, gp), ("vnb", hd)], w=KS)
                S.act(lambda h, ci=ci: h.copy(out=oT[:, hd, tsl(ci)], in_=ps[:, sbk, 384:512]), r=KS, w=[("oT", hd)])
                S.dve(lambda h, ci=ci: h.scalar_tensor_tensor(out=Sst[:, hd, :], in0=Sst[:, hd, :], scalar=gl[:, hd, gp, ci:ci + 1],
                                                              in1=ps[:, sbk, 256:384], op0=ALU.mult, op1=ALU.add),
                      r=[("S", hd), ("gl", hd, gp)] + KS, w=[("S", hd)])
                S.act(lambda h: h.copy(out=Sb[:, hd, :], in_=Sst[:, hd, :]), r=[("S", hd)], w=[("Sb", hd)])
                yield
            S.act(lambda h: h.activation(out=sqo[:, hd, :], in_=oT[:, hd, :], func=AF.Square), r=[("oT", hd)], w=[("sqo", hd)])
            S.pe(lambda h: h.matmul(ps[:, PROJ, :], lhsT=ones_bf[:, :], rhs=sqo[:, hd, :], start=True, stop=True),
                 r=[("sqo", hd), "ones_bf"], w=[("ps", PROJ)])
            S.act(lambda h: h.activation(out=rro[:, hd, :], in_=ps[:, PROJ, :], func=AF.Sqrt, scale=1.0 / 128, bias=eps_col[:, 0:1]),
                  r=[("ps", PROJ), "eps_col"], w=[("rro", hd)])
            S.dve(lambda h: h.reciprocal(out=rro[:, hd, :], in_=rro[:, hd, :]), r=[("rro", hd)], w=[("rro", hd)])
            S.dve(lambda h: h.scalar_tensor_tensor(out=oT[:, hd, :], in0=oT[:, hd, :], scalar=sm[:, 28:29], in1=rro[:, hd, :],
                                                   op0=ALU.mult, op1=ALU.mult), r=[("oT", hd), "sm", ("rro", hd)], w=[("oT", hd)])
            S.act(lambda h: h.activation(out=sz[:, hd, :], in_=zT[:, hd, gp, :], func=AF.Silu), r=[("zT", hd, gp)], w=[("sz", hd)])
            S.dve(lambda h: h.tensor_tensor(out=og[:, hd, :], in0=oT[:, hd, :], in1=sz[:, hd, :], op=ALU.mult),
                  r=[("oT", hd), ("sz", hd)], w=[("og", hd)])
            S.dma(lambda h: h.dma_start(out=og_dst(hd, tg), in_=og[:, hd, :]), r=[("og", hd)])
            yield

        def preamble(tg):
            b = tg % 2
            S.dma(lambda h, b=b, tg=tg: h.dma_start(out=hTg[:, b, :, :], in_=h_src(tg)), r=list(h_keys), w=[("hTg", b)])
            for t in range(4):
                for c in range(8):
                    S.pe(lambda h, b=b, t=t, c=c: h.matmul(ps[:, MISC, t * 4:(t + 1) * 4], lhsT=hTg[:, b, c, t * 128:(t + 1) * 128],
                                                            rhs=wq[:, c, 1024:1028], start=(c == 0), stop=(c == 7)),
                         r=[("hTg", b)] + WQ, w=[("ps", MISC)])
            S.act(lambda h: h.copy(out=bat[:, :, :], in_=ps[:, MISC, 0:16].rearrange("p (t f) -> p t f", f=4)),
                  r=[("ps", MISC)], w=["bat"])
            S.act(lambda h: h.activation(out=beta[:, :, :], in_=bat[:, :, 0:2], func=AF.Sigmoid), r=["bat"], w=["beta"])
            for hd in range(2):
                S.act(lambda h, hd=hd: h.activation(out=esp[:, :, hd:hd + 1], in_=bat[:, :, 2 + hd:3 + hd], func=AF.Exp,
                                                    bias=sm[:, 26 + hd:27 + hd]), r=["bat", "sm"], w=[("esp", hd)])
                S.act(lambda h, hd=hd: h.activation(out=esp[:, :, hd:hd + 1], in_=esp[:, :, hd:hd + 1], func=AF.Ln,
                                                    bias=one_col[:, 0:1]), r=[("esp", hd), "one_col"], w=[("esp", hd)])
                S.dve(lambda h, hd=hd: h.tensor_scalar(out=gg[:, :, hd:hd + 1], in0=esp[:, :, hd:hd + 1],
                                                       scalar1=negA[:, hd:hd + 1], scalar2=None, op0=ALU.mult),
                      r=[("esp", hd), "negA"], w=[("gg", hd)])
            for t in range(4):
                S.pe(lambda h, t=t: h.matmul(ps[:, MISC, 32 + 2 * t:34 + 2 * t], lhsT=cm[:, TRIU, :], rhs=gg[:, t, :],
                                             start=True, stop=True),
                     r=["cm", ("gg", 0), ("gg", 1)], w=[("ps", MISC)])
            S.act(lambda h: h.copy(out=gcs[:, :, :], in_=ps[:, MISC, 32:40].rearrange("p (t f) -> p t f", f=2)),
                  r=[("ps", MISC)], w=["gcs"])

        def drive(fast, slow):
            live_f = list(fast)
            live_s = list(slow)
            rnd = 0
            while live_f or live_s:
                for gen in list(live_f):
                    if next(gen, StopIteration) is StopIteration:
                        live_f.remove(gen)
                if live_s and (rnd % 3 == 0 or not live_f):
                    for gen in list(live_s):
                        if next(gen, StopIteration) is StopIteration:
                            live_s.remove(gen)
                rnd += 1

        preamble(0)
        drive([prep(0, 0, 0, 0), prep(0, 1, 0, 0)], [])
        for tg in range(NG):
            gp = tg % 2
            if tg + 1 < NG:
                preamble(tg + 1)
                drive([prep(tg + 1, 0, (tg + 1) % 2, (tg + 1) % 2), prep(tg + 1, 1, (tg + 1) % 2, (tg + 1) % 2)],
                      [scan_out(tg, 0, gp), scan_out(tg, 1, gp)])
            else:
                drive([], [scan_out(tg, 0, gp), scan_out(tg, 1, gp)])
        emit_phase(nc, S, ss)


def gdn_phase4(nc, ss, ps, h_src, og_dst, w_my, sm_d, cm_d, NG, cut=0, pre_fn=None, h_keys=(), og_key=None, post_fn=None):
    import contextlib
    from itertools import zip_longest
    with contextlib.ExitStack() as st:
        sb = lambda name, shape, dt: st.enter_context(sbt(nc, name, shape, dt))
        ones_bf = sb("ones_bf", [128, 128], BF16)
        ones_f = sb("ones_f", [128, 128], F32)
        eps_col = sb("eps_col", [128, 1], F32)
        one_col = sb("one_col", [128, 1], F32)
        sm = sb("sm_sb", [128, 32], F32)
        cm = sb("cm_sb", [128, 4, 128], F32)
        negA = sb("negA", [128, 2], F32)
        wq = sb("wq", [128, 8, 1028], BF16)
        hTg = sb("hTg", [128, 3, 8, 512], BF16)
        pre = sb("pre", [128, 2, 3, 515], BF16)
        dgw = sb("dgw", [128, 24, 128], BF16)
        idb = sb("idb", [128, 128], BF16)
        vbf = sb("vbf", [128, 2, 512], BF16)
        cs = sb("cs", [128, 2, 3, 512], F32)
        sqb = sb("sqb", [128, 2, 512], BF16)
        sqo = sb("sqo", [128, 2, 512], BF16)
        rro = sb("rro", [128, 2, 512], F32)
        rr = sb("rr", [128, 2, 512], F32)
        qkn = sb("qkn", [128, 2, 2, 512], F32)
        qkb = sb("qkb", [128, 2, 2, 512], BF16)
        zT = sb("zT", [128, 2, 2, 512], BF16)
        bat = sb("bat", [128, 2, 4, 4], F32)
        beta = sb("beta", [128, 2, 4, 2], F32)
        esp = sb("esp", [128, 2, 4, 2], F32)
        gg = sb("gg", [128, 2, 4, 2], F32)
        gcs = sb("gcs", [128, 2, 4, 2], F32)
        Rm = sb("Rm", [128, 2, 4, 128], F32)
        tdm = sb("tdm", [128, 2, 4, 128], F32)
        egcB = sb("egcB", [128, 2, 4, 128], F32)
        W4 = sb("W4", [128, 2, 4, 128], F32)
        Am = sb("Am", [128, 2, 4, 128], BF16)
        X = sb("X", [128, 2, 2, 4, 3, 128], BF16)
        P0f = sb("P0f", [128, 2, 4, 128], F32)
        ILf = sb("ILf", [128, 2, 4, 128], F32)
        Ttf = sb("Ttf", [128, 2, 4, 128], F32)
        Ttq = sb("Ttq", [128, 2, 4, 128], BF16)
        RpT = sb("RpT", [128, 2, 4, 128], BF16)
        Tnb = sb("Tnb", [128, 2, 4, 128], BF16)
        glr = sb("glr", [128, 2, 4], F32)
        gl = sb("gl", [128, 2, 2, 4], F32)
        ekd = sb("ekd", [128, 2, 4], F32)
        egc = sb("egc", [128, 2, 4], F32)
        bg = sb("bg", [128, 2, 4], F32)
        Ttb = sb("Ttb", [128, 2, 4, 128], BF16)
        ATb = sb("ATb", [128, 2, 2, 4, 128], BF16)
        qdT = sb("qdT", [128, 2, 2, 4, 128], BF16)
        kd = sb("kd", [128, 2, 2, 4, 128], BF16)
        kbg = sb("kbg", [128, 2, 4, 128], BF16)
        vb = sb("vb", [128, 2, 4, 128], BF16)
        uu = sb("uu", [128, 2, 2, 4, 128], F32)
        wTb = sb("wTb", [128, 2, 2, 4, 128], BF16)
        vnb = sb("vnb", [128, 2, 128], BF16)
        Sst = sb("Sst", [128, 2, 128], F32)
        Sb = sb("Sb", [128, 2, 128], BF16)
        oT = sb("oT", [128, 2, 512], F32)
        sz = sb("sz", [128, 2, 512], F32)
        og = sb("og", [128, 2, 512], BF16)

        IDN, TRIU, NMI, STR = 0, 1, 2, 3
        B4 = [128, 4, 128]
        mask4 = lambda mi: cm[:, mi, :].unsqueeze(1).to_broadcast(B4)
        PROJ, MISC = 0, 1
        XB = lambda hd: 2 + hd
        YB = lambda hd: 4 + 2 * hd
        psX = lambda hd: ps[:, XB(hd), :].rearrange("p (t c) -> p t c", c=128)
        psY = lambda hd: ps[:, YB(hd):YB(hd) + 2, :].rearrange("p b (t c) -> p (b t) c", c=256)
        KX = lambda hd: [("ps", XB(hd))]
        KY = lambda hd: [("ps", YB(hd)), ("ps", YB(hd) + 1)]
        S = Sched()
        if pre_fn is not None:
            pre_fn(S)
        S.pool(lambda h: h.memset(ones_bf[:, :], 1.0), w=["ones_bf"])
        S.pool(lambda h: h.memset(ones_f[:, :], 1.0), w=["ones_f"])
        S.pool(lambda h: h.memset(eps_col[:, :], EPS), w=["eps_col"])
        S.pool(lambda h: h.memset(one_col[:, :], 1.0), w=["one_col"])
        S.pool(lambda h: h.memset(pre[:, :, :, :], 0.0), w=[("pre", a, b) for a in range(2) for b in range(3)])
        S.pool(lambda h: h.memset(Sst[:, :, :], 0.0), w=[("S", 0), ("S", 1)])
        S.pool(lambda h: h.memset(Sb[:, :, :], 0.0), w=[("Sb", 0), ("Sb", 1)])
        S.dma(lambda h: h.dma_start(out=sm[:, :], in_=sm_d[:, :]), w=["sm"])
        S.dma(lambda h: h.dma_start(out=cm[:, :, :], in_=cm_d[:, :, :]), w=["cm"])
        wv = w_my.rearrange("(c p) n -> p c n", p=128)
        for c in range(8):
            S.dma(lambda h, c=c: h.dma_start(out=wq[:, c, :], in_=wv[:, c, :]), w=[("wq", c)], q="pool")
        S.act(lambda h: h.activation(out=negA[:, :], in_=sm[:, 24:26], func=AF.Exp), r=["sm"], w=["negA"])
        S.dve(lambda h: h.tensor_scalar(out=negA[:, :], in0=negA[:, :], scalar1=-1.0, scalar2=None, op0=ALU.mult),
              r=["negA"], w=["negA"])
        for idx in range(24):
            S.dve(lambda h, idx=idx: h.tensor_scalar(out=dgw[:, idx, :], in0=cm[:, IDN, :], scalar1=sm[:, idx:idx + 1], scalar2=None,
                                                     op0=ALU.mult), r=["cm", "sm"], w=["dgw"])
        S.dve(lambda h: h.tensor_copy(out=idb[:, :], in_=cm[:, IDN, :]), r=["cm"], w=["idb"])
        WQ = [("wq", c) for c in range(8)]
        tsl = lambda ci: slice(ci * 128, (ci + 1) * 128)

        def prep(tg, hd, b, gp):
            beta_, gg_, gcs_ = beta[:, gp, :, :], gg[:, gp, :, :], gcs[:, gp, :, :]
            bcol = lambda t4: t4[:, :, hd:hd + 1].to_broadcast(B4)
            bvec = lambda v: v[:, hd, :].unsqueeze(2).to_broadcast(B4)
            for j in range(4):
                for c in range(8):
                    S.pe(lambda h, j=j, c=c: h.matmul(ps[:, PROJ, :], lhsT=wq[:, c, hd * 512 + j * 128:hd * 512 + (j + 1) * 128],
                                                      rhs=hTg[:, b, c, :], start=(c == 0), stop=(c == 7)),
                         r=[("hTg", b)] + WQ, w=[("ps", PROJ)])
                if j < 3:
                    S.act(lambda h, j=j: h.copy(out=pre[:, hd, j, 3:515], in_=ps[:, PROJ, :]), r=[("ps", PROJ)], w=[("pre", hd, j)])
                else:
                    S.act(lambda h: h.activation(out=zT[:, hd, gp, :], in_=ps[:, PROJ, :], func=AF.Silu), r=[("ps", PROJ)], w=[("zT", hd, gp)])
                yield
            for j in range(3):
                for k in range(4):
                    S.pe(lambda h, j=j, k=k: h.matmul(ps[:, MISC, :], lhsT=dgw[:, (hd * 3 + j) * 4 + k, :], rhs=pre[:, hd, j, k:k + 512],
                                                      start=(k == 0), stop=(k == 3)), r=[("pre", hd, j), "dgw"], w=[("ps", MISC)])
                S.pool(lambda h, j=j: h.tensor_copy(out=pre[:, hd, j, 0:3], in_=pre[:, hd, j, 512:515]),
                       r=[("pre", hd, j)], w=[("pre", hd, j)])
                if j < 2:
                    S.act(lambda h, j=j: h.activation(out=cs[:, hd, j, :], in_=ps[:, MISC, :], func=AF.Silu),
                          r=[("ps", MISC)], w=[("cs", hd, j)])
                else:
                    S.act(lambda h: h.activation(out=vbf[:, hd, :], in_=ps[:, MISC, :], func=AF.Silu),
                          r=[("ps", MISC)], w=[("vbf", hd)])
                yield
            for j in range(2):
                S.act(lambda h, j=j: h.activation(out=sqb[:, hd, :], in_=cs[:, hd, j, :], func=AF.Square), r=[("cs", hd, j)], w=[("sqb", hd)])
                S.pe(lambda h: h.matmul(ps[:, PROJ, :], lhsT=ones_bf[:, :], rhs=sqb[:, hd, :], start=True, stop=True),
                     r=[("sqb", hd), "ones_bf"], w=[("ps", PROJ)])
                S.act(lambda h: h.activation(out=rr[:, hd, :], in_=ps[:, PROJ, :], func=AF.Ln, bias=eps_col[:, 0:1]),
                      r=[("ps", PROJ), "eps_col"], w=[("rr", hd)])
                S.act(lambda h: h.activation(out=rr[:, hd, :], in_=rr[:, hd, :], func=AF.Exp, scale=-0.5), r=[("rr", hd)], w=[("rr", hd)])
                sc = (128.0 ** -0.5) if j == 0 else 1.0
                S.dve(lambda h, j=j, sc=sc: h.scalar_tensor_tensor(out=qkn[:, hd, j, :], in0=cs[:, hd, j, :], scalar=sc, in1=rr[:, hd, :],
                                                                   op0=ALU.mult, op1=ALU.mult),
                      r=[("cs", hd, j), ("rr", hd)], w=[("qkn", hd, j)])
                S.act(lambda h, j=j: h.copy(out=qkb[:, hd, j, :], in_=qkn[:, hd, j, :]), r=[("qkn", hd, j)], w=[("qkb", hd, j)])
                yield
            if cut == 1:
                return
            q4 = qkn[:, hd, 0, :].rearrange("p (t c) -> p t c", c=128)
            S.dve(lambda h: h.tensor_tensor(out=Rm[:, hd, :, :], in0=mask4(TRIU), in1=bcol(gg_), op=ALU.mult),
                  r=["cm", ("gg", hd, gp)], w=[("Rm", hd)])
            S.pe(lambda h: h.matmul(ps[:, XB(hd), :], lhsT=ones_f[:, :], rhs=Rm[:, hd, :, :].rearrange("p t c -> p (t c)"),
                                    start=True, stop=True), r=[("Rm", hd), "ones_f"], w=KX(hd))
            yield
            S.dve(lambda h: h.tensor_tensor(out=tdm[:, hd, :, :], in0=psX(hd), in1=bcol(gcs_), op=ALU.subtract),
                  r=KX(hd) + [("gcs", gp)], w=[("tdm", hd)])
            S.act(lambda h: h.activation(out=egcB[:, hd, :, :], in_=psX(hd), func=AF.Exp), r=KX(hd), w=[("egcB", hd)])
            S.act(lambda h: h.copy(out=glr[:, hd, :].unsqueeze(2), in_=psX(hd)[:, :, 127:128]), r=KX(hd), w=[("glr", hd)])
            S.dve(lambda h: h.tensor_tensor(out=tdm[:, hd, :, :], in0=tdm[:, hd, :, :], in1=mask4(NMI), op=ALU.max),
                  r=[("tdm", hd), "cm"], w=[("tdm", hd)])
            S.act(lambda h: h.activation(out=tdm[:, hd, :, :], in_=tdm[:, hd, :, :], func=AF.Exp, scale=-1.0),
                  r=[("tdm", hd)], w=[("tdm", hd)])
            S.act(lambda h: h.activation(out=gl[:, hd, gp, :], in_=glr[:, hd, :], func=AF.Exp), r=[("glr", hd)], w=[("gl", hd, gp)])
            S.dve(lambda h: h.tensor_tensor(out=ekd[:, hd, :], in0=glr[:, hd, :], in1=gcs_[:, :, hd], op=ALU.subtract),
                  r=[("glr", hd), ("gcs", gp)], w=[("ekd", hd)])
            S.act(lambda h: h.activation(out=ekd[:, hd, :], in_=ekd[:, hd, :], func=AF.Exp), r=[("ekd", hd)], w=[("ekd", hd)])
            S.act(lambda h: h.activation(out=egc[:, hd, :], in_=gcs_[:, :, hd], func=AF.Exp), r=[("gcs", gp)], w=[("egc", hd)])
            S.dve(lambda h: h.tensor_tensor(out=bg[:, hd, :], in0=egc[:, hd, :], in1=beta_[:, :, hd], op=ALU.mult),
                  r=[("egc", hd), ("beta", gp)], w=[("bg", hd)])
            S.dve(lambda h: h.tensor_tensor(out=qdT[:, hd, gp, :, :], in0=q4, in1=egcB[:, hd, :, :], op=ALU.mult),
                  r=[("qkn", hd, 0), ("egcB", hd)], w=[("qdT", hd, gp)])
            S.dve(lambda h: h.tensor_tensor(out=W4[:, hd, :, :], in0=tdm[:, hd, :, :], in1=mask4(STR), op=ALU.mult),
                  r=[("tdm", hd), "cm"], w=[("W4", hd)])
            S.dve(lambda h: h.scalar_tensor_tensor(out=W4[:, hd, :, :], in0=W4[:, hd, :, :], scalar=-1.0, in1=bcol(beta_),
                                                   op0=ALU.mult, op1=ALU.mult), r=[("W4", hd), ("beta", gp)], w=[("W4", hd)])
            yield
            for ci in range(4):
                S.pe(lambda h, ci=ci: h.matmul(psX(hd)[:, ci, :], lhsT=qkb[:, hd, 1, tsl(ci)], rhs=qkb[:, hd, 1, tsl(ci)],
                                               start=True, stop=True), r=[("qkb", hd, 1)], w=KX(hd))
            S.dve(lambda h: h.tensor_tensor(out=P0f[:, hd, :, :], in0=psX(hd), in1=W4[:, hd, :, :], op=ALU.mult),
                  r=KX(hd) + [("W4", hd)], w=[("P0f", hd)])
            S.act(lambda h: h.copy(out=X[:, hd, 0, :, 0, :], in_=P0f[:, hd, :, :]), r=[("P0f", hd)], w=[("XP", hd, 0)])
            S.dve(lambda h: h.scalar_tensor_tensor(out=ILf[:, hd, :, :], in0=P0f[:, hd, :, :], scalar=-1.0, in1=mask4(IDN),
                                                   op0=ALU.mult, op1=ALU.add), r=[("P0f", hd), "cm"], w=[("ILf", hd)])
            yield
            for ci in range(4):
                S.pe(lambda h, ci=ci: h.matmul(psX(hd)[:, ci, :], lhsT=qkb[:, hd, 0, tsl(ci)], rhs=qkb[:, hd, 1, tsl(ci)],
                                               start=True, stop=True), r=[("qkb", hd, 0), ("qkb", hd, 1)], w=KX(hd))
            S.dve(lambda h: h.tensor_tensor(out=Am[:, hd, :, :], in0=psX(hd), in1=tdm[:, hd, :, :], op=ALU.mult),
                  r=KX(hd) + [("tdm", hd)], w=[("Am", hd)])
            yield
            for ci in range(4):
                S.pe(lambda h, ci=ci: h.matmul(psX(hd)[:, ci, :], lhsT=P0f[:, hd, ci, :], rhs=cm[:, IDN, :], start=True, stop=True),
                     r=[("P0f", hd), "cm"], w=KX(hd))
            S.act(lambda h: h.copy(out=X[:, hd, 0, :, 1, :], in_=psX(hd)), r=KX(hd), w=[("XQ", hd, 0)])
            S.dve(lambda h: h.tensor_tensor(out=X[:, hd, 1, :, 2, :], in0=psX(hd), in1=mask4(IDN), op=ALU.add),
                  r=KX(hd) + ["cm"], w=[("XT", hd, 1)])
            yield
            for ci in range(4):
                S.pe(lambda h, ci=ci: h.matmul(psX(hd)[:, ci, :], lhsT=Am[:, hd, ci, :], rhs=idb[:, :], start=True, stop=True),
                     r=[("Am", hd), "idb"], w=KX(hd))
            S.act(lambda h: h.copy(out=ATb[:, hd, gp, :, :], in_=psX(hd)), r=KX(hd), w=[("ATb", hd, gp)])
            yield
            for ci in range(4):
                S.pe(lambda h, ci=ci: h.matmul(psX(hd)[:, ci, :], lhsT=qkb[:, hd, 1, tsl(ci)], rhs=idb[:, :], start=True, stop=True),
                     r=[("qkb", hd, 1), "idb"], w=KX(hd))
            S.dve(lambda h: h.tensor_tensor(out=kd[:, hd, gp, :, :], in0=psX(hd), in1=bvec(ekd), op=ALU.mult),
                  r=KX(hd) + [("ekd", hd)], w=[("kd", hd, gp)])
            S.dve(lambda h: h.tensor_tensor(out=kbg[:, hd, :, :], in0=psX(hd), in1=bvec(bg), op=ALU.mult),
                  r=KX(hd) + [("bg", hd)], w=[("kbg", hd)])
            yield
            for ci in range(4):
                S.pe(lambda h, ci=ci: h.matmul(psX(hd)[:, ci, :], lhsT=vbf[:, hd, tsl(ci)], rhs=idb[:, :], start=True, stop=True),
                     r=[("vbf", hd), "idb"], w=KX(hd))
            S.dve(lambda h: h.tensor_tensor(out=vb[:, hd, :, :], in0=psX(hd), in1=bcol(beta_), op=ALU.mult),
                  r=KX(hd) + [("beta", gp)], w=[("vb", hd)])
            yield
            NL = 5
            for m in range(NL):
                cur, nxt = m % 2, (m + 1) % 2
                for ci in range(4):
                    if m == 0:
                        S.pe(lambda h, ci=ci: h.matmul(psY(hd)[:, ci, 0:128], lhsT=X[:, hd, 0, ci, 0, :], rhs=X[:, hd, 0, ci, 1, :],
                                                       start=True, stop=True), r=[("XP", hd, 0), ("XQ", hd, 0)], w=KY(hd))
                    elif m < NL - 1:
                        S.pe(lambda h, ci=ci, cur=cur: h.matmul(psY(hd)[:, ci, :], lhsT=X[:, hd, cur, ci, 0, :],
                                                                rhs=X[:, hd, cur, ci, 1:3, :].rearrange("p a b -> p (a b)"),
                                                                start=True, stop=True),
                             r=[("XP", hd, cur), ("XQ", hd, cur), ("XT", hd, cur)], w=KY(hd))
                    else:
                        S.pe(lambda h, ci=ci, cur=cur: h.matmul(psY(hd)[:, ci, 128:256], lhsT=X[:, hd, cur, ci, 0, :],
                                                                rhs=X[:, hd, cur, ci, 2, :], start=True, stop=True),
                             r=[("XP", hd, cur), ("XT", hd, cur)], w=KY(hd))
                if m < NL - 1:
                    for ci in range(4):
                        S.pe(lambda h, ci=ci, cur=cur: h.matmul(psX(hd)[:, ci, :], lhsT=X[:, hd, cur, ci, 1, :], rhs=X[:, hd, cur, ci, 0, :],
                                                                start=True, stop=True), r=[("XP", hd, cur), ("XQ", hd, cur)], w=KX(hd))
                    S.act(lambda h, nxt=nxt: h.copy(out=X[:, hd, nxt, :, 1, :], in_=psY(hd)[:, :, 0:128]), r=KY(hd), w=[("XQ", hd, nxt)])
                    S.act(lambda h, nxt=nxt: h.copy(out=X[:, hd, nxt, :, 0, :], in_=psX(hd)), r=KX(hd), w=[("XP", hd, nxt)])
                if 1 <= m < NL - 1:
                    S.dve(lambda h, cur=cur, nxt=nxt: h.tensor_tensor(out=X[:, hd, nxt, :, 2, :], in0=psY(hd)[:, :, 128:256],
                                                                      in1=X[:, hd, cur, :, 2, :], op=ALU.add),
                          r=KY(hd) + [("XT", hd, cur)], w=[("XT", hd, nxt)])
                if m == NL - 1:
                    S.dve(lambda h, cur=cur: h.tensor_tensor(out=Ttf[:, hd, :, :], in0=psY(hd)[:, :, 128:256], in1=X[:, hd, cur, :, 2, :],
                                                             op=ALU.add), r=KY(hd) + [("XT", hd, cur)], w=[("Ttf", hd)])
                    S.act(lambda h: h.copy(out=Ttq[:, hd, :, :], in_=Ttf[:, hd, :, :]), r=[("Ttf", hd)], w=[("Ttq", hd)])
                yield
            for ci in range(4):
                S.pe(lambda h, ci=ci: h.matmul(psX(hd)[:, ci, :], lhsT=ILf[:, hd, ci, :], rhs=Ttf[:, hd, ci, :], start=True, stop=True),
                     r=[("ILf", hd), ("Ttf", hd)], w=KX(hd))
            for ci in range(4):
                S.pe(lambda h, ci=ci: h.matmul(psY(hd)[:, ci, 0:128], lhsT=Ttq[:, hd, ci, :], rhs=idb[:, :], start=True, stop=True),
                     r=[("Ttq", hd), "idb"], w=KY(hd))
            S.dve(lambda h: h.scalar_tensor_tensor(out=RpT[:, hd, :, :], in0=psX(hd), scalar=-1.0, in1=mask4(IDN),
                                                   op0=ALU.mult, op1=ALU.add), r=KX(hd) + ["cm"], w=[("RpT", hd)])
            S.act(lambda h: h.copy(out=Tnb[:, hd, :, :], in_=psY(hd)[:, :, 0:128]), r=KY(hd), w=[("Tnb", hd)])
            yield
            for ci in range(4):
                S.pe(lambda h, ci=ci: h.matmul(psX(hd)[:, ci, :], lhsT=Tnb[:, hd, ci, :], rhs=RpT[:, hd, ci, :], start=True, stop=True),
                     r=[("Tnb", hd), ("RpT", hd)], w=KX(hd))
            S.dve(lambda h: h.tensor_tensor(out=Ttb[:, hd, :, :], in0=psX(hd), in1=Ttf[:, hd, :, :], op=ALU.add),
                  r=KX(hd) + [("Ttf", hd)], w=[("Ttb", hd)])
            yield
            for ci in range(4):
                S.pe(lambda h, ci=ci: h.matmul(psX(hd)[:, ci, :], lhsT=Ttb[:, hd, ci, :], rhs=vb[:, hd, ci, :], start=True, stop=True),
                     r=[("Ttb", hd), ("vb", hd)], w=KX(hd))
            S.act(lambda h: h.copy(out=uu[:, hd, gp, :, :], in_=psX(hd)), r=KX(hd), w=[("uu", hd, gp)])
            for ci in range(4):
                S.pe(lambda h, ci=ci: h.matmul(psY(hd)[:, ci, 0:128], lhsT=kbg[:, hd, ci, :], rhs=Ttb[:, hd, ci, :], start=True, stop=True),
                     r=[("Ttb", hd), ("kbg", hd)], w=KY(hd))
            S.act(lambda h: h.copy(out=wTb[:, hd, gp, :, :], in_=psY(hd)[:, :, 0:128]), r=KY(hd), w=[("wTb", hd, gp)])
            yield

        def scan_out(tg, hd, gp):
            sbk = YB(hd) + 1
            KS = [("ps", sbk)]
            for ci in range(4):
                S.pe(lambda h, ci=ci: h.matmul(ps[:, sbk, 128:256], lhsT=wTb[:, hd, gp, ci, :], rhs=Sb[:, hd, :], start=True, stop=True),
                     r=[("wTb", hd, gp), ("Sb", hd)], w=KS)
                S.dve(lambda h, ci=ci: h.tensor_tensor(out=vnb[:, hd, :], in0=uu[:, hd, gp, ci, :], in1=ps[:, sbk, 128:256], op=ALU.subtract),
                      r=[("uu", hd, gp)] + KS, w=[("vnb", hd)])
                S.pe(lambda h, ci=ci: h.matmul(ps[:, sbk, 384:512], lhsT=Sb[:, hd, :], rhs=qdT[:, hd, gp, ci, :], start=True, stop=False),
                     r=[("Sb", hd), ("qdT", hd, gp)], w=KS)
                S.pe(lambda h, ci=ci: h.matmul(ps[:, sbk, 384:512], lhsT=vnb[:, hd, :], rhs=ATb[:, hd, gp, ci, :], start=False, stop=True),
                     r=[("vnb", hd), ("ATb", hd, gp)], w=KS)
                S.pe(lambda h, ci=ci: h.matmul(ps[:, sbk, 256:384], lhsT=kd[:, hd, gp, ci, :], rhs=vnb[:, hd, :], start=True, stop=True),
                     r=[("kd", hd, gp), ("vnb", hd)], w=KS)
                S.act(lambda h, ci=ci: h.copy(out=oT[:, hd, tsl(ci)], in_=ps[:, sbk, 384:512]), r=KS, w=[("oT", hd)])
                S.dve(lambda h, ci=ci: h.scalar_tensor_tensor(out=Sst[:, hd, :], in0=Sst[:, hd, :], scalar=gl[:, hd, gp, ci:ci + 1],
                                                              in1=ps[:, sbk, 256:384], op0=ALU.mult, op1=ALU.add),
                      r=[("S", hd), ("gl", hd, gp)] + KS, w=[("S", hd)])
                S.act(lambda h: h.copy(out=Sb[:, hd, :], in_=Sst[:, hd, :]), r=[("S", hd)], w=[("Sb", hd)])
                yield
            S.act(lambda h: h.activation(out=sqo[:, hd, :], in_=oT[:, hd, :], func=AF.Square), r=[("oT", hd)], w=[("sqo", hd)])
            S.pe(lambda h: h.matmul(ps[:, PROJ, :], lhsT=ones_bf[:, :], rhs=sqo[:, hd, :], start=True, stop=True),
                 r=[("sqo", hd), "ones_bf"], w=[("ps", PROJ)])
            S.act(lambda h: h.activation(out=rro[:, hd, :], in_=ps[:, PROJ, :], func=AF.Ln, scale=1.0 / 128, bias=eps_col[:, 0:1]),
                  r=[("ps", PROJ), "eps_col"], w=[("rro", hd)])
            S.act(lambda h: h.activation(out=rro[:, hd, :], in_=rro[:, hd, :], func=AF.Exp, scale=-0.5), r=[("rro", hd)], w=[("rro", hd)])
            S.dve(lambda h: h.scalar_tensor_tensor(out=oT[:, hd, :], in0=oT[:, hd, :], scalar=sm[:, 28:29], in1=rro[:, hd, :],
                                                   op0=ALU.mult, op1=ALU.mult), r=[("oT", hd), "sm", ("rro", hd)], w=[("oT", hd)])
            S.dve(lambda h: h.tensor_tensor(out=og[:, hd, :], in0=oT[:, hd, :], in1=zT[:, hd, gp, :], op=ALU.mult),
                  r=[("oT", hd), ("zT", hd, gp)], w=[("og", hd)])
            S.dma(lambda h: h.dma_start(out=og_dst(hd, tg), in_=og[:, hd, :]), r=[("og", hd)],
                  w=([og_key(tg)] if og_key is not None else []))
            done_units.add((tg, hd))
            if post_fn is not None and (tg, 0) in done_units and (tg, 1) in done_units:
                post_fn(S, tg)
            yield

        def load_h(tg):
            b = tg % 3
            hk = h_keys(tg) if callable(h_keys) else list(h_keys)
            S.dma(lambda h, b=b, tg=tg: h.dma_start(out=hTg[:, b, :, :], in_=h_src(tg)), r=hk, w=[("hTg", b)])

        def preamble(tg):
            b = tg % 3
            gq = tg % 2
            if tg == 0:
                load_h(0)
                if NG > 1:
                    load_h(1)
            if tg + 2 < NG:
                load_h(tg + 2)
            for t in range(4):
                for c in range(8):
                    S.pe(lambda h, b=b, t=t, c=c: h.matmul(ps[:, MISC, t * 4:(t + 1) * 4], lhsT=hTg[:, b, c, t * 128:(t + 1) * 128],
                                                            rhs=wq[:, c, 1024:1028], start=(c == 0), stop=(c == 7)),
                         r=[("hTg", b)] + WQ, w=[("ps", MISC)])
            S.act(lambda h: h.copy(out=bat[:, gq, :, :], in_=ps[:, MISC, 0:16].rearrange("p (t f) -> p t f", f=4)),
                  r=[("ps", MISC)], w=[("bat", gq)])
            S.act(lambda h: h.activation(out=beta[:, gq, :, :], in_=bat[:, gq, :, 0:2], func=AF.Exp, scale=-1.0), r=[("bat", gq)], w=[("beta", gq)])
            S.dve(lambda h: h.tensor_scalar(out=beta[:, gq, :, :], in0=beta[:, gq, :, :], scalar1=1.0, scalar2=None, op0=ALU.add),
                  r=[("beta", gq)], w=[("beta", gq)])
            S.dve(lambda h: h.reciprocal(out=beta[:, gq, :, :], in_=beta[:, gq, :, :]), r=[("beta", gq)], w=[("beta", gq)])
            for hd in range(2):
                S.act(lambda h, hd=hd: h.activation(out=esp[:, gq, :, hd:hd + 1], in_=bat[:, gq, :, 2 + hd:3 + hd], func=AF.Exp,
                                                    bias=sm[:, 26 + hd:27 + hd]), r=[("bat", gq), "sm"], w=[("esp", hd, gq)])
                S.act(lambda h, hd=hd: h.activation(out=esp[:, gq, :, hd:hd + 1], in_=esp[:, gq, :, hd:hd + 1], func=AF.Ln,
                                                    bias=one_col[:, 0:1]), r=[("esp", hd, gq), "one_col"], w=[("esp", hd, gq)])
                S.dve(lambda h, hd=hd: h.tensor_scalar(out=gg[:, gq, :, hd:hd + 1], in0=esp[:, gq, :, hd:hd + 1],
                                                       scalar1=negA[:, hd:hd + 1], scalar2=None, op0=ALU.mult),
                      r=[("esp", hd, gq), "negA"], w=[("gg", hd, gq)])
            for t in range(4):
                S.pe(lambda h, t=t: h.matmul(ps[:, MISC, 32 + 2 * t:34 + 2 * t], lhsT=cm[:, TRIU, :], rhs=gg[:, gq, t, :],
                                             start=True, stop=True),
                     r=["cm", ("gg", 0, gq), ("gg", 1, gq)], w=[("ps", MISC)])
            S.act(lambda h: h.copy(out=gcs[:, gq, :, :], in_=ps[:, MISC, 32:40].rearrange("p (t f) -> p t f", f=2)),
                  r=[("ps", MISC)], w=[("gcs", gq)])

        done_units = set()

        def stream(hd):
            for tg in range(NG):
                if hd == 0:
                    preamble(tg)
                    yield
                yield from prep(tg, hd, tg % 3, tg % 2)
                yield from scan_out(tg, hd, tg % 2)

        LAG = 14
        g0, g1 = stream(0), stream(1)
        alive0 = alive1 = True
        for _ in range(LAG):
            alive0 = next(g0, StopIteration) is not StopIteration
        while alive0 or alive1:
            if alive1:
                alive1 = next(g1, StopIteration) is not StopIteration
            if alive0:
                alive0 = next(g0, StopIteration) is not StopIteration
        emit_phase(nc, S, ss)


def gdn_masks():
    i = np.arange(128)
    idn = (i[:, None] == i[None, :]).astype(np.float32)
    triu = (i[:, None] <= i[None, :]).astype(np.float32)
    nmi = np.where(i[:, None] >= i[None, :], 0.0, 30000.0).astype(np.float32)
    strict = (i[:, None] > i[None, :]).astype(np.float32)
    return np.ascontiguousarray(np.stack([idn, triu, nmi, strict], axis=1))


def l2_inputs(hT_b, inp):
    w_in = np.asarray(inp["a_w_in"][0], np.float32)
    w_conv = np.asarray(inp["a_w_conv"][0], np.float32)
    cm = gdn_masks()
    maps = []
    for c in range(NCORES):
        bsel, hp = c // 4, c % 4
        cols = []
        sm = np.zeros((128, 32), np.float32)
        for hd in range(2):
            hh = hp * 2 + hd
            for j in range(3):
                cols.append(w_in[:, j * 1024 + hh * 128:j * 1024 + (hh + 1) * 128])
                for k in range(4):
                    sm[:, (hd * 3 + j) * 4 + k] = w_conv[k, j * 1024 + hh * 128:j * 1024 + (hh + 1) * 128]
            cols.append(w_in[:, 3072 + hh * 128:3072 + (hh + 1) * 128])
            sm[:, 24 + hd] = inp["a_A_log"][0][hh]
            sm[:, 26 + hd] = inp["a_dt_bias"][0][hh]
        sm[:, 28] = inp["a_out_norm"][0]
        for hd in range(2):
            cols.append(w_in[:, 4096 + hp * 2 + hd:4096 + hp * 2 + hd + 1])
        for hd in range(2):
            cols.append(w_in[:, 4104 + hp * 2 + hd:4104 + hp * 2 + hd + 1])
        w_my = np.ascontiguousarray(np.concatenate(cols, axis=1))
        assert w_my.shape == (1024, 1028)
        maps.append(dict(hT=hT_b[bsel], w_my=w_my, sm=sm, cm=cm))
    return maps


def proj_residual(S, C, w_dram, src, T, bias_cols=None, wkey="wg", extra_w=None):
    ps, xT, wg = C["ps"], C["xT"], C["wg"]
    wv = w_dram.rearrange("(c p) n -> p c n", p=128)
    for half in range(2):
        S.dma(lambda h, half=half: h.dma_start(out=wg[:, half, :, :], in_=wv[:, :, half * 512:(half + 1) * 512]),
              w=[(wkey, half, 0), (wkey, half, 1)], q="pool")
        for k in range(4):
            dc = half * 4 + k
            bs = 4 * (dc % 2)
            for kc in range(8):
                for tt in range(4):
                    S.pe(lambda h, half=half, k=k, kc=kc, tt=tt, bs=bs: h.matmul(
                        ps[:, bs + tt, :], lhsT=wg[:, half, kc, k * 128:(k + 1) * 128], rhs=src[:, kc, tt * 512:(tt + 1) * 512],
                        start=(kc == 0), stop=(kc == 7)),
                        r=[(wkey, half, 0), ("hT", kc, tt)], w=[("ps", bs + tt)] + list((extra_w or {}).get(bs + tt, [])))
            pv = ps[:, bs:bs + 4, :].rearrange("p a b -> p (a b)")
            rk = [("ps", bs + t) for t in range(4)] + [("xT", dc, t) for t in range(4)]
            wk = [("xT", dc, t) for t in range(4)]
            if bias_cols is None:
                S.dve(lambda h, dc=dc, pv=pv: h.tensor_tensor(out=xT[:, dc, :], in0=pv, in1=xT[:, dc, :], op=ALU.add), r=rk, w=wk)
            else:
                S.dve(lambda h, dc=dc, pv=pv: h.scalar_tensor_tensor(out=xT[:, dc, :], in0=pv, scalar=bias_cols[:, dc:dc + 1],
                                                                     in1=xT[:, dc, :], op0=ALU.add, op1=ALU.add),
                      r=rk + ["smalls"], w=wk)


def build_L3(T=2048):
    import contextlib
    nc = bass.Bass("TRN2", target_bir_lowering=False)
    x = nc.dram_tensor("xT_in", [D, T], F32, kind="ExternalInput").ap()
    og = nc.dram_tensor("ogT_in", [D, T], BF16, kind="ExternalInput").ap()
    smalls = nc.dram_tensor("smalls", [128, 24], F32, kind="ExternalInput").ap()
    w_out = nc.dram_tensor("w_out", [D, D], F32, kind="ExternalInput").ap()
    w_gu_a = nc.dram_tensor("w_gu_a", [D, 2 * DFF], F32, kind="ExternalInput").ap()
    w_down_a = nc.dram_tensor("w_down_a", [DFF, D], F32, kind="ExternalInput").ap()
    w_gu_b = nc.dram_tensor("w_gu_b", [D, 2 * DFF], F32, kind="ExternalInput").ap()
    w_down_b = nc.dram_tensor("w_down_b", [DFF, D], F32, kind="ExternalInput").ap()
    x_out = nc.dram_tensor("xT_out", [D, T], F32, kind="ExternalOutput").ap()
    h_out = nc.dram_tensor("hT_out", [D, T], BF16, kind="ExternalOutput").ap()
    with contextlib.ExitStack() as st:
        ss = SemState(nc, st)
        C = alloc_common(nc, st, T)
        alloc_ffn(nc, st, C, T)
        C["smalls"] = st.enter_context(sbt(nc, "smalls_sb", [128, 24], F32))
        S = Sched()
        init_consts(S, C)
        S.dma(lambda h: h.dma_start(out=C["smalls"][:, :], in_=smalls[:, :]), w=["smalls"], q="sp")
        load_xT(S, C, x, T)
        ogv = og.rearrange("(c p) t -> p c t", p=128)
        for tt in range(4):
            S.dma(lambda h, tt=tt: h.dma_start(out=C["hT"][:, :, tt * 512:(tt + 1) * 512], in_=ogv[:, :, tt * 512:(tt + 1) * 512]),
                  w=[("hT", c, tt) for c in range(8)], q="sp")
        proj_residual(S, C, w_out, C["hT"], T)
        rmsnorm_fm(S, C, C["xT"], C["hT"], C["smalls"][:, 0:8], T, "n1")
        ffn_fm(S, C, C["xT"], C["hT"], w_gu_a, w_down_a, T, "fa")
        rmsnorm_fm(S, C, C["xT"], C["hT"], C["smalls"][:, 8:16], T, "n2")
        ffn_fm(S, C, C["xT"], C["hT"], w_gu_b, w_down_b, T, "fb")
        rmsnorm_fm(S, C, C["xT"], C["hT"], C["smalls"][:, 16:24], T, "n3")
        o1 = store_T(S, C, "xT", C["xT"], x_out, T)
        o2 = store_T(S, C, "hT", C["hT"], h_out, T)
        emit_phase(nc, S, ss, final_wait=o1 + o2)
    return nc


def build_L4(T=2048):
    import contextlib
    nc = bass.Bass("TRN2", target_bir_lowering=False)
    TH = T + 128
    A = dict(
        x=nc.dram_tensor("xT_in", [D, T], F32, kind="ExternalInput").ap(),
        hh=nc.dram_tensor("hTh_in", [D, TH], BF16, kind="ExternalInput").ap(),
        smalls=nc.dram_tensor("smalls", [128, 40], F32, kind="ExternalInput").ap(),
        bq64=nc.dram_tensor("bq64", [64, 20], F32, kind="ExternalInput").ap(),
        bvb=nc.dram_tensor("bvb", [128, 256], F32, kind="ExternalInput").ap(),
        masks=nc.dram_tensor("masks", [128, 3, 512], BF16, kind="ExternalInput").ap(),
        w_in=nc.dram_tensor("w_in", [D, 1536], F32, kind="ExternalInput").ap(),
        w_out=nc.dram_tensor("w_out", [D, D], F32, kind="ExternalInput").ap(),
        w_gu=nc.dram_tensor("w_gu", [D, 2 * DFF], F32, kind="ExternalInput").ap(),
        w_down=nc.dram_tensor("w_down", [DFF, D], F32, kind="ExternalInput").ap(),
        y_out=nc.dram_tensor("yT_out", [D, T], F32, kind="ExternalOutput").ap())
    with contextlib.ExitStack() as st:
        ss = SemState(nc, st)
        C = alloc_common(nc, st, T)
        C["xT"] = st.enter_context(sbt(nc, "xT", [128, 8, T], F32))
        l4_phases(nc, ss, C, A, T, fused=False)
    return nc


def l4_phases(nc, ss, C, A, T, fused, pre_fn=None):
    import contextlib
    TH = T + 128
    NB = T // 128
    x, hh, smalls, bq64, bvb, masks = A.get("x"), A.get("hh"), A["smalls"], A["bq64"], A["bvb"], A["masks"]
    w_in, w_out, w_gu, w_down, y_out = A["w_in"], A["w_out"], A["w_gu"], A["w_down"], A["y_out"]
    if True:
        st4 = contextlib.ExitStack()
        C["smalls"] = st4.enter_context(sbt(nc, "smalls4_sb", [128, 40], F32))
        ps, xT, sm = C["ps"], C["xT"], C["smalls"]
        with contextlib.ExitStack() as sa:
            sb = lambda name, shape, dt: sa.enter_context(sbt(nc, name, shape, dt))
            hTh = sb("hTh", [128, 8, TH], BF16)
            OT = sb("OT", [128, 8, T], BF16)
            wk3 = sb("wk3", [128, 2, 8, 384], BF16)
            qT = sb("qT", [64, 4, T], BF16)
            kT = sb("kT", [64, TH], BF16)
            V = sb("V", [128, NB + 1, 2, 128], BF16)
            expT = sb("expT", [128, 2, 2, 512], BF16)
            mk = sb("mk", [128, 3, 512], BF16)
            bq = sb("bq", [64, 20], F32)
            bv = sb("bv", [128, 256], F32)
            esk = sb("esk", [128, 8], F32)
            onesLH = sb("onesLH", [128, 2, 128], BF16)
            rec = sb("rec", [128, 2, 128], F32)
            wg = sb("wgA", [128, 2, 8, 512], BF16)
            C["wg"] = wg
            S = Sched()
            init_consts(S, C)
            S.dve(lambda h: h.memset(V[:, :, :, :], 0.0), w=["V"])
            S.dve(lambda h: h.memset(onesLH[:, :, :], 0.0), w=["onesLH"])
            S.dve(lambda h: h.memset(onesLH[:, 0, 0:64], 1.0), r=["onesLH"], w=["onesLH"])
            S.dve(lambda h: h.memset(onesLH[:, 1, 64:128], 1.0), r=["onesLH"], w=["onesLH"])
            S.dma(lambda h: h.dma_start(out=sm[:, :], in_=smalls[:, :]), w=["smalls"])
            S.dma(lambda h: h.dma_start(out=bq[:, :], in_=bq64[:, :]), w=["bq"])
            S.dma(lambda h: h.dma_start(out=bv[:, :], in_=bvb[:, :]), w=["bv"])
            S.dma(lambda h: h.dma_start(out=mk[:, :, :], in_=masks[:, :, :]), w=["mk"])
            if not fused:
                load_xT(S, C, x, T)
                hv = hh.rearrange("(c p) t -> p c t", p=128)
                S.dma(lambda h: h.dma_start(out=hTh[:, :, 0:1152], in_=hv[:, :, 0:1152]), w=["hTh"])
                S.dma(lambda h: h.dma_start(out=hTh[:, :, 1152:TH], in_=hv[:, :, 1152:TH]), w=["hTh2"])
                HR = ["hTh", "hTh2"]
            else:
                pre_fn(S)
                hv = A["h3_loc"].rearrange("(c p) t -> p c t", p=128)
                S.dma(lambda h: h.dma_start(out=hTh[:, :, 128:1152], in_=hv[:, :, 0:1024]), w=["hTh"])
                S.dma(lambda h: h.dma_start(out=hTh[:, :, 1152:TH], in_=hv[:, :, 1024:2048]), w=["hTh2"], q="act")
                hal = A["halo_all"].rearrange("(r c p) t -> p c r t", r=4, c=8, p=128)
                stg = wg[:, 0, :, :].rearrange("p c (r t) -> p c r t", t=128)
                for r_ in range(4):
                    S.dma(lambda h, r_=r_: h.dma_start(out=stg[:, :, r_, :], in_=hal[:, :, r_, :]), r=["halo_all"],
                          w=[("wgA", 0, 0), ("wgA", 0, 1)])
                S.dve(lambda h: h.tensor_scalar(out=hTh[:, :, 0:128], in0=stg[:, :, 0, :], scalar1=sm[:, 32:33], scalar2=None,
                                                op0=ALU.mult), r=[("wgA", 0, 0), "smalls"], w=["hTh3"])
                for r_ in range(1, 4):
                    S.dve(lambda h, r_=r_: h.scalar_tensor_tensor(out=hTh[:, :, 0:128], in0=stg[:, :, r_, :], scalar=sm[:, 32 + r_:33 + r_],
                                                                   in1=hTh[:, :, 0:128], op0=ALU.mult, op1=ALU.add),
                          r=[("wgA", 0, 0), "smalls", "hTh3"], w=["hTh3"])
                HR = ["hTh", "hTh2", "hTh3"]
            S.act(lambda h: h.activation(out=esk[:, :], in_=sm[:, 24:32], func=AF.Exp), r=["smalls"], w=["esk"])
            wv = w_in.rearrange("(c p) n -> p c n", p=128)
            for kvh in range(4):
                wb = kvh % 2
                S.dma(lambda h, wb=wb, kvh=kvh: h.dma_start(out=wk3[:, wb, :, 0:256], in_=wv[:, :, kvh * 256:(kvh + 1) * 256]),
                      w=[("wk3", wb, 0)], q="pool")
                S.dma(lambda h, wb=wb, kvh=kvh: h.dma_start(out=wk3[:, wb, :, 256:320], in_=wv[:, :, 1024 + kvh * 64:1024 + (kvh + 1) * 64]),
                      w=[("wk3", wb, 1)], q="pool")
                S.dma(lambda h, wb=wb, kvh=kvh: h.dma_start(out=wk3[:, wb, :, 320:384], in_=wv[:, :, 1280 + kvh * 64:1280 + (kvh + 1) * 64]),
                      w=[("wk3", wb, 2)], q="pool")
                for g in range(4):
                    for tt in range(4):
                        bank = tt % 2
                        for c in range(8):
                            S.pe(lambda h, wb=wb, g=g, tt=tt, c=c, bank=bank: h.matmul(
                                ps[0:64, bank, :], lhsT=wk3[:, wb, c, g * 64:(g + 1) * 64], rhs=hTh[:, c, 128 + tt * 512:128 + (tt + 1) * 512],
                                start=(c == 0), stop=(c == 7)), r=[("wk3", wb, 0)] + HR[:2], w=[("ps", bank)])
                        S.act(lambda h, g=g, tt=tt, bank=bank, kvh=kvh: h.activation(
                            out=qT[:, g, tt * 512:(tt + 1) * 512], in_=ps[0:64, bank, :], func=AF.Identity,
                            bias=bq[:, kvh * 4 + g:kvh * 4 + g + 1]), r=[("ps", bank), "bq"], w=[("qT", g)])
                for tt in (1, 2, 3, 4, 0):
                    wdt = 512 if tt < 4 else 128
                    bank = tt % 2
                    for c in range(8):
                        S.pe(lambda h, wb=wb, tt=tt, c=c, bank=bank, wdt=wdt: h.matmul(
                            ps[0:64, bank, 0:wdt], lhsT=wk3[:, wb, c, 256:320], rhs=hTh[:, c, tt * 512:tt * 512 + wdt],
                            start=(c == 0), stop=(c == 7)), r=[("wk3", wb, 1)] + (HR if tt == 0 else HR[:2]), w=[("ps", bank)])
                    S.act(lambda h, tt=tt, bank=bank, wdt=wdt, kvh=kvh: h.activation(
                        out=kT[:, tt * 512:tt * 512 + wdt], in_=ps[0:64, bank, 0:wdt], func=AF.Identity,
                        bias=bq[:, 16 + kvh:17 + kvh]), r=[("ps", bank), "bq"], w=["kT"])
                for blk in list(range(1, NB + 1)) + [0]:
                    bank = 2 + blk % 2
                    for c in range(8):
                        S.pe(lambda h, wb=wb, blk=blk, c=c, bank=bank: h.matmul(
                            ps[:, bank, 0:64], lhsT=hTh[:, c, blk * 128:(blk + 1) * 128], rhs=wk3[:, wb, c, 320:384],
                            start=(c == 0), stop=(c == 7)), r=[("wk3", wb, 2)] + (HR if blk == 0 else HR[:2]), w=[("ps", bank)])
                    S.dve(lambda h, blk=blk, bank=bank, kvh=kvh: h.tensor_tensor(
                        out=V[:, blk, 0, 0:64], in0=ps[:, bank, 0:64], in1=bv[:, kvh * 64:(kvh + 1) * 64], op=ALU.add),
                        r=[("ps", bank), "bv", "V"], w=[("V", blk, 0)])
                    S.dve(lambda h, blk=blk, bank=bank, kvh=kvh: h.tensor_tensor(
                        out=V[:, blk, 1, 64:128], in0=ps[:, bank, 0:64], in1=bv[:, kvh * 64:(kvh + 1) * 64], op=ALU.add),
                        r=[("ps", bank), "bv", "V"], w=[("V", blk, 1)])
                def att_stage1(n, kvh=kvh):
                    eb = n % 2
                    for kb in range(2):
                        bank = 4 + 2 * eb + kb
                        kcol = (n + kb) * 128
                        S.pe(lambda h, n=n, bank=bank, kcol=kcol: h.matmul(
                            ps[:, bank, :], lhsT=kT[:, kcol:kcol + 128], rhs=qT[:, :, n * 128:(n + 1) * 128],
                            start=True, stop=True), r=["kT"] + [("qT", g) for g in range(4)], w=[("ps", bank)])
                        S.act(lambda h, eb=eb, kb=kb, bank=bank: h.activation(
                            out=expT[:, eb, kb, :], in_=ps[:, bank, :], func=AF.Exp, scale=0.125),
                            r=[("ps", bank)], w=[("expT", eb, kb)])
                        mi = (2 if n == 0 else 0) if kb == 0 else 1
                        S.dve(lambda h, eb=eb, kb=kb, mi=mi: h.tensor_tensor(
                            out=expT[:, eb, kb, :], in0=expT[:, eb, kb, :], in1=mk[:, mi, :], op=ALU.mult),
                            r=[("expT", eb, kb), "mk"], w=[("expT", eb, kb)])

                def att_stage2(n, kvh=kvh):
                    eb = n % 2
                    for pair in range(2):
                        ch = kvh * 2 + pair
                        bk = 2 + pair
                        pso = ps[:, bk, 128:256]
                        psd = ps[:, bk, 256:384]
                        key = ("ps", bk)
                        for (dst, lo) in ((pso, None), (psd, 0)):
                            i = 0
                            for kb in range(2):
                                for gi in range(2):
                                    g = pair * 2 + gi
                                    if lo is None:
                                        lh = V[:, n + kb, gi, :]
                                        rk = [("V", n + kb, gi)]
                                    else:
                                        lh = onesLH[:, gi, :]
                                        rk = ["onesLH"]
                                    S.pe(lambda h, dst=dst, lh=lh, eb=eb, kb=kb, g=g, i=i: h.matmul(
                                        dst, lhsT=lh, rhs=expT[:, eb, kb, g * 128:(g + 1) * 128], start=(i == 0), stop=(i == 3)),
                                        r=rk + [("expT", eb, kb)], w=[key])
                                    i += 1
                        S.act(lambda h, pair=pair, psd=psd, ch=ch: h.activation(out=rec[:, pair, :], in_=psd, func=AF.Ln,
                                                                                bias=esk[:, ch:ch + 1]),
                              r=[key, "esk"], w=[("rec", pair)])
                        S.act(lambda h, pair=pair: h.activation(out=rec[:, pair, :], in_=rec[:, pair, :], func=AF.Exp, scale=-1.0),
                              r=[("rec", pair)], w=[("rec", pair)])
                        S.dve(lambda h, pair=pair, pso=pso, ch=ch, n=n: h.tensor_tensor(
                            out=OT[:, ch, n * 128:(n + 1) * 128], in0=pso, in1=rec[:, pair, :], op=ALU.mult),
                            r=[key, ("rec", pair)], w=[("hT", ch, n // 4)])

                att_stage1(0)
                for n in range(NB):
                    if n + 1 < NB:
                        att_stage1(n + 1)
                    att_stage2(n)
            proj_residual(S, C, w_out, OT, T, bias_cols=sm[:, 16:24], wkey="wgA")
            emit_phase(nc, S, ss)
        with contextlib.ExitStack() as sb_:
            C["hT"] = sb_.enter_context(sbt(nc, "hT", [128, 8, T], BF16))
            C["aT"] = sb_.enter_context(sbt(nc, "aT", [128, 11, T // 512, 512], BF16))
            C["sg"] = sb_.enter_context(sbt(nc, "sg", [128, 2, T // 512, 512], BF16))
            C["sqb"] = sb_.enter_context(sbt(nc, "sqb", [128, 8, 512], BF16))
            C["rstd"] = sb_.enter_context(sbt(nc, "rstd", [128, 2, 512], F32))
            C["wg"] = sb_.enter_context(sbt(nc, "wg", [128, 2, 8, 512], BF16))
            C["wd"] = sb_.enter_context(sbt(nc, "wd", [128, 2, 11, 256], BF16))
            S = Sched()
            rmsnorm_fm(S, C, xT, C["hT"], sm[:, 0:8], T, "n1")
            ffn_fm(S, C, xT, C["hT"], w_gu, w_down, T, "f2")
            sqb, rstd, ones_bf = C["sqb"], C["rstd"], C["ones_bf"]
            for tt in range(4):
                sl = slice(tt * 512, (tt + 1) * 512)
                bank = tt % 2
                for c in range(8):
                    S.act(lambda h, c=c, sl=sl: h.activation(out=sqb[:, c, :], in_=xT[:, c, sl], func=AF.Square),
                          r=[("xT", c, tt)], w=[("sqb", c)])
                for c in range(8):
                    S.pe(lambda h, c=c, bank=bank: h.matmul(ps[:, bank, :], lhsT=ones_bf[:, :], rhs=sqb[:, c, :],
                                                             start=(c == 0), stop=(c == 7)), r=[("sqb", c), "ones_bf"], w=[("ps", bank)])
                S.act(lambda h, bank=bank: h.activation(out=rstd[:, bank, :], in_=ps[:, bank, :], func=AF.Ln, scale=1.0 / D,
                                                         bias=C["eps_col"][:, 0:1]), r=[("ps", bank), "eps_col"], w=[("rstd", bank)])
                S.act(lambda h, bank=bank: h.activation(out=rstd[:, bank, :], in_=rstd[:, bank, :], func=AF.Exp, scale=-0.5),
                      r=[("rstd", bank)], w=[("rstd", bank)])
                for c in range(8):
                    S.dve(lambda h, c=c, sl=sl, bank=bank: h.scalar_tensor_tensor(
                        out=xT[:, c, sl], in0=xT[:, c, sl], scalar=sm[:, 8 + c:9 + c], in1=rstd[:, bank, :], op0=ALU.mult, op1=ALU.mult),
                        r=[("xT", c, tt), ("rstd", bank), "smalls"], w=[("xT", c, tt)])
            o1 = store_T(S, C, "xT", xT, y_out, T)
            emit_phase(nc, S, ss, final_wait=o1)
        st4.close()


def l4_consts():
    kj = np.arange(128)[:, None]
    qi = np.arange(128)[None, :]
    mp = (kj > qi).astype(np.float32)
    mc = (kj <= qi).astype(np.float32)
    tile4 = lambda m: np.tile(m, (1, 4))
    return tile4(mp), tile4(mc)


def l4_inputs(x3, h3, inp):
    T = 2048
    bf = ml_dtypes.bfloat16
    cores = list(range(NCORES))
    sinks = np.asarray(inp["b_sinks"][0], np.float32)
    sk = np.zeros((128, 8), np.float32)
    for ch in range(8):
        sk[0:64, ch] = sinks[2 * ch]
        sk[64:128, ch] = sinks[2 * ch + 1]
    sm4 = np.concatenate([col8(inp["ffn2_norm"][1]), col8(inp["final_norm"]), col8(inp["b_b_out"][0]), sk, np.zeros((128, 8), np.float32)], axis=1)
    b_in = np.asarray(inp["b_b_in"][0], np.float32)
    bq64 = np.ascontiguousarray(b_in[:1280].reshape(20, 64).T)
    bvb = np.ascontiguousarray(np.tile(b_in[1280:1536][None, :], (128, 1)))
    mp, mc = l4_consts()
    maps = []
    for c in cores:
        own = np.asarray(h3[c])
        if c % 4 == 0:
            halo = np.zeros((D, 128), bf)
            m0 = np.zeros_like(mp)
        else:
            halo = np.asarray(h3[c - 1])[:, T - 128:]
            m0 = mp
        masks = np.ascontiguousarray(np.stack([mp, mc, m0], axis=1)).astype(bf)
        maps.append(dict(xT_in=x3[c], hTh_in=np.ascontiguousarray(np.concatenate([halo, own], axis=1)),
                         smalls=sm4, bq64=bq64, bvb=bvb, masks=masks, w_in=inp["b_w_in"][0], w_out=inp["b_w_out"][0],
                         w_gu=inp["ffn2_w_gu"][1], w_down=inp["ffn2_w_down"][1]))
    return maps


RG = [[0, 1, 2, 3], [4, 5, 6, 7]]


def build_fused(T=2048):
    import contextlib
    nc = bass.Bass("TRN2", target_bir_lowering=False)
    ext = lambda name, shape, dt: nc.dram_tensor(name, shape, dt, kind="ExternalInput").ap()
    x = ext("xT_in", [D, T], F32)
    sm1_d = ext("sm1", [128, 16], F32)
    sm2_d = ext("sm", [128, 32], F32)
    cm_d = ext("cm", [128, 4, 128], F32)
    w_my = ext("w_my", [D, 1028], F32)
    sm3_d = ext("sm3", [128, 28], F32)
    W = {}
    for l in range(2):
        for k in (1, 2):
            W["gu%d%d" % (l, k)] = ext("w_gu_%d_%d" % (l, k), [D, 2 * DFF], F32)
            W["dn%d%d" % (l, k)] = ext("w_down_%d_%d" % (l, k), [DFF, D], F32)
    a_w_out = ext("a_w_out", [D, D], F32)
    A = dict(smalls=ext("sm4", [128, 40], F32), bq64=ext("bq64", [64, 20], F32), bvb=ext("bvb", [128, 256], F32),
             masks=ext("masks", [128, 3, 512], BF16), w_in=ext("b_w_in", [D, 1536], F32), w_out=ext("b_w_out", [D, D], F32),
             w_gu=W["gu12"], w_down=W["dn12"],
             y_out=nc.dram_tensor("yT_out", [D, T], F32, kind="ExternalOutput").ap())
    h1_loc = [nc.dram_tensor("h1_loc%d" % i, [D, 512], BF16) for i in range(4)]
    h1_all = [nc.dram_tensor("h1_all%d" % i, [4 * D, 512], BF16) for i in range(4)]
    og_loc = [nc.dram_tensor("og_loc%d" % i, [D, 512], BF16) for i in range(4)]
    og_all = [nc.dram_tensor("og_all%d" % i, [4 * D, 512], BF16) for i in range(4)]
    h3_loc = nc.dram_tensor("h3_loc", [D, T], BF16)
    halo_loc = nc.dram_tensor("halo_loc", [D, 128], BF16)
    halo_all = nc.dram_tensor("halo_all", [4 * D, 128], BF16)
    A["h3_loc"] = h3_loc.ap()
    A["halo_all"] = halo_all.ap()

    def allgather(S, src, dst, wkey):
        return S.cc(lambda h: h.collective_compute("AllGather", ALU.bypass, replica_groups=RG, ins=[src.ap()], outs=[dst.ap()]),
                    w=[wkey])

    with contextlib.ExitStack() as st:
        ss = SemState(nc, st)
        C = alloc_common(nc, st, T)
        x1_loc = nc.dram_tensor("x1_loc", [D, T], F32)
        with contextlib.ExitStack() as p1:
            C["xT"] = p1.enter_context(sbt(nc, "xT", [128, 8, T], F32))
            xT = C["xT"]
            alloc_ffn(nc, p1, C, T, with_x=False)
            C["smalls"] = p1.enter_context(sbt(nc, "sm1_sb", [128, 16], F32))
            S = Sched()
            init_consts(S, C)
            S.dma(lambda h: h.dma_start(out=C["smalls"][:, :], in_=sm1_d[:, :]), w=["smalls"], q="sp")
            load_xT(S, C, x, T)
            rmsnorm_fm(S, C, xT, C["hT"], C["smalls"][:, 0:8], T, "n1")
            ffn_fm(S, C, xT, C["hT"], W["gu01"], W["dn01"], T, "f1")
            store_T(S, C, "xT", xT, x1_loc.ap(), T)
            rmsnorm_fm(S, C, xT, C["hT"], C["smalls"][:, 8:16], T, "n2")
            for tt in range(4):
                S.dma(lambda h, tt=tt: h.dma_start(out=h1_loc[tt].ap().rearrange("(c p) t -> p c t", p=128),
                                                   in_=C["hT"][:, :, tt * 512:(tt + 1) * 512]),
                      r=[("hT", c, tt) for c in range(8)], q="sp")
            emit_phase(nc, S, ss)
        h1v = [h1_all[i].ap().rearrange("(r c p) t -> p r c t", r=4, c=8, p=128) for i in range(4)]

        def pre2(S):
            for i in range(4):
                allgather(S, h1_loc[i], h1_all[i], ("h1_all", i))

        gdn_phase4(nc, ss, C["ps"],
                  lambda tg: h1v[tg % 4][:, tg // 4, :, :],
                  lambda hd, tg: og_loc[tg % 4].ap()[(tg // 4) * 256 + hd * 128:(tg // 4) * 256 + (hd + 1) * 128, :],
                  w_my, sm2_d, cm_d, 4 * T // 512,
                  pre_fn=pre2, h_keys=lambda tg: [("h1_all", tg % 4)],
                  og_key=lambda tg: ("og_loc", tg % 4),
                  post_fn=lambda S, tg: (S.cc(lambda h, i=tg % 4: h.collective_compute(
                      "AllGather", ALU.bypass, replica_groups=RG, ins=[og_loc[i].ap()], outs=[og_all[i].ap()]),
                      r=[("og_loc", tg % 4)], w=[("og_all_dram", tg % 4)]) if tg >= 12 else None))
        C["xT"] = st.enter_context(sbt(nc, "xT", [128, 8, T], F32))
        xT = C["xT"]
        with contextlib.ExitStack() as p3:
            alloc_ffn(nc, p3, C, T, with_x=False)
            C["smalls"] = p3.enter_context(sbt(nc, "sm3_sb", [128, 28], F32))
            sm3 = C["smalls"]
            hT, aT = C["hT"], C["aT"]
            S = Sched()
            S.dma(lambda h: h.dma_start(out=sm3[:, :], in_=sm3_d[:, :]), w=["smalls"], q="sp")
            for tt in range(4):
                ogv = og_all[tt].ap().rearrange("(r g q p) t -> p r g q t", r=4, g=4, q=2, p=128)
                sl = slice(tt * 512, (tt + 1) * 512)
                for g in range(4):
                    for r_ in range(4):
                        S.dma(lambda h, g=g, ogv=ogv, r_=r_: h.dma_start(out=aT[:, 2 * r_:2 * r_ + 2, g, :], in_=ogv[:, r_, g, :, :]),
                              w=[("aS", r_, g)], q=("sp" if (g + r_) % 2 == 0 else "act"))
                AR = [("aS", r_, g) for r_ in range(4) for g in range(4)]
                HW = [("hT", c, tt) for c in range(8)]
                S.dve(lambda h, sl=sl: h.tensor_scalar(out=hT[:, :, sl], in0=aT[:, 0:8, 0, :], scalar1=sm3[:, 24:25], scalar2=None,
                                                       op0=ALU.mult), r=AR + ["smalls"], w=HW)
                for g in range(1, 4):
                    S.dve(lambda h, sl=sl, g=g: h.scalar_tensor_tensor(out=hT[:, :, sl], in0=aT[:, 0:8, g, :], scalar=sm3[:, 24 + g:25 + g],
                                                                       in1=hT[:, :, sl], op0=ALU.mult, op1=ALU.add),
                          r=AR + ["smalls"] + HW, w=HW)
            load_xT(S, C, x1_loc.ap(), T)
            proj_residual(S, C, a_w_out, hT, T)
            rmsnorm_fm(S, C, xT, hT, sm3[:, 0:8], T, "n1")
            ffn_fm(S, C, xT, hT, W["gu02"], W["dn02"], T, "fa")
            rmsnorm_fm(S, C, xT, hT, sm3[:, 8:16], T, "n2")
            ffn_fm(S, C, xT, hT, W["gu11"], W["dn11"], T, "fb")
            rmsnorm_fm(S, C, xT, hT, sm3[:, 16:24], T, "n3")
            store_T(S, C, "hT", hT, h3_loc.ap(), T)
            S.dma(lambda h: h.dma_start(out=halo_loc.ap().rearrange("(c p) t -> p c t", p=128), in_=hT[:, :, T - 128:T]),
                  r=[("hT", c, 3) for c in range(8)], q="sp")
            emit_phase(nc, S, ss)
        l4_phases(nc, ss, C, A, T, fused=True, pre_fn=lambda S: allgather(S, halo_loc, halo_all, "halo_all"))
    return nc


_NC_CACHE = {}


def _get(name, fn):
    if name not in _NC_CACHE:
        _NC_CACHE[name] = fn()
    return _NC_CACHE[name]


def onehot_cols(idx):
    v = np.zeros((128, 4), np.float32)
    if 0 <= idx < 4:
        v[:, idx] = 1.0
    return v


def kernel(**inp):
    inp = {k: np.asarray(v) for k, v in inp.items()}
    T = 2048
    cores = list(range(NCORES))
    xs = inp["x"].reshape(2 * 8192, D)
    nc = _get("fused", build_fused)
    sm1 = np.concatenate([col8(inp["ffn1_norm"][0]), col8(inp["mix_norm"][0])], axis=1)
    l2 = l2_inputs([None, None], inp)
    l4 = l4_inputs([None] * NCORES, [np.zeros((D, T), ml_dtypes.bfloat16)] * NCORES, inp)
    maps = []
    for c in cores:
        s_ = c % 4
        sm3 = np.concatenate([col8(inp["ffn2_norm"][0]), col8(inp["ffn1_norm"][1]), col8(inp["mix_norm"][1]), onehot_cols(s_)], axis=1)
        sm4 = l4[c]["smalls"].copy()
        sm4[:, 32:36] = onehot_cols(s_ - 1)
        m = dict(xT_in=np.ascontiguousarray(xs[c * T:(c + 1) * T].T), sm1=sm1, sm=l2[c]["sm"], cm=l2[c]["cm"], w_my=l2[c]["w_my"],
                 sm3=np.ascontiguousarray(sm3), a_w_out=inp["a_w_out"][0], sm4=sm4, bq64=l4[c]["bq64"], bvb=l4[c]["bvb"],
                 masks=l4[c]["masks"], b_w_in=inp["b_w_in"][0], b_w_out=inp["b_w_out"][0])
        for l in range(2):
            m["w_gu_%d_1" % l] = inp["ffn1_w_gu"][l]
            m["w_down_%d_1" % l] = inp["ffn1_w_down"][l]
            m["w_gu_%d_2" % l] = inp["ffn2_w_gu"][l]
            m["w_down_%d_2" % l] = inp["ffn2_w_down"][l]
        maps.append(m)
    res = run_bass_kernel_spmd(nc, maps, core_ids=cores).results
    out = np.concatenate([np.asarray(res[c]["yT_out"]).T for c in cores], axis=0)
    return np.ascontiguousarray(out.reshape(2, 8192, D).astype(np.float32))
```
